# Optimizing a Trainium2 kernel written in Bass

```python
import math
import jax, jax.numpy as jnp
from jax import lax
import numpy as np

D_MODEL = 2048
BATCH = 32
SEQ = 256
DEPTH = 1
DEC_BATCH = 4
DEC_SEQ = 4096
PAST_LEN = 256

GRID_W = 64
CTX_CHUNK = 64
D_A = 1024
H_A = 8
DH_A = D_A // H_A
D_B = 1024
S5_CH = 16
G_B = D_B // S5_CH
P_B = 64
D_FF = ((8 * D_MODEL // 3 + 255) // 256) * 256
N_IN = 4 * D_A + 4 * H_A + D_B + 2 * D_MODEL
EPS = 1e-6

kernel_name = "bidir_mlstm_s5_hybrid_dit_step"


def rmsnorm(x, g):
    xf = x.astype(jnp.float32)
    return xf * lax.rsqrt(jnp.mean(xf * xf, axis=-1, keepdims=True) + EPS) * g.astype(jnp.float32)


def _to_chunks(t, n_chunks, chunk):
    t = t.reshape((t.shape[0], n_chunks, chunk) + t.shape[2:])
    return jnp.swapaxes(jnp.moveaxis(t, 1, 0), 2, 3)


def _from_chunks(t):
    t = jnp.moveaxis(jnp.swapaxes(t, 2, 3), 0, 1)
    return t.reshape((t.shape[0], t.shape[1] * t.shape[2]) + t.shape[3:])


def mlstm_scan(q, k, v, ig, fg, C0, n0, m0, n_chunks, chunk):
    f32 = jnp.float32
    tril = jnp.tril(jnp.ones((chunk, chunk), dtype=bool))

    def step(carry, inp):
        C, n, m = carry
        qc, kc, vc, ic, fc = inp
        b = jnp.cumsum(jax.nn.log_sigmoid(fc), axis=-1)
        d_intra = jnp.where(tril, b[..., :, None] - b[..., None, :] + ic[..., None, :], -jnp.inf)
        d_inter = b + m[..., None]
        m_row = jnp.maximum(jnp.max(d_intra, axis=-1), d_inter)
        s = jnp.einsum('bhjd,bhsd->bhjs', qc, kc) * jnp.exp(d_intra - m_row[..., None])
        w_inter = jnp.exp(d_inter - m_row)
        num = (jnp.einsum('bhjs,bhse->bhje', s, vc)
               + w_inter[..., None] * jnp.einsum('bhjd,bhde->bhje', qc, C))
        den = jnp.sum(s, axis=-1) + w_inter * jnp.einsum('bhjd,bhd->bhj', qc, n)
        h = num / jnp.maximum(jnp.abs(den), jnp.exp(-m_row))[..., None]
        b_last = b[..., -1]
        g = b_last[..., None] - b + ic
        m_new = jnp.maximum(b_last + m, jnp.max(g, axis=-1))
        wk = jnp.exp(g - m_new[..., None])
        decay = jnp.exp(b_last + m - m_new)
        C_new = decay[..., None, None] * C + jnp.einsum('bhs,bhsd,bhse->bhde', wk, kc, vc)
        n_new = decay[..., None] * n + jnp.einsum('bhs,bhsd->bhd', wk, kc)
        return (C_new, n_new, m_new), h

    xs = tuple(_to_chunks(t.astype(f32), n_chunks, chunk) for t in (q, k, v, ig, fg))
    (Cf, nf, mf), h = lax.scan(step, (C0.astype(f32), n0.astype(f32), m0.astype(f32)), xs)
    return _from_chunks(h), Cf, nf, mf


def mlstm_bidir(q, k, v, i_f, i_b, f_f, f_b, C0, n0, m0, n_chunks, chunk):
    flip = lambda t: jnp.flip(t, axis=1)
    h_f, Cf, nf, mf = mlstm_scan(q, k, v, i_f, f_f, C0[:, 0], n0[:, 0], m0[:, 0], n_chunks, chunk)
    h_b, Cb, nb, mb = mlstm_scan(flip(q), flip(k), flip(v), flip(i_b), flip(f_b),
                                 C0[:, 1], n0[:, 1], m0[:, 1], n_chunks, chunk)
    return (h_f + flip(h_b), jnp.stack([Cf, Cb], 1), jnp.stack([nf, nb], 1),
            jnp.stack([mf, mb], 1))


def _cplx_combine(e1, e2):
    a1r, a1i, x1r, x1i = e1
    a2r, a2i, x2r, x2i = e2
    return (a2r * a1r - a2i * a1i, a2r * a1i + a2i * a1r,
            a2r * x1r - a2i * x1i + x2r, a2r * x1i + a2i * x1r + x2i)


def s5_scan(u, lr, li, log_step, br, bi, cr, ci, h0r, h0i):
    f32 = jnp.float32
    lr, li, br, bi, cr, ci = (t.astype(f32) for t in (lr, li, br, bi, cr, ci))
    dt = jnp.exp(log_step.astype(f32))[:, None]
    mag = jnp.exp(lr * dt)
    ar, ai = mag * jnp.cos(li * dt), mag * jnp.sin(li * dt)
    den = lr * lr + li * li
    er = ar - 1.0
    kr = (er * lr + ai * li) / den
    ki = (ai * lr - er * li) / den
    bbr = kr[..., None] * br - ki[..., None] * bi
    bbi = kr[..., None] * bi + ki[..., None] * br
    xr = jnp.einsum('blgc,gpc->blgp', u, bbr)
    xi = jnp.einsum('blgc,gpc->blgp', u, bbi)
    h0r, h0i = h0r.astype(f32), h0i.astype(f32)
    xr = xr.at[:, 0].add(ar * h0r - ai * h0i)
    xi = xi.at[:, 0].add(ar * h0i + ai * h0r)
    L = u.shape[1]
    a_r = jnp.broadcast_to(ar, (1, L) + ar.shape)
    a_i = jnp.broadcast_to(ai, (1, L) + ai.shape)
    _, _, sr, si = lax.associative_scan(_cplx_combine, (a_r, a_i, xr, xi), axis=1)
    y = jnp.einsum('blgp,gcp->blgc', sr, cr) - jnp.einsum('blgp,gcp->blgc', si, ci)
    return y, sr[:, -1], si[:, -1]


def s5_bidir(u, lam_re, lam_im, log_step, B_re, B_im, C_re, C_im, s0r, s0i):
    flip = lambda t: jnp.flip(t, axis=1)
    y_f, fr, fi = s5_scan(u, lam_re[0], lam_im[0], log_step[0], B_re[0], B_im[0],
                          C_re[0], C_im[0], s0r[:, 0], s0i[:, 0])
    y_b, bkr, bki = s5_scan(flip(u), lam_re[1], lam_im[1], log_step[1], B_re[1], B_im[1],
                            C_re[1], C_im[1], s0r[:, 1], s0i[:, 1])
    return y_f + flip(y_b), jnp.stack([fr, bkr], 1), jnp.stack([fi, bki], 1)


def mixer(h, states, w_in, b_gates, mh_g, lam_re, lam_im, log_step, B_re, B_im, C_re, C_im,
          s5_d, w_up_a, w_glu, w_out, n_chunks, chunk):
    f32 = jnp.float32
    C0, n0, m0, s0r, s0i = states
    Bn, L, _ = h.shape
    z = jnp.einsum('bld,dn->bln', h, w_in.astype(f32))
    idx = [D_A, 2 * D_A, 3 * D_A, 4 * D_A, 4 * D_A + 4 * H_A, 4 * D_A + 4 * H_A + D_B]
    q, k, v, o, gt, u, mg = jnp.split(z, idx, axis=-1)
    gt = (gt + b_gates.astype(f32)).reshape(Bn, L, 4, H_A)
    q = q.reshape(Bn, L, H_A, DH_A)
    k = k.reshape(Bn, L, H_A, DH_A) * (DH_A ** -0.5)
    v = v.reshape(Bn, L, H_A, DH_A)
    h_a, Cn, nn, mn = mlstm_bidir(q, k, v, gt[:, :, 0], gt[:, :, 1], gt[:, :, 2], gt[:, :, 3],
                                  C0, n0, m0, n_chunks, chunk)
    h_a = h_a * lax.rsqrt(jnp.mean(h_a * h_a, axis=-1, keepdims=True) + EPS)
    h_a = h_a.reshape(Bn, L, D_A) * mh_g.astype(f32) * jax.nn.sigmoid(o)
    y_a = h_a @ w_up_a.astype(f32)
    u = u.reshape(Bn, L, G_B, S5_CH)
    y_s, sr, si = s5_bidir(u, lam_re, lam_im, log_step, B_re, B_im, C_re, C_im, s0r, s0i)
    act = jax.nn.gelu(y_s + s5_d.astype(f32) * u).reshape(Bn, L, D_B)
    ga, gb = jnp.split(act @ w_glu.astype(f32), 2, axis=-1)
    y_b = ga * jax.nn.sigmoid(gb)
    g_a, g_b = jnp.split(mg, 2, axis=-1)
    out = (jax.nn.sigmoid(g_a) * y_a + jax.nn.sigmoid(g_b) * y_b) @ w_out.astype(f32)
    return out, (Cn, nn, mn, sr, si)


def block(x, mod, states, params, n_chunks, chunk):
    f32 = jnp.float32
    (g1, g2, w_in, b_gates, mh_g, lam_re, lam_im, log_step, B_re, B_im, C_re, C_im, s5_d,
     w_up_a, w_glu, w_out, w_ffn_in, w_ffn_out) = params
    sh1, sc1, gt1, sh2, sc2, gt2 = jnp.split(mod, 6, axis=-1)
    h = rmsnorm(x, g1) * (1.0 + sc1) + sh1
    mix, new_states = mixer(h, states, w_in, b_gates, mh_g, lam_re, lam_im, log_step, B_re, B_im,
                            C_re, C_im, s5_d, w_up_a, w_glu, w_out, n_chunks, chunk)
    x = x.astype(f32) + gt1 * mix
    h = rmsnorm(x, g2) * (1.0 + sc2) + sh2
    a, b = jnp.split(h @ w_ffn_in.astype(f32), 2, axis=-1)
    x = x + gt2 * ((jax.nn.silu(a) * b) @ w_ffn_out.astype(f32))
    return x, new_states


def setup_inputs(seed: int = 0) -> dict:
    key = jax.random.key(seed)
    ks = jax.random.split(key, 32)
    f32 = jnp.float32
    nrm = lambda k, shape, s: jax.random.normal(k, shape, f32) * s
    f_bias_base = jnp.tile(jnp.linspace(3.0, 6.0, H_A, dtype=f32), 2)[None]
    return {
        "x_prompt": nrm(ks[0], (BATCH, SEQ, D_MODEL), 1.0),
        "x_sample": nrm(ks[1], (DEC_BATCH, DEC_SEQ, D_MODEL), 1.0),
        "c": nrm(ks[2], (DEC_BATCH, D_MODEL), 1.0),
        "state_mlstm_C": nrm(ks[3], (DEC_BATCH, DEPTH, 2, H_A, DH_A, DH_A), 0.3),
        "state_mlstm_n": nrm(ks[4], (DEC_BATCH, DEPTH, 2, H_A, DH_A), 0.3),
        "state_mlstm_m": nrm(ks[5], (DEC_BATCH, DEPTH, 2, H_A), 1.0),
        "state_s5_re": nrm(ks[6], (DEC_BATCH, DEPTH, 2, G_B, P_B), 0.3),
        "state_s5_im": nrm(ks[7], (DEC_BATCH, DEPTH, 2, G_B, P_B), 0.3),
        "c_ctx": nrm(ks[8], (D_MODEL,), 1.0),
        "w_mod": nrm(ks[9], (DEPTH, D_MODEL, 6 * D_MODEL), 0.5 * D_MODEL ** -0.5),
        "b_mod": nrm(ks[10], (DEPTH, 6 * D_MODEL), 0.02),
        "norm1_g": 1.0 + nrm(ks[11], (DEPTH, D_MODEL), 0.02),
        "norm2_g": 1.0 + nrm(ks[12], (DEPTH, D_MODEL), 0.02),
        "w_in": nrm(ks[13], (DEPTH, D_MODEL, N_IN), D_MODEL ** -0.5),
        "b_gates": jnp.concatenate([nrm(ks[14], (DEPTH, 2 * H_A), 0.1),
                                    f_bias_base + nrm(ks[15], (DEPTH, 2 * H_A), 0.1)], axis=-1),
        "mh_norm_g": 1.0 + nrm(ks[16], (DEPTH, D_A), 0.02),
        "s5_lam_re": -0.5 + nrm(ks[17], (DEPTH, 2, G_B, P_B), 0.01),
        "s5_lam_im": math.pi * jnp.arange(P_B, dtype=f32) + nrm(ks[18], (DEPTH, 2, G_B, P_B), 0.01),
        "s5_log_step": jax.random.uniform(ks[19], (DEPTH, 2, G_B), f32,
                                          math.log(1e-3), math.log(1e-1)),
        "s5_B_re": nrm(ks[20], (DEPTH, 2, G_B, P_B, S5_CH), (2 * S5_CH) ** -0.5),
        "s5_B_im": nrm(ks[21], (DEPTH, 2, G_B, P_B, S5_CH), (2 * S5_CH) ** -0.5),
        "s5_C_re": nrm(ks[22], (DEPTH, 2, G_B, S5_CH, P_B), P_B ** -0.5),
        "s5_C_im": nrm(ks[23], (DEPTH, 2, G_B, S5_CH, P_B), P_B ** -0.5),
        "s5_D": nrm(ks[24], (DEPTH, G_B, S5_CH), 1.0),
        "w_up_a": nrm(ks[25], (DEPTH, D_A, D_MODEL), D_A ** -0.5),
        "w_glu": nrm(ks[26], (DEPTH, D_B, 2 * D_MODEL), D_B ** -0.5),
        "w_out": nrm(ks[27], (DEPTH, D_MODEL, D_MODEL), D_MODEL ** -0.5),
        "w_ffn_in": nrm(ks[28], (DEPTH, D_MODEL, 2 * D_FF), D_MODEL ** -0.5),
        "w_ffn_out": nrm(ks[29], (DEPTH, D_FF, D_MODEL), D_FF ** -0.5),
        "norm_f_g": 1.0 + nrm(ks[30], (D_MODEL,), 0.02),
    }


def reference(x_prompt, x_sample, c, state_mlstm_C, state_mlstm_n, state_mlstm_m, state_s5_re,
              state_s5_im, c_ctx, w_mod, b_mod, norm1_g, norm2_g, w_in, b_gates, mh_norm_g,
              s5_lam_re, s5_lam_im, s5_log_step, s5_B_re, s5_B_im, s5_C_re, s5_C_im, s5_D,
              w_up_a, w_glu, w_out, w_ffn_in, w_ffn_out, norm_f_g):
    f32 = jnp.float32
    n_ctx = x_prompt.shape[0]
    ctx_chunks = x_prompt.shape[1] // CTX_CHUNK
    rows = x_sample.shape[1] // GRID_W
    zero_states = (jnp.zeros((n_ctx, 2, H_A, DH_A, DH_A), f32),
                   jnp.zeros((n_ctx, 2, H_A, DH_A), f32),
                   jnp.zeros((n_ctx, 2, H_A), f32),
                   jnp.zeros((n_ctx, 2, G_B, P_B), f32),
                   jnp.zeros((n_ctx, 2, G_B, P_B), f32))
    xc, xs = x_prompt, x_sample
    out_C, out_n, out_m, out_sr, out_si = [], [], [], [], []
    for l in range(DEPTH):
        params = (norm1_g[l], norm2_g[l], w_in[l], b_gates[l], mh_norm_g[l], s5_lam_re[l],
                  s5_lam_im[l], s5_log_step[l], s5_B_re[l], s5_B_im[l], s5_C_re[l], s5_C_im[l],
                  s5_D[l], w_up_a[l], w_glu[l], w_out[l], w_ffn_in[l], w_ffn_out[l])
        wm, bm = w_mod[l].astype(f32), b_mod[l].astype(f32)
        mod_ctx = (jax.nn.silu(c_ctx.astype(f32)) @ wm + bm)[None, None, :]
        mod_lat = (jax.nn.silu(c.astype(f32)) @ wm + bm)[:, None, :]
        xc, (Cn, nn, mn, sr, si) = block(xc, mod_ctx, zero_states, params, ctx_chunks, CTX_CHUNK)
        out_C.append(Cn)
        out_n.append(nn)
        out_m.append(mn)
        out_sr.append(sr)
        out_si.append(si)
        lat_states = (state_mlstm_C[:, l], state_mlstm_n[:, l], state_mlstm_m[:, l],
                      state_s5_re[:, l], state_s5_im[:, l])
        xs, _ = block(xs, mod_lat, lat_states, params, rows, GRID_W)
    y_prompt = rmsnorm(xc, norm_f_g)
    y_sample = rmsnorm(xs, norm_f_g)
    return (y_prompt, y_sample, jnp.stack(out_C, 1), jnp.stack(out_n, 1), jnp.stack(out_m, 1),
            jnp.stack(out_sr, 1), jnp.stack(out_si, 1))
```

```python
import numpy as np
from contextlib import ExitStack
import concourse.bass as bass
import concourse.mybir as mybir
from concourse.bass_utils import run_bass_kernel_spmd

F32 = mybir.dt.float32
BF16 = mybir.dt.bfloat16
AF = mybir.ActivationFunctionType
ALU = mybir.AluOpType
AX = mybir.AxisListType

D = 2048
KC = D // 128
D_A = 1024
H_A = 8
N_IN = 9248
D_FF = 5632
EPS = 1e-6


class Sem:
    def __init__(self, h, dma):
        self.h = h
        self.count = 0
        self.dma = dma


class Res:
    def __init__(self, name):
        self.name = name
        self.w = None
        self.r = []
        self.dsem = None


class Ctx:
    def __init__(self, nc, stack):
        self.nc = nc
        self.stack = stack
        self.eng = {'pe': nc.tensor, 'act': nc.scalar, 'dve': nc.vector, 'pool': nc.gpsimd, 'sp': nc.sync}
        self.esem = {}
        for e in ['pe', 'act', 'dve', 'pool']:
            self.esem[e] = Sem(stack.enter_context(nc.semaphore('s_' + e)), False)
        self.waited = {e: {} for e in self.eng}
        self.nres = 0
        self.dsems_all = []
        self.dsems_free = []
        self.live = []

    def res(self, name=None):
        self.nres += 1
        r = Res(name or f"r{self.nres}")
        self.live.append(r)
        return r

    def dsem(self, r):
        if r.dsem is None:
            if self.dsems_free:
                r.dsem = self.dsems_free.pop()
            else:
                s = Sem(self.stack.enter_context(self.nc.semaphore(f'd{len(self.dsems_all)}')), True)
                self.dsems_all.append(s)
                r.dsem = s
        return r.dsem

    def _wait(self, e, ev):
        if ev is None:
            return
        s, v = ev
        if s.dma:
            v = s.count
        w = self.waited[e]
        if w.get(id(s), -1) >= v:
            return
        w[id(s)] = v
        self.eng[e].wait_ge(s.h, v)

    def _skip(self, e, ev):
        return e == 'pe' and ev[0] is self.esem['pe']

    def _deps(self, e, reads, writes):
        for r in reads:
            if r.w is not None and not self._skip(e, r.w):
                self._wait(e, r.w)
        for r in writes:
            if r.w is not None and not self._skip(e, r.w):
                self._wait(e, r.w)
            for ev in r.r:
                if not self._skip(e, ev):
                    self._wait(e, ev)

    def op(self, e, fn, reads=(), writes=()):
        self._deps(e, reads, writes)
        ins = fn()
        s = self.esem[e]
        s.count += 1
        ins.then_inc(s.h, 1)
        ev = (s, s.count)
        for r in reads:
            r.r.append(ev)
        for r in writes:
            r.w = ev
            r.r = []
        return ins

    def dma(self, e, out, in_, reads=(), writes=(), sres=None, **kw):
        self._deps(e, reads, writes)
        if sres is None:
            sres = (list(writes) + list(reads))[0]
        s = self.dsem(sres)
        ins = self.eng[e].dma_start(out=out, in_=in_, **kw)
        s.count += 16
        ins.then_inc(s.h, 16)
        ev = (s, s.count)
        for r in reads:
            r.r.append(ev)
        for r in writes:
            r.w = ev
            r.r = []
        return ins

    def barrier(self, engines=('pe', 'act', 'dve', 'pool', 'sp')):
        for e in engines:
            for s in list(self.esem.values()) + self.dsems_all:
                if s.count > 0:
                    self._wait(e, (s, s.count))
        for r in self.live:
            if r.dsem is not None:
                self.dsems_free.append(r.dsem)
                r.dsem = None
            r.w = None
            r.r = []


class Pool:
    def __init__(self, cx, tiles):
        self.cx = cx
        self.tiles = tiles
        self.res = [cx.res() for _ in tiles]
        self.i = 0

    def next(self):
        t, r = self.tiles[self.i], self.res[self.i]
        self.i = (self.i + 1) % len(self.tiles)
        return t, r


import math

def stage_mlstm(nc, cx, T, NT, ident, r_ident):
    NCH = NT // 64
    NST = NT // 512
    NS = NT // 256
    with ExitStack() as st:
        sb = lambda name, shape, dt=F32: st.enter_context(nc.sbuf_tensor("m_" + name, shape, dt))
        psum = lambda name, shape: st.enter_context(nc.psum_tensor("mp_" + name, shape, F32))

        keep = sb("keep", [128, 1]); r_keep = cx.res()
        cx.dma('sp', keep[:], T['keep'][:, :], writes=[r_keep])
        ones8 = sb("ones8", [8, 128]); r_ones8 = cx.res()
        cx.op('dve', lambda: nc.vector.memset(ones8[:], 1.0), writes=[r_ones8])
        masks = []
        for d in range(2):
            mk = sb(f"mask{d}", [64, 64]); r_mk = cx.res()
            cx.dma('sp', mk[:], T['masks'][d, :, :], writes=[r_mk])
            masks.append((mk, r_mk))

        WK, CL, DECB, MM = [], [], [], []
        for d in range(2):
            WK.append((sb(f"WK{d}", [64, NCH, 8]), cx.res()))
            CL.append((sb(f"CL{d}", [64, NCH, 8]), cx.res()))
            DECB.append((sb(f"DECB{d}", [128, NCH, 8]), cx.res()))
            MM.append((sb(f"MM{d}", [8, NCH]), cx.res()))

        with ExitStack() as st2:
            sb2 = lambda name, shape, dt=F32: st2.enter_context(nc.sbuf_tensor("m2_" + name, shape, dt))
            PX = st2.enter_context(nc.psum_tensor("mp_PX", [128, 512], F32)); r_PX = cx.res()
            for d in range(2):
                order = list(range(NCH)) if d == 0 else list(range(NCH - 1, -1, -1))
                Ig = sb2(f"Ig{d}", [8, NCH, 64]); r_I = cx.res()
                Fa = sb2(f"Fa{d}", [8, NCH, 64]); r_Fa = cx.res()
                Fb = sb2(f"Fb{d}", [8, NCH, 64]); r_Fb = cx.res()
                cx.dma('sp', Ig[:], T['gates_s'][8 * d:8 * d + 8, :].rearrange("p (c s) -> p c s", s=64), writes=[r_I])
                cx.dma('sp', Fa[:], T['gates_s'][16 + 8 * d:16 + 8 * d + 8, :].rearrange("p (c s) -> p c s", s=64), writes=[r_Fa])
                cx.op('act', lambda: nc.scalar.activation(out=Fa[:], in_=Fa[:], func=AF.Exp, scale=-1.0), reads=[r_Fa], writes=[r_Fa])
                cx.op('act', lambda: nc.scalar.activation(out=Fa[:], in_=Fa[:], func=AF.Ln, bias=1.0), reads=[r_Fa], writes=[r_Fa])
                A, rA, B, rB = Fa, r_Fa, Fb, r_Fb
                for sh in (1, 2, 4, 8, 16, 32):
                    cx.op('pool', lambda: nc.gpsimd.tensor_copy(out=B[:], in_=A[:]), reads=[rA], writes=[rB])
                    if d == 0:
                        cx.op('dve', lambda: nc.vector.tensor_tensor(out=B[:, :, sh:], in0=A[:, :, sh:], in1=A[:, :, :64 - sh], op=ALU.add),
                              reads=[rA], writes=[rB])
                    else:
                        cx.op('dve', lambda: nc.vector.tensor_tensor(out=B[:, :, :64 - sh], in0=A[:, :, :64 - sh], in1=A[:, :, sh:], op=ALU.add),
                              reads=[rA], writes=[rB])
                    A, rA, B, rB = B, rB, A, rA
                NB, r_NB = A, rA
                R, r_R = B, rB
                cx.op('dve', lambda: nc.vector.tensor_tensor(out=R[:], in0=Ig[:], in1=NB[:], op=ALU.add), reads=[r_I, r_NB], writes=[r_R])
                rmax = sb2(f"rmax{d}", [8, NCH]); r_rmax = cx.res()
                cx.op('dve', lambda: nc.vector.tensor_reduce(out=rmax[:], in_=R[:], axis=AX.X, op=ALU.max), reads=[r_R], writes=[r_rmax])
                M63 = sb2(f"M63{d}", [8, NCH]); r_M63 = cx.res()
                mpe = sb2(f"mpe{d}", [8, NCH]); r_mpe = cx.res()
                mm, r_mm = MM[d]
                lastcol = 63 if d == 0 else 0
                c0 = order[0]
                cx.dma('sp', mpe[:, c0:c0 + 1], T['m0'][d, :].rearrange("(h o) -> h o", o=1), writes=[r_mpe])
                for k in range(NCH):
                    c = order[k]
                    cx.op('dve', lambda: nc.vector.tensor_tensor(out=M63[:, c:c + 1], in0=mpe[:, c:c + 1], in1=rmax[:, c:c + 1], op=ALU.max),
                          reads=[r_mpe, r_rmax], writes=[r_M63])
                    cx.op('dve', lambda: nc.vector.tensor_tensor(out=mm[:, c:c + 1], in0=M63[:, c:c + 1], in1=NB[:, c, lastcol:lastcol + 1],
                                                                 op=ALU.subtract), reads=[r_M63, r_NB], writes=[r_mm])
                    if k + 1 < NCH:
                        cn = order[k + 1]
                        if (k + 1) % 4 == 0:
                            cx.op('dve', lambda: nc.vector.tensor_scalar(out=mpe[:, cn:cn + 1], in0=mm[:, c:c + 1], scalar1=keep[0:8, 0:1],
                                                                         scalar2=None, op0=ALU.mult), reads=[r_mm, r_keep], writes=[r_mpe])
                        else:
                            cx.op('dve', lambda: nc.vector.tensor_copy(out=mpe[:, cn:cn + 1], in_=mm[:, c:c + 1]), reads=[r_mm], writes=[r_mpe])
                M63b = M63[:].unsqueeze(2).to_broadcast([8, NCH, 64])
                cx.op('dve', lambda: nc.vector.tensor_tensor(out=R[:], in0=R[:], in1=M63b, op=ALU.subtract), reads=[r_R, r_M63], writes=[r_R])
                cx.op('act', lambda: nc.scalar.activation(out=R[:], in_=R[:], func=AF.Exp), reads=[r_R], writes=[r_R])
                cx.op('dve', lambda: nc.vector.tensor_tensor(out=NB[:], in0=NB[:], in1=M63b, op=ALU.subtract), reads=[r_NB, r_M63], writes=[r_NB])
                cx.op('act', lambda: nc.scalar.activation(out=NB[:], in_=NB[:], func=AF.Exp), reads=[r_NB], writes=[r_NB])
                DEC = sb2(f"DEC{d}", [8, NCH]); r_DEC = cx.res()
                cx.op('dve', lambda: nc.vector.tensor_tensor(out=DEC[:], in0=mpe[:], in1=M63[:], op=ALU.subtract), reads=[r_mpe, r_M63], writes=[r_DEC])
                cx.op('act', lambda: nc.scalar.activation(out=DEC[:], in_=DEC[:], func=AF.Exp), reads=[r_DEC], writes=[r_DEC])
                if NCH > 4:
                    bsl = slice(4, NCH, 4) if d == 0 else slice(3, NCH - 4, 4)
                    cx.op('dve', lambda: nc.vector.tensor_scalar(out=DEC[:, bsl], in0=DEC[:, bsl], scalar1=keep[0:8, 0:1], scalar2=None, op0=ALU.mult),
                          reads=[r_DEC, r_keep], writes=[r_DEC])
                for (src, rsrc, (dst, rdst)) in ((R, r_R, WK[d]), (NB, r_NB, CL[d])):
                    for c in range(NCH):
                        cx.op('pe', lambda: nc.tensor.transpose(PX[0:64, c * 8:(c + 1) * 8], src[:, c, :], ident[0:8, 0:8]),
                              reads=[rsrc, r_ident], writes=[r_PX])
                    cx.op('dve', lambda: nc.vector.tensor_copy(out=dst[:].rearrange("p c h -> p (c h)"), in_=PX[0:64, 0:NCH * 8]),
                          reads=[r_PX], writes=[rdst])
                DECX = sb2(f"DECX{d}", [8, NCH, 8]); r_DECX = cx.res()
                cx.op('dve', lambda: nc.vector.tensor_tensor(out=DECX[:], in0=DEC[:].unsqueeze(2).to_broadcast([8, NCH, 8]),
                                                             in1=ident[0:8, 0:8].unsqueeze(1).to_broadcast([8, NCH, 8]), op=ALU.mult),
                      reads=[r_DEC, r_ident], writes=[r_DECX])
                cx.op('pe', lambda: nc.tensor.matmul(PX[:, 0:NCH * 8], ones8[:], DECX[:].rearrange("p c h -> p (c h)"), start=True, stop=True),
                      reads=[r_ones8, r_DECX], writes=[r_PX])
                db, r_db = DECB[d]
                cx.op('dve', lambda: nc.vector.tensor_copy(out=db[:].rearrange("p c h -> p (c h)"), in_=PX[:, 0:NCH * 8]), reads=[r_PX], writes=[r_db])
                msl = slice(3, NCH, 4) if d == 0 else slice(0, NCH, 4)
                cx.dma('sp', T['out_m'][:, d, :].rearrange("s h -> h s"), mm[:, msl], reads=[r_mm], allow_slow_non_contiguous=True)
            cx.barrier()

        ST_TOK = 256
        CPS = ST_TOK // 64
        NSUP = NT // ST_TOK
        ones64 = sb("ones64", [64, 1], BF16); r_ones64 = cx.res()
        cx.op('pool', lambda: nc.gpsimd.memset(ones64[:], 1.0), writes=[r_ones64])
        hout = [T['hf_s'], T['hb_s']]

        def run_dir(d):
            PSTd = (psum(f"PSTd{d}", [128, 512]), cx.res())
            PNUMd = (psum(f"PNUMd{d}", [128, 512]), cx.res())
            PUPDd = (psum(f"PUPDd{d}", [128, 512]), cx.res())
            PDUd = (psum(f"PDUd{d}", [128, 512]), cx.res())
            r_pun = cx.res()
            qT = Pool(cx, [sb(f"qT{d}{i}", [128, 8, ST_TOK], BF16) for i in range(2)])
            kT = Pool(cx, [sb(f"kT{d}{i}", [128, 8, ST_TOK], BF16) for i in range(2)])
            kt = Pool(cx, [sb(f"kt{d}{i}", [64, CPS, 8, 128], BF16) for i in range(2)])
            v1 = Pool(cx, [sb(f"v1{d}{i}", [64, CPS, 8, 128], BF16) for i in range(2)])
            Cst = [(sb(f"C{d}{i}", [128, 8, 129]), cx.res()) for i in range(2)]
            Cb = sb(f"Cb{d}", [128, 8, 129], BF16); r_Cb = cx.res()
            Sm1 = sb(f"Sm1{d}", [64, 8, 64]); r_Sm1 = cx.res()
            Sm = sb(f"Sm{d}", [64, 8, 64], BF16); r_Sm = cx.res()
            Kt = sb(f"Kt{d}", [64, 8, 128], BF16); r_Kt = cx.res()
            dn = sb(f"dn{d}", [64, 8]); r_dn = cx.res()
            dn2 = sb(f"dn2{d}", [64, 8]); r_dn2 = cx.res()
            rc = sb(f"rc{d}", [64, 8]); r_rc = cx.res()
            hst = Pool(cx, [sb(f"hst{d}{i}", [64, 4, 128]) for i in range(3)])
            cx.dma('sp', Cst[0][0][:, :, 0:128], T['C0'][d].rearrange("h d e -> d h e"), writes=[Cst[0][1]])
            cx.dma('sp', Cst[0][0][:, :, 128], T['n0'][d].rearrange("h d -> d h"), writes=[Cst[0][1]], allow_slow_non_contiguous=True)
            wk, r_wk = WK[d]; cl, r_cl = CL[d]; db, r_db = DECB[d]
            mk, r_mk = masks[d]

            def load_super(stile):
                q, rq = qT.next(); kk, rk = kT.next(); ktk, rkt = kt.next(); vv, rv = v1.next()
                ts = slice(stile * ST_TOK, (stile + 1) * ST_TOK)
                cx.dma('sp', q[:], T['qT_s'][:, :, ts].rearrange("h d t -> d h t"), writes=[rq])
                cx.dma('sp', kk[:], T['kT_s'][:, :, ts].rearrange("h d t -> d h t"), writes=[rk])
                cx.dma('sp', ktk[:], T['ktok_s'][ts, :].rearrange("(c s) (h e) -> s c h e", s=64, e=128), writes=[rkt])
                cx.dma('sp', vv[:], T['vtok_s'][ts, :].rearrange("(c s) (h e) -> s c h e", s=64, e=128), writes=[rv])
                return (q, rq, kk, rk, ktk, rkt, vv, rv)

            order = list(range(NCH)) if d == 0 else list(range(NCH - 1, -1, -1))
            sup_order = []
            for c in order:
                if not sup_order or sup_order[-1] != c // CPS:
                    sup_order.append(c // CPS)
            loaded = {sup_order[0]: load_super(sup_order[0])}
            yield
            for k in range(NCH):
                c = order[k]
                stile, cs = c // CPS, c % CPS
                if k % CPS == 0:
                    si_ = sup_order.index(stile)
                    if si_ + 1 < len(sup_order):
                        loaded[sup_order[si_ + 1]] = load_super(sup_order[si_ + 1])
                q, rq, kk, rk, ktk, rkt, vv, rv = loaded[stile]
                cur, nxt = k % 2, (k + 1) % 2
                Cc, rCc = Cst[cur]; Cn, rCn = Cst[nxt]
                tsl = slice(cs * 64, (cs + 1) * 64)
                pst, r_pst = PSTd; pnum, r_pnum = PNUMd; pupd, r_pupd = PUPDd; pdu, r_pdu = PDUd
                cx.op('dve', lambda: nc.vector.tensor_tensor(out=Cn[:], in0=Cc[:], in1=db[:, c, :].unsqueeze(2).to_broadcast([128, 8, 129]), op=ALU.mult),
                      reads=[rCc, r_db], writes=[rCn])
                cx.op('act', lambda: nc.scalar.copy(out=Cb[:], in_=Cn[:]), reads=[rCn], writes=[r_Cb])
                for h in range(8):
                    cx.op('pe', lambda: nc.tensor.matmul(pst[0:64, h * 64:(h + 1) * 64], kk[:, h, tsl], q[:, h, tsl], start=True, stop=True),
                          reads=[rk, rq], writes=[r_pst])
                yield
                cx.op('dve', lambda: nc.vector.tensor_tensor(out=Sm1[:], in0=pst[0:64, :].rearrange("p (h j) -> p h j", h=8),
                                                             in1=wk[:, c, :].unsqueeze(2).to_broadcast([64, 8, 64]), op=ALU.mult),
                      reads=[r_pst, r_wk], writes=[r_Sm1])
                cx.op('pool', lambda: nc.gpsimd.tensor_tensor(out=Sm[:], in0=Sm1[:], in1=mk[:].unsqueeze(1).to_broadcast([64, 8, 64]), op=ALU.mult),
                      reads=[r_Sm1, r_mk], writes=[r_Sm])
                cx.op('pool', lambda: nc.gpsimd.tensor_tensor(out=Kt[:], in0=ktk[:, cs, :, :], in1=wk[:, c, :].unsqueeze(2).to_broadcast([64, 8, 128]), op=ALU.mult),
                      reads=[rkt, r_wk], writes=[r_Kt])
                yield
                for h in range(8):
                    cx.op('pe', lambda: nc.tensor.matmul(pdu[0:64, h:h + 1], Sm[:, h, :], ones64[:, 0:1], start=True, stop=False),
                          reads=[r_Sm, r_ones64], writes=[r_pdu])
                    cx.op('pe', lambda: nc.tensor.matmul(pdu[0:64, h:h + 1], q[:, h, tsl], Cb[:, h, 128:129], start=False, stop=True),
                          reads=[rq, r_Cb], writes=[r_pdu])
                for half in range(2):
                    for hh in range(4):
                        h = half * 4 + hh
                        cx.op('pe', lambda: nc.tensor.matmul(pnum[0:64, hh * 128:(hh + 1) * 128], Sm[:, h, :], vv[:, cs, h, :], start=True, stop=False),
                              reads=[r_Sm, rv], writes=[r_pnum])
                        cx.op('pe', lambda: nc.tensor.matmul(pnum[0:64, hh * 128:(hh + 1) * 128], q[:, h, tsl], Cb[:, h, 0:128], start=False, stop=True),
                              reads=[rq, r_Cb], writes=[r_pnum])
                    if half == 0:
                        cx.op('dve', lambda: nc.vector.tensor_tensor(out=dn[:], in0=pdu[0:64, 0:8], in1=cl[:, c, :], op=ALU.max),
                              reads=[r_pdu, r_cl], writes=[r_dn])
                        cx.op('dve', lambda: nc.vector.tensor_scalar(out=dn2[:], in0=pdu[0:64, 0:8], scalar1=-1.0, scalar2=None, op0=ALU.mult),
                              reads=[r_pdu], writes=[r_dn2])
                        cx.op('dve', lambda: nc.vector.tensor_tensor(out=dn[:], in0=dn[:], in1=dn2[:], op=ALU.max), reads=[r_dn, r_dn2], writes=[r_dn])
                        cx.op('dve', lambda: nc.vector.reciprocal(out=rc[:], in_=dn[:]), reads=[r_dn], writes=[r_rc])
                    for hh in range(4):
                        h = half * 4 + hh
                        cx.op('pe', lambda: nc.tensor.matmul(pupd[:, hh * 128:(hh + 1) * 128], Kt[:, h, :], vv[:, cs, h, :], start=True, stop=True),
                              reads=[r_Kt, rv], writes=[r_pupd])
                    if half == 0:
                        for h in range(8):
                            cx.op('pe', lambda: nc.tensor.matmul(pdu[:, 8 + h:9 + h], Kt[:, h, :], ones64[:, 0:1], start=True, stop=True),
                                  reads=[r_Kt, r_ones64], writes=[r_pun])
                    yield
                    hs, rhs = hst.next()
                    cx.op('dve', lambda: nc.vector.tensor_tensor(out=hs[:], in0=pnum[0:64, :].rearrange("p (h e) -> p h e", h=4),
                                                                 in1=rc[:, half * 4:(half + 1) * 4].unsqueeze(2).to_broadcast([64, 4, 128]), op=ALU.mult),
                          reads=[r_pnum, r_rc], writes=[rhs])
                    cx.dma('sp', hout[d][c * 64:(c + 1) * 64, half * 512:(half + 1) * 512], hs[:].rearrange("p h e -> p (h e)"), reads=[rhs])
                    cx.op('dve', lambda: nc.vector.tensor_tensor(out=Cn[:, half * 4:(half + 1) * 4, 0:128], in0=Cn[:, half * 4:(half + 1) * 4, 0:128],
                                                                 in1=pupd[:, :].rearrange("p (h e) -> p h e", h=4), op=ALU.add),
                          reads=[rCn, r_pupd], writes=[rCn])
                    if half == 0:
                        cx.op('dve', lambda: nc.vector.tensor_tensor(out=Cn[:, :, 128], in0=Cn[:, :, 128], in1=pdu[:, 8:16], op=ALU.add),
                              reads=[rCn, r_pun], writes=[rCn])
                    yield
                if (k + 1) % 4 == 0:
                    slot = c // 4
                    cx.dma('sp', T['out_C'][slot, d].rearrange("h d e -> d h e"), Cn[:, :, 0:128], reads=[rCn])
                    cx.dma('sp', T['out_n'][slot, d].rearrange("h d -> d h"), Cn[:, :, 128], reads=[rCn], allow_slow_non_contiguous=True)

        gens = [run_dir(0), run_dir(1)]
        alive = [True, True]
        while any(alive):
            for d in range(2):
                if alive[d]:
                    try:
                        next(gens[d])
                    except StopIteration:
                        alive[d] = False
        cx.barrier()

MAGIC = 12582912.0
TWO_PI = 2.0 * math.pi


def stage_s5(nc, cx, T, NT, ident, r_ident, hook=None):
    NK = NT // 8
    KS = 64
    NSEG = NT // 512
    NS = NT // 256
    with ExitStack() as st:
        sb = lambda name, shape, dt=F32: st.enter_context(nc.sbuf_tensor("s_" + name, shape, dt))
        keep = sb("keep", [128, 1]); r_keep = cx.res()
        cx.dma('sp', keep[:], T['keep'][:, :], writes=[r_keep])
        identb = sb("identb", [128, 128], BF16); r_identb = cx.res()
        cx.op('dve', lambda: nc.vector.tensor_copy(out=identb[:], in_=ident[:]), reads=[r_ident], writes=[r_identb])
        with ExitStack() as st_rec:
            sbr = lambda name, shape, dt=F32: st_rec.enter_context(nc.sbuf_tensor("sr_" + name, shape, dt))
            HT = [(sbr(f"HT{d}", [128, 64, 128], BF16), cx.res()) for d in range(2)]
            COEF = [(sbr(f"COEF{d}", [64, 2, 2, 64]), cx.res()) for d in range(2)]
            W0 = [(sbr(f"W0{d}", [64, 2, 64]), cx.res()) for d in range(2)]
            with ExitStack() as st2:
                sb2 = lambda name, shape, dt=F32: st2.enter_context(nc.sbuf_tensor("s2_" + name, shape, dt))
                psum = lambda name, shape, dt=F32: st2.enter_context(nc.psum_tensor("s2p_" + name, shape, dt))
                PA = Pool(cx, [psum(f"PA{i}", [128, 512]) for i in range(2)])
                PB = Pool(cx, [psum(f"PB{i}", [128, 512]) for i in range(2)])
                MTf = sb2("MTf", [128, 64, 128]); r_MTf = cx.res()
                MT16 = sb2("MT16", [128, 64, 128], BF16); r_MT16 = cx.res()
                pmtmp = sb2("pmtmp", [128, 4, 128]); r_pmtmp = cx.res()
                bmask = []
                for d in range(2):
                    bm = sb2(f"bmask{d}", [128, 128]); r_bm = cx.res()
                    cx.dma('sp', bm[:], T['bmask'][d], writes=[r_bm])
                    bmask.append((bm, r_bm))
                _cache = {}

                def salloc(name, shape, dt=F32):
                    if name not in _cache:
                        _cache[name] = (st2.enter_context(nc.sbuf_tensor("s2c_" + name, shape, dt)), cx.res())
                    return _cache[name]
                for d in range(2):
                  if True:
                    E = 'dve'
                    sh3 = [64, 64, 18]
                    sh8 = [64, 64, 8]
                    PR, r_PR = salloc("PR", sh3)
                    PI, r_PI = salloc("PI", sh3)
                    QR, r_QR = salloc("QR", sh8)
                    QI, r_QI = salloc("QI", sh8)
                    Br, r_Br = salloc("Br", [64, 64, 16])
                    Bi, r_Bi = salloc("Bi", [64, 64, 16])
                    CrT, r_CrT = salloc("CrT", [64, 64, 16])
                    CiT, r_CiT = salloc("CiT", [64, 64, 16])
                    if True:
                        lr, r_lr = salloc("lr", [64, 64])
                        li, r_li = salloc("li", [64, 64])
                        dtb, r_dtb = salloc("dtb", [64, 64])
                        expo, r_expo = salloc("expo", [64, 18])
                        cx.dma('sp', lr[:], T['lam_re'][d].rearrange("g p -> p g"), writes=[r_lr], allow_slow_non_contiguous=True)
                        cx.dma('sp', li[:], T['lam_im'][d].rearrange("g p -> p g"), writes=[r_li], allow_slow_non_contiguous=True)
                        cx.dma('sp', dtb[:], T['log_step'][d].partition_broadcast(64), writes=[r_dtb])
                        cx.dma('sp', expo[:], T['expo'][d].partition_broadcast(64), writes=[r_expo])
                        cx.dma('sp', Br[:], T['B_re'][d].rearrange("g p c -> p g c"), writes=[r_Br])
                        cx.dma('sp', Bi[:], T['B_im'][d].rearrange("g p c -> p g c"), writes=[r_Bi])
                        cx.op('act', lambda: nc.scalar.activation(out=dtb[:], in_=dtb[:], func=AF.Exp), reads=[r_dtb], writes=[r_dtb])
                        LD, r_LD = salloc("LD", [64, 64])
                        TH, r_TH = salloc("TH", [64, 64])
                        cx.op(E, lambda: nc.vector.tensor_tensor(out=LD[:], in0=lr[:], in1=dtb[:], op=ALU.mult), reads=[r_lr, r_dtb], writes=[r_LD])
                        cx.op(E, lambda: nc.vector.tensor_tensor(out=TH[:], in0=li[:], in1=dtb[:], op=ALU.mult), reads=[r_li, r_dtb], writes=[r_TH])
                        MAG, r_MAG = salloc("MAG", sh3)
                        ANG, r_ANG = salloc("ANG", sh3)
                        SN, r_SN = salloc("SN", sh3)
                        CS, r_CS = salloc("CS", sh3)
                        eb = expo[:].unsqueeze(1).to_broadcast(sh3)
                        cx.op(E, lambda: nc.vector.tensor_tensor(out=MAG[:], in0=LD[:].unsqueeze(2).to_broadcast(sh3), in1=eb, op=ALU.mult),
                              reads=[r_LD, r_expo], writes=[r_MAG])
                        cx.op('act', lambda: nc.scalar.activation(out=MAG[:], in_=MAG[:], func=AF.Exp), reads=[r_MAG], writes=[r_MAG])
                        cx.op(E, lambda: nc.vector.tensor_tensor(out=ANG[:], in0=TH[:].unsqueeze(2).to_broadcast(sh3), in1=eb, op=ALU.mult),
                              reads=[r_TH, r_expo], writes=[r_ANG])
                        for (dst, rdst, ph) in ((SN, r_SN, 0.0), (CS, r_CS, math.pi / 2)):
                            cx.op(E, lambda: nc.vector.tensor_scalar(out=dst[:], in0=ANG[:], scalar1=ph, scalar2=1.0 / TWO_PI, op0=ALU.add, op1=ALU.mult),
                                  reads=[r_ANG], writes=[rdst])
                            cx.op(E, lambda: nc.vector.tensor_scalar(out=dst[:], in0=dst[:], scalar1=MAGIC, scalar2=None, op0=ALU.add), reads=[rdst], writes=[rdst])
                            cx.op(E, lambda: nc.vector.tensor_scalar(out=dst[:], in0=dst[:], scalar1=-MAGIC, scalar2=-TWO_PI, op0=ALU.add, op1=ALU.mult),
                                  reads=[rdst], writes=[rdst])
                            cx.op(E, lambda: nc.vector.scalar_tensor_tensor(out=dst[:], in0=ANG[:], scalar=ph, in1=dst[:], op0=ALU.add, op1=ALU.add),
                                  reads=[r_ANG, rdst], writes=[rdst])
                            cx.op(E, lambda: nc.vector.tensor_scalar(out=dst[:], in0=dst[:], scalar1=-math.pi, scalar2=math.pi, op0=ALU.max, op1=ALU.min),
                                  reads=[rdst], writes=[rdst])
                            cx.op('act', lambda: nc.scalar.activation(out=dst[:], in_=dst[:], func=AF.Sin), reads=[rdst], writes=[rdst])
                        cx.op(E, lambda: nc.vector.tensor_tensor(out=PR[:], in0=MAG[:], in1=CS[:], op=ALU.mult), reads=[r_MAG, r_CS], writes=[r_PR])
                        cx.op(E, lambda: nc.vector.tensor_tensor(out=PI[:], in0=MAG[:], in1=SN[:], op=ALU.mult), reads=[r_MAG, r_SN], writes=[r_PI])
                        tmpa, r_ta = salloc("tmpa", [64, 64])
                        tmpb, r_tb = salloc("tmpb", [64, 64])
                        den, r_den = salloc("den", [64, 64])
                        er, r_er = salloc("er", [64, 64])
                        kr, r_kr = salloc("kr", [64, 64])
                        ki, r_ki = salloc("ki", [64, 64])
                        ar, ai = PR[:, :, 17], PI[:, :, 17]
                        cx.op(E, lambda: nc.vector.tensor_tensor(out=den[:], in0=lr[:], in1=lr[:], op=ALU.mult), reads=[r_lr], writes=[r_den])
                        cx.op(E, lambda: nc.vector.tensor_tensor(out=tmpa[:], in0=li[:], in1=li[:], op=ALU.mult), reads=[r_li], writes=[r_ta])
                        cx.op(E, lambda: nc.vector.tensor_tensor(out=den[:], in0=den[:], in1=tmpa[:], op=ALU.add), reads=[r_den, r_ta], writes=[r_den])
                        cx.op(E, lambda: nc.vector.reciprocal(out=den[:], in_=den[:]), reads=[r_den], writes=[r_den])
                        cx.op(E, lambda: nc.vector.tensor_scalar(out=er[:], in0=ar, scalar1=-1.0, scalar2=None, op0=ALU.add), reads=[r_PR], writes=[r_er])
                        cx.op(E, lambda: nc.vector.tensor_tensor(out=tmpa[:], in0=er[:], in1=lr[:], op=ALU.mult), reads=[r_er, r_lr], writes=[r_ta])
                        cx.op(E, lambda: nc.vector.tensor_tensor(out=tmpb[:], in0=ai, in1=li[:], op=ALU.mult), reads=[r_PI, r_li], writes=[r_tb])
                        cx.op(E, lambda: nc.vector.tensor_tensor(out=tmpa[:], in0=tmpa[:], in1=tmpb[:], op=ALU.add), reads=[r_ta, r_tb], writes=[r_ta])
                        cx.op(E, lambda: nc.vector.tensor_tensor(out=kr[:], in0=tmpa[:], in1=den[:], op=ALU.mult), reads=[r_ta, r_den], writes=[r_kr])
                        cx.op(E, lambda: nc.vector.tensor_tensor(out=tmpa[:], in0=ai, in1=lr[:], op=ALU.mult), reads=[r_PI, r_lr], writes=[r_ta])
                        cx.op(E, lambda: nc.vector.tensor_tensor(out=tmpb[:], in0=er[:], in1=li[:], op=ALU.mult), reads=[r_er, r_li], writes=[r_tb])
                        cx.op(E, lambda: nc.vector.tensor_tensor(out=tmpa[:], in0=tmpa[:], in1=tmpb[:], op=ALU.subtract), reads=[r_ta, r_tb], writes=[r_ta])
                        cx.op(E, lambda: nc.vector.tensor_tensor(out=ki[:], in0=tmpa[:], in1=den[:], op=ALU.mult), reads=[r_ta, r_den], writes=[r_ki])
                        q1, r_q1 = salloc("q1", sh8)
                        krb = kr[:].unsqueeze(2).to_broadcast(sh8)
                        kib = ki[:].unsqueeze(2).to_broadcast(sh8)
                        cx.op(E, lambda: nc.vector.tensor_tensor(out=QR[:], in0=PR[:, :, 0:8], in1=krb, op=ALU.mult), reads=[r_PR, r_kr], writes=[r_QR])
                        cx.op(E, lambda: nc.vector.tensor_tensor(out=q1[:], in0=PI[:, :, 0:8], in1=kib, op=ALU.mult), reads=[r_PI, r_ki], writes=[r_q1])
                        cx.op(E, lambda: nc.vector.tensor_tensor(out=QR[:], in0=QR[:], in1=q1[:], op=ALU.subtract), reads=[r_QR, r_q1], writes=[r_QR])
                        cx.op(E, lambda: nc.vector.tensor_tensor(out=QI[:], in0=PR[:, :, 0:8], in1=kib, op=ALU.mult), reads=[r_PR, r_ki], writes=[r_QI])
                        cx.op(E, lambda: nc.vector.tensor_tensor(out=q1[:], in0=PI[:, :, 0:8], in1=krb, op=ALU.mult), reads=[r_PI, r_kr], writes=[r_q1])
                        cx.op(E, lambda: nc.vector.tensor_tensor(out=QI[:], in0=QI[:], in1=q1[:], op=ALU.add), reads=[r_QI, r_q1], writes=[r_QI])
                        cf, r_cf = COEF[d]
                        a8r, a8i = PR[:, :, 16], PI[:, :, 16]
                        cx.op(E, lambda: nc.vector.tensor_copy(out=cf[:, 0, 0, :], in_=a8r), reads=[r_PR], writes=[r_cf])
                        cx.op(E, lambda: nc.vector.tensor_scalar(out=cf[:, 0, 1, :], in0=a8i, scalar1=-1.0, scalar2=None, op0=ALU.mult), reads=[r_PI], writes=[r_cf])
                        cx.op(E, lambda: nc.vector.tensor_copy(out=cf[:, 1, 0, :], in_=a8i), reads=[r_PI], writes=[r_cf])
                        cx.op(E, lambda: nc.vector.tensor_copy(out=cf[:, 1, 1, :], in_=a8r), reads=[r_PR], writes=[r_cf])
                        s0, r_s0 = salloc("s0", [64, 2, 64])
                        cx.dma('sp', s0[:, 0, :], T['s0r'][d].rearrange("g p -> p g"), writes=[r_s0], allow_slow_non_contiguous=True)
                        cx.dma('sp', s0[:, 1, :], T['s0i'][d].rearrange("g p -> p g"), writes=[r_s0], allow_slow_non_contiguous=True)
                        p0, r_p0 = salloc("p0", [64, 2, 2, 64])
                        w0, r_w0 = W0[d]
                        cx.op(E, lambda: nc.vector.tensor_tensor(out=p0[:], in0=cf[:], in1=s0[:].unsqueeze(1).to_broadcast([64, 2, 2, 64]), op=ALU.mult),
                              reads=[r_cf, r_s0], writes=[r_p0])
                        cx.op(E, lambda: nc.vector.tensor_tensor(out=w0[:], in0=p0[:, :, 0, :], in1=p0[:, :, 1, :], op=ALU.add), reads=[r_p0], writes=[r_w0])
                        for ci_, (src, dst, rdst) in enumerate(((T['C_re'], CrT, r_CrT), (T['C_im'], CiT, r_CiT))):
                            cin, r_cin = salloc(f"cin{ci_}", [128, 8, 64])
                            cx.dma('sp', cin[:], src[d].rearrange("(gt gi) c p -> (gi c) gt p", gi=8), writes=[r_cin])
                            for half in range(2):
                                pt, rp = PA.next()
                                for q in range(4):
                                    gt = half * 4 + q
                                    cx.op('pe', lambda: nc.tensor.transpose(pt[0:64, q * 128:(q + 1) * 128], cin[:, gt, :], ident[:]),
                                          reads=[r_cin, r_ident], writes=[rp])
                                cx.op('act', lambda: nc.scalar.copy(out=dst[:, half * 32:(half + 1) * 32, :].rearrange("p g c -> p (g c)"), in_=pt[0:64, :]),
                                      reads=[rp], writes=[rdst])
                    ht, r_ht = HT[d]
                    bm, r_bm = bmask[d]
                    GQ = 8
                    for gq in range(64 // GQ):
                      if True:
                        gs = slice(gq * GQ, (gq + 1) * GQ)
                        sh4 = [64, GQ, 8, 16]
                        HR, r_HR = salloc("HR", sh4)
                        HI, r_HI = salloc("HI", sh4)
                        GR, r_GR = salloc("GR", sh4)
                        GN, r_GN = salloc("GN", sh4)
                        TM, r_TM = salloc("TM", sh4)
                        qrb = QR[:, gs, :].unsqueeze(3).to_broadcast(sh4)
                        qib = QI[:, gs, :].unsqueeze(3).to_broadcast(sh4)
                        brb = Br[:, gs, :].unsqueeze(2).to_broadcast(sh4)
                        bib = Bi[:, gs, :].unsqueeze(2).to_broadcast(sh4)
                        prb = PR[:, gs, 8:16].unsqueeze(3).to_broadcast(sh4)
                        pib = PI[:, gs, 8:16].unsqueeze(3).to_broadcast(sh4)
                        crb = CrT[:, gs, :].unsqueeze(2).to_broadcast(sh4)
                        cib = CiT[:, gs, :].unsqueeze(2).to_broadcast(sh4)
                        cx.op('dve', lambda: nc.vector.tensor_tensor(out=HR[:], in0=qrb, in1=brb, op=ALU.mult), reads=[r_QR, r_Br], writes=[r_HR])
                        cx.op('pool', lambda: nc.gpsimd.tensor_tensor(out=TM[:], in0=qib, in1=bib, op=ALU.mult), reads=[r_QI, r_Bi], writes=[r_TM])
                        cx.op('dve', lambda: nc.vector.tensor_tensor(out=HR[:], in0=HR[:], in1=TM[:], op=ALU.subtract), reads=[r_HR, r_TM], writes=[r_HR])
                        cx.op('dve', lambda: nc.vector.tensor_tensor(out=HI[:], in0=qrb, in1=bib, op=ALU.mult), reads=[r_QR, r_Bi], writes=[r_HI])
                        cx.op('pool', lambda: nc.gpsimd.tensor_tensor(out=TM[:], in0=qib, in1=brb, op=ALU.mult), reads=[r_QI, r_Br], writes=[r_TM])
                        cx.op('dve', lambda: nc.vector.tensor_tensor(out=HI[:], in0=HI[:], in1=TM[:], op=ALU.add), reads=[r_HI, r_TM], writes=[r_HI])
                        cx.op('dve', lambda: nc.vector.tensor_tensor(out=GR[:], in0=prb, in1=crb, op=ALU.mult), reads=[r_PR, r_CrT], writes=[r_GR])
                        cx.op('pool', lambda: nc.gpsimd.tensor_tensor(out=TM[:], in0=pib, in1=cib, op=ALU.mult), reads=[r_PI, r_CiT], writes=[r_TM])
                        cx.op('dve', lambda: nc.vector.tensor_tensor(out=GR[:], in0=GR[:], in1=TM[:], op=ALU.subtract), reads=[r_GR, r_TM], writes=[r_GR])
                        cx.op('dve', lambda: nc.vector.tensor_tensor(out=GN[:], in0=prb, in1=cib, op=ALU.mult), reads=[r_PR, r_CiT], writes=[r_GN])
                        cx.op('pool', lambda: nc.gpsimd.tensor_tensor(out=TM[:], in0=pib, in1=crb, op=ALU.mult), reads=[r_PI, r_CrT], writes=[r_TM])
                        cx.op('dve', lambda: nc.vector.tensor_tensor(out=GN[:], in0=GN[:], in1=TM[:], op=ALU.add), reads=[r_GN, r_TM], writes=[r_GN])
                        cx.op('dve', lambda: nc.vector.tensor_scalar(out=GN[:], in0=GN[:], scalar1=-1.0, scalar2=None, op0=ALU.mult), reads=[r_GN], writes=[r_GN])
                        HRf = HR[:].rearrange("p g j c -> p g (j c)")
                        HIf = HI[:].rearrange("p g j c -> p g (j c)")
                        GRf = GR[:].rearrange("p g j c -> p g (j c)")
                        GNf = GN[:].rearrange("p g j c -> p g (j c)")
                        for (src, rsrc, c0) in ((HRf, r_HR, 0), (HIf, r_HI, 64)):
                            for g8 in range(GQ // 8):
                                pt, rp = PA.next()
                                for gi in range(8):
                                    gl = g8 * 8 + gi
                                    cx.op('pe', lambda: nc.tensor.transpose(pt[:, gi * 64:(gi + 1) * 64], src[:, gl, :], ident[0:64, 0:64]),
                                          reads=[rsrc, r_ident], writes=[rp])
                                g0 = gq * GQ + g8 * 8
                                cx.op('act', lambda: nc.scalar.copy(out=ht[:, g0:g0 + 8, c0:c0 + 64], in_=pt[:, :].rearrange("p (g q) -> p g q", q=64)),
                                      reads=[rp], writes=[r_ht])
                        for g4 in range(GQ // 4):
                            pt, rp = PB.next()
                            for gi in range(4):
                                gl = g4 * 4 + gi
                                cx.op('pe', lambda: nc.tensor.matmul(pt[:, gi * 128:(gi + 1) * 128], HRf[:, gl, :], GRf[:, gl, :], start=True, stop=False),
                                      reads=[r_HR, r_GR], writes=[rp])
                                cx.op('pe', lambda: nc.tensor.matmul(pt[:, gi * 128:(gi + 1) * 128], HIf[:, gl, :], GNf[:, gl, :], start=False, stop=True),
                                      reads=[r_HI, r_GN], writes=[rp])
                            g0 = gq * GQ + g4 * 4
                            pv = pt[:, :].rearrange("p (g q) -> p g q", q=128)
                            bmb = bm[:].unsqueeze(1).to_broadcast([128, 4, 128])
                            if d == 0:
                                cx.op('dve', lambda: nc.vector.tensor_tensor(out=MTf[:, g0:g0 + 4, :], in0=pv, in1=bmb, op=ALU.mult),
                                      reads=[rp, r_bm], writes=[r_MTf])
                            else:
                                cx.op('dve', lambda: nc.vector.tensor_tensor(out=pmtmp[:], in0=pv, in1=bmb, op=ALU.mult), reads=[rp, r_bm], writes=[r_pmtmp])
                                cx.op('dve', lambda: nc.vector.tensor_tensor(out=MT16[:, g0:g0 + 4, :], in0=pmtmp[:], in1=MTf[:, g0:g0 + 4, :], op=ALU.add),
                                      reads=[r_pmtmp, r_MTf], writes=[r_MT16])
                        for (src, rsrc, ri) in ((GRf, r_GR, 0), (GNf, r_GN, 1)):
                            gb, r_gb = salloc(f"gb{ri}", [64, GQ, 128], BF16)
                            cx.op('act', lambda: nc.scalar.copy(out=gb[:], in_=src), reads=[rsrc], writes=[r_gb])
                            cx.dma('sp', T['GB_s'][d, ri, :, gs, :], gb[:], reads=[r_gb])
                cx.barrier()
                cx.dma('sp', T['MT_s'][:, :, :], MT16[:], reads=[r_MT16])
                cx.barrier()
            if hook is not None:
                hook()
            with ExitStack() as st3:
                sb3 = lambda name, shape, dt=F32: st3.enter_context(nc.sbuf_tensor("s3_" + name, shape, dt))
                PU = Pool(cx, [st3.enter_context(nc.psum_tensor(f"s3p_PU{i}", [128, 1024], BF16)) for i in range(2)])
                Ub = Pool(cx, [sb3(f"Ub{i}", [64, 8, 1024], BF16) for i in range(2)])
                Ug = Pool(cx, [sb3(f"Ug{i}", [64, 64, 128], BF16) for i in range(2)])
                UTt = Pool(cx, [sb3(f"UTt{i}", [128, 64, 64], BF16) for i in range(2)])
                for seg in range(NSEG):
                    ub, rub = Ub.next(); ug, rug = Ug.next(); ut, rut = UTt.next()
                    cx.dma('pool', ub[:], T['utok_s'][seg * 512:(seg + 1) * 512, :].rearrange("(k j) c -> k j c", j=8), writes=[rub])
                    cx.op('dve', lambda: nc.vector.tensor_copy(out=ug[:].rearrange("p g (j c) -> p g j c", c=16),
                                                               in_=ub[:].rearrange("p j (g c) -> p g j c", c=16)), reads=[rub], writes=[rug])
                    for g8 in range(8):
                        pt, rp = PU.next()
                        for gi in range(8):
                            g = g8 * 8 + gi
                            cx.op('pe', lambda: nc.tensor.transpose(pt[:, gi * 64:(gi + 1) * 64], ug[:, g, :], identb[0:64, 0:64]),
                                  reads=[rug, r_identb], writes=[rp])
                        cx.op('act', lambda: nc.scalar.copy(out=ut[:, g8 * 8:(g8 + 1) * 8, :], in_=pt[:, 0:512].rearrange("p (g k) -> p g k", k=64)),
                              reads=[rp], writes=[rut])
                    cx.dma('sp', T['UT_s'][:, :, seg * 64:(seg + 1) * 64], ut[:], reads=[rut])
                cx.barrier()
            with ExitStack() as st4:
                sb4 = lambda name, shape, dt=F32: st4.enter_context(nc.sbuf_tensor("s4_" + name, shape, dt))
                PE_ = [[(st4.enter_context(nc.psum_tensor(f"s4p_E{d}{ri}", [128, 512], F32)), cx.res()) for ri in range(2)] for d in range(2)]
                PO = (st4.enter_context(nc.psum_tensor("s4p_O", [128, 512], F32)), cx.res())
                UT = [(sb4(f"UT{d}", [128, 64, 64], BF16), cx.res()) for d in range(2)]
                EE = [(sb4(f"EE{d}", [64, 2, 64, 64]), cx.res()) for d in range(2)]
                WW = [(sb4(f"WW{d}", [64, 2, 64, 64], BF16), cx.res()) for d in range(2)]
                Wst = [[(sb4(f"W{d}{i}", [64, 2, 64]), cx.res()) for i in range(2)] for d in range(2)]
                Sst = [(sb4(f"S{d}", [64, 2, 64]), cx.res()) for d in range(2)]
                Pst = [(sb4(f"P{d}", [64, 2, 2, 64]), cx.res()) for d in range(2)]
                OUTS = [(sb4(f"OUTS{d}", [64, 2, NS, 64]), cx.res()) for d in range(2)]
                OS1 = (sb4("OS", [64, 2, NS, 64]), cx.res())
                ENG = ['dve', 'dve']
                EOP = [nc.vector, nc.vector]
                for d in range(2):
                    cx.op(ENG[d], lambda: EOP[d].tensor_copy(out=Wst[d][0][0][:], in_=W0[d][0][:]), reads=[W0[d][1]], writes=[Wst[d][0][1]])
                step = [0, 0]
                for si in range(NSEG):
                    segs = [si, NSEG - 1 - si]
                    for d in range(2):
                        seg = segs[d]
                        ut, rut = UT[d]; ee, ree = EE[d]
                        ht, r_ht = HT[d]
                        cx.dma('sp', ut[:], T['UT_s'][:, :, seg * 64:(seg + 1) * 64], writes=[rut])
                        for g8 in range(8):
                            for ri in range(2):
                                pt, rp = PE_[d][ri]
                                for gi in range(8):
                                    g = g8 * 8 + gi
                                    cx.op('pe', lambda: nc.tensor.matmul(pt[0:64, gi * 64:(gi + 1) * 64], ht[:, g, ri * 64:(ri + 1) * 64], ut[:, g, :],
                                                                         start=True, stop=True), reads=[r_ht, rut], writes=[rp])
                                cx.op('act', lambda: nc.scalar.copy(out=ee[:, ri, g8 * 8:(g8 + 1) * 8, :], in_=pt[0:64, :].rearrange("p (g k) -> p g k", k=64)),
                                      reads=[rp], writes=[ree])
                    for kk_ in range(KS):
                        kks = [kk_, KS - 1 - kk_]
                        cur, nxt = step[0] % 2, (step[0] + 1) % 2
                        for d in range(2):
                            W, rW = Wst[d][cur]
                            cx.op('act', lambda: nc.scalar.copy(out=WW[d][0][:, :, :, kks[d]], in_=W[:]), reads=[rW], writes=[WW[d][1]])
                        for d in range(2):
                            W, rW = Wst[d][cur]; S, rS = Sst[d]
                            cx.op('dve', lambda: nc.vector.tensor_tensor(out=S[:], in0=W[:], in1=EE[d][0][:, :, :, kks[d]], op=ALU.add),
                                  reads=[rW, EE[d][1]], writes=[rS])
                        for d in range(2):
                            S, rS = Sst[d]; P, rP = Pst[d]; cf, r_cf = COEF[d]
                            cx.op('dve', lambda: nc.vector.tensor_tensor(out=P[:], in0=cf[:], in1=S[:].unsqueeze(1).to_broadcast([64, 2, 2, 64]), op=ALU.mult),
                                  reads=[r_cf, rS], writes=[rP])
                        for d in range(2):
                            P, rP = Pst[d]; Wn, rWn = Wst[d][nxt]
                            cx.op('dve', lambda: nc.vector.tensor_tensor(out=Wn[:], in0=P[:, :, 0, :], in1=P[:, :, 1, :], op=ALU.add), reads=[rP], writes=[rWn])
                        step[0] += 1
                        if step[0] % 32 == 0:
                            for d in range(2):
                                S, rS = Sst[d]; Wn, rWn = Wst[d][nxt]
                                cglob = segs[d] * KS + kks[d]
                                slot = cglob // 32
                                osd, r_osd = OUTS[d]
                                cx.op('act', lambda: nc.scalar.copy(out=osd[:, :, slot, :], in_=S[:]), reads=[rS], writes=[r_osd])
                                cx.op('dve', lambda: nc.vector.tensor_scalar(out=Wn[:], in0=Wn[:], scalar1=keep[0:64, 0:1], scalar2=None, op0=ALU.mult),
                                      reads=[rWn, r_keep], writes=[rWn])
                    for d in range(2):
                        ww, rww = WW[d]
                        for ri in range(2):
                            cx.dma('sp', T['WW_s'][d, ri, :, :, segs[d] * 64:(segs[d] + 1) * 64], ww[:, ri, :, :], reads=[rww])
                for d in range(2):
                    osd, r_osd = OUTS[d]; os_, r_os = OS1
                    pt, rp = PO
                    for ri in range(2):
                        for s0_ in range(0, NS, 8):
                            ns = min(8, NS - s0_)
                            for s in range(ns):
                                cx.op('pe', lambda: nc.tensor.transpose(pt[0:64, s * 64:(s + 1) * 64], osd[:, ri, s0_ + s, :], ident[0:64, 0:64]),
                                      reads=[r_osd, r_ident], writes=[rp])
                            cx.op('act', lambda: nc.scalar.copy(out=os_[:, ri, s0_:s0_ + ns, :], in_=pt[0:64, 0:ns * 64].rearrange("p (s q) -> p s q", q=64)),
                                  reads=[rp], writes=[r_os])
                    cx.dma('sp', T['out_sr'][:, d, :, :].rearrange("s g p -> g s p"), os_[:, 0, :, :], reads=[r_os])
                    cx.dma('sp', T['out_si'][:, d, :, :].rearrange("s g p -> g s p"), os_[:, 1, :, :], reads=[r_os])
                cx.barrier()
        with ExitStack() as st5:
            sb5 = lambda name, shape, dt=F32: st5.enter_context(nc.sbuf_tensor("s5_" + name, shape, dt))
            GBs = [(sb5(f"GBs{d}", [128, 64, 128], BF16), cx.res()) for d in range(2)]
            MT16 = sb5("MT16y", [128, 64, 128], BF16); r_MT16 = cx.res()
            cx.dma('sp', MT16[:], T['MT_s'][:, :, :], writes=[r_MT16])
            for d in range(2):
                cx.dma('sp', GBs[d][0][:], T['GB_s'][d].rearrange("r p g q -> (r p) g q"), writes=[GBs[d][1]])
            UTy = Pool(cx, [sb5(f"UTy{i}", [128, 64, 64], BF16) for i in range(2)])
            WWy = [Pool(cx, [sb5(f"WWy{d}{i}", [128, 64, 64], BF16) for i in range(2)]) for d in range(2)]
            YS = Pool(cx, [sb5(f"YS{i}", [128, 8, 64]) for i in range(2)])
            YT = Pool(cx, [sb5(f"YT{i}", [64, 8, 1024]) for i in range(1)])
            PY = Pool(cx, [st5.enter_context(nc.psum_tensor(f"s5p_Y{i}", [128, 512], F32)) for i in range(2)])
            PT = Pool(cx, [st5.enter_context(nc.psum_tensor(f"s5p_T{i}", [128, 1024], F32)) for i in range(2)])
            for seg in range(NSEG):
                ut, rut = UTy.next()
                cx.dma('sp', ut[:], T['UT_s'][:, :, seg * 64:(seg + 1) * 64], writes=[rut])
                wws = []
                for d in range(2):
                    w_, rw_ = WWy[d].next()
                    cx.dma('sp', w_[:], T['WW_s'][d, :, :, :, seg * 64:(seg + 1) * 64].rearrange("r p g k -> (r p) g k"), writes=[rw_])
                    wws.append((w_, rw_))
                yt, ryt = YT.next()
                for g8 in range(8):
                    py, rpy = PY.next()
                    for gi in range(8):
                        g = g8 * 8 + gi
                        cx.op('pe', lambda: nc.tensor.matmul(py[:, gi * 64:(gi + 1) * 64], MT16[:, g, :], ut[:, g, :], start=True, stop=False),
                              reads=[r_MT16, rut], writes=[rpy])
                        for d in range(2):
                            cx.op('pe', lambda: nc.tensor.matmul(py[:, gi * 64:(gi + 1) * 64], GBs[d][0][:, g, :], wws[d][0][:, g, :], start=False, stop=(d == 1)),
                                  reads=[GBs[d][1], wws[d][1]], writes=[rpy])
                    ys, rys = YS.next()
                    cx.op('act', lambda: nc.scalar.copy(out=ys[:], in_=py[:, :].rearrange("p (g k) -> p g k", k=64)), reads=[rpy], writes=[rys])
                    pt, rpt = PT.next()
                    for gi in range(8):
                        cx.op('pe', lambda: nc.tensor.transpose(pt[0:64, gi * 128:(gi + 1) * 128], ys[:, gi, :], ident[:]),
                              reads=[rys, r_ident], writes=[rpt])
                    cx.op('dve', lambda: nc.vector.tensor_copy(out=yt[:, :, g8 * 128:(g8 + 1) * 128].rearrange("p t (g c) -> p g t c", c=16),
                                                               in_=pt[0:64, :].rearrange("p (g t c) -> p g t c", t=8, c=16)),
                          reads=[rpt], writes=[ryt])
                cx.dma('sp', T['ys_s'][seg * 512:(seg + 1) * 512, :].rearrange("(k t) c -> k t c", t=8), yt[:], reads=[ryt])
            cx.barrier()


def s5_consts():
    expo = np.zeros((2, 18), np.float32)
    for j in range(8):
        expo[0, j] = 7 - j; expo[0, 8 + j] = j - 7
        expo[1, j] = j; expo[1, 8 + j] = -j
    expo[:, 16] = 8; expo[:, 17] = 1
    jj = np.arange(128) // 16
    bmask = np.stack([(jj[:, None] <= jj[None, :]), (jj[:, None] >= jj[None, :])]).astype(np.float32)
    return expo, bmask


class WCache:
    def __init__(self, nc, cx, T):
        self.nc, self.cx, self.T = nc, cx, T
        self.blocks = {}

    def view(self, blk, kc_n):
        return self.T['wbf_s'][blk, :, 0:kc_n * 512].rearrange("p (k n) -> p k n", n=512)

    def load(self, pool, key, src_view, kc_n):
        cx = self.cx
        wt, rw = pool.next()
        if key not in self.blocks:
            blk = len(self.blocks)
            assert blk < NWBLK
            wres = cx.res()
            self.blocks[key] = (blk, wres)
            cx.dma('pool', wt[:, 0:kc_n, :], src_view, writes=[rw])
            cx.dma('sp', self.view(blk, kc_n), wt[:, 0:kc_n, :], reads=[rw], writes=[wres], sres=rw)
        else:
            blk, wres = self.blocks[key]
            cx.dma('sp', wt[:, 0:kc_n, :], self.view(blk, kc_n), reads=[wres], writes=[rw], sres=rw)
        return wt, rw

    def bounce_load(self, tile, res, key, src_view, kc_n):
        assert key not in self.blocks
        cx = self.cx
        blk = len(self.blocks)
        assert blk < NWBLK
        wres = cx.res()
        self.blocks[key] = (blk, wres)
        cx.dma('pool', tile[:, 0:kc_n, :], src_view, writes=[res])
        return (tile, res, blk, wres, kc_n)

    def bounce_store(self, pend):
        tile, res, blk, wres, kc_n = pend
        self.cx.dma('sp', self.view(blk, kc_n), tile[:, 0:kc_n, :], reads=[res], writes=[wres], sres=res)

    def precast(self, key, src_view, kc_n):
        if key in self.blocks:
            return
        cx = self.cx
        if not hasattr(self, 'pre_res'):
            self.pre_res = cx.res()
        blk = len(self.blocks)
        assert blk < NWBLK
        wres = cx.res()
        self.blocks[key] = (blk, wres)
        cx.dma('pool', self.view(blk, kc_n), src_view, writes=[wres], sres=self.pre_res)


class Prefetch:
    def __init__(self, wc, pool, reqs):
        self.wc, self.pool, self.reqs = wc, pool, reqs
        self.nbuf = len(pool.tiles)
        self.issued = []
        self.taken = 0
        self.released = 0

    def top_up(self):
        while len(self.issued) < len(self.reqs) and len(self.issued) < self.released + self.nbuf:
            key, view, kc_n = self.reqs[len(self.issued)]
            self.issued.append(self.wc.load(self.pool, key, view, kc_n))

    def take(self, key):
        self.top_up()
        assert self.taken < len(self.issued), "prefetch: consumer holds too many blocks"
        assert self.reqs[self.taken][0] == key, (self.reqs[self.taken][0], key)
        r = self.issued[self.taken]
        self.taken += 1
        return r

    def release(self, n=1):
        self.released += n
        self.top_up()


NWBLK = 68


def stage0_mod(nc, cx, T):
    with ExitStack() as st:
        sb = lambda name, shape, dt=F32: st.enter_context(nc.sbuf_tensor("z_" + name, shape, dt))
        pp = Pool(cx, [st.enter_context(nc.psum_tensor(f"zp{i}", [128, 512], F32)) for i in range(2)])
        MOD = sb("MOD", [128, 6, D]); r_mod = cx.res()
        cT = sb("cT", [128, KC]); r_cT = cx.res()
        cs = sb("cs", [128, KC]); r_cs = cx.res()
        csrep = sb("csrep", [128, KC, 128]); r_csrep = cx.res()
        g1row = sb("g1row", [128, D]); r_g1row = cx.res()
        g2row = sb("g2row", [128, D]); r_g2row = cx.res()
        wm = Pool(cx, [sb(f"wm{i}", [128, KC, 512]) for i in range(2)])
        bm = Pool(cx, [sb(f"bm{i}", [128, 512]) for i in range(2)])
        cx.dma('sp', cT[:], T['cvec'].rearrange("(k p) -> p k", p=128), writes=[r_cT], allow_slow_non_contiguous=True)
        cx.dma('sp', g1row[:], T['g1'].partition_broadcast(128), writes=[r_g1row])
        cx.dma('sp', g2row[:], T['g2'].partition_broadcast(128), writes=[r_g2row])
        cx.op('act', lambda: nc.scalar.activation(out=cs[:], in_=cT[:], func=AF.Silu), reads=[r_cT], writes=[r_cs])
        cx.op('dve', lambda: nc.vector.tensor_copy(out=csrep[:], in_=cs[:].unsqueeze(2).to_broadcast([128, KC, 128])),
              reads=[r_cs], writes=[r_csrep])
        wmv = T['w_mod'].rearrange("(k p) n -> p k n", p=128)
        for blk in range(24):
            wt, rw = wm.next()
            bt, rb = bm.next()
            pt, rp = pp.next()
            cx.dma('sp', wt[:], wmv[:, :, blk * 512:(blk + 1) * 512], writes=[rw])
            cx.dma('sp', bt[:], T['b_mod'][blk * 512:(blk + 1) * 512].partition_broadcast(128), writes=[rb])
            for k in range(KC):
                cx.op('pe', lambda: nc.tensor.matmul(pt[:], csrep[:, k, :], wt[:, k, :], start=(k == 0), stop=(k == KC - 1)),
                      reads=[r_csrep, rw], writes=[rp])
            mi, c0 = blk // 4, (blk % 4) * 512
            cx.op('dve', lambda: nc.vector.tensor_tensor(out=MOD[:, mi, c0:c0 + 512], in0=pt[:], in1=bt[:], op=ALU.add),
                  reads=[rp, rb], writes=[r_mod])
        for (mi, grow, rg) in ((1, g1row, r_g1row), (4, g2row, r_g2row)):
            cx.op('dve', lambda: nc.vector.scalar_tensor_tensor(out=MOD[:, mi, :], in0=MOD[:, mi, :], scalar=1.0, in1=grow[:],
                                                                op0=ALU.add, op1=ALU.mult),
                  reads=[r_mod, rg], writes=[r_mod])
        cx.dma('sp', T['mod_s'][:, :], MOD[0:1, :, :], reads=[r_mod])
        cx.barrier()


def stage1_inproj(nc, cx, T, NT, ident, r_ident, wc, pre_list=()):
    NTT = NT // 512
    with ExitStack() as st:
        sb = lambda name, shape, dt=F32: st.enter_context(nc.sbuf_tensor("a_" + name, shape, dt))
        psb = [st.enter_context(nc.psum_tensor(f"ap{i}", [128, 512], F32)) for i in range(8)]
        A1 = sb("A1", [128, D]); r_A1 = cx.res()
        B1 = sb("B1", [128, D]); r_B1 = cx.res()
        cx.dma('sp', A1[:], T['mod_s'][1, :].partition_broadcast(128), writes=[r_A1])
        cx.dma('sp', B1[:], T['mod_s'][0, :].partition_broadcast(128), writes=[r_B1])
        xt = Pool(cx, [sb(f"xt{i}", [128, D]) for i in range(2)])
        junk = sb("junk", [128, D], BF16); r_junk = cx.res()
        stat = Pool(cx, [sb(f"stat{i}", [128, 4]) for i in range(2)])
        hx = Pool(cx, [sb(f"hx{i}", [128, D]) for i in range(2)])
        hT = sb("hT", [128, KC, 512], BF16); r_hT = cx.res()
        WB = Pool(cx, [sb(f"WB{i}", [128, KC, 512], BF16) for i in range(4)])
        WG = sb("WG", [128, KC, 32], BF16); r_WG = cx.res()
        bg = sb("bg", [32, 1]); r_bg = cx.res()
        stg_b = Pool(cx, [sb(f"stgb{i}", [128, 512], BF16) for i in range(4)])
        stg_f = Pool(cx, [sb(f"stgf{i}", [128, 512], F32) for i in range(3)])
        ptr = Pool(cx, [psb[0], psb[1]])
        pfm = Pool(cx, [psb[2], psb[3]])
        ptm = [psb[4], psb[5], psb[6], psb[7]]
        r_ptm = [cx.res() for _ in range(4)]
        winv = T['w_in'].rearrange("(k p) n -> p k n", p=128)
        cx.dma('pool', WG[:], winv[:, :, 4096:4128], writes=[r_WG])
        cx.dma('sp', bg[:], T['b_gates'].rearrange("(p o) -> p o", o=1), writes=[r_bg])
        identb = sb("identb", [128, 128], BF16); r_identb = cx.res()
        cx.op('dve', lambda: nc.vector.tensor_copy(out=identb[:], in_=ident[:]), reads=[r_ident], writes=[r_identb])
        evi = [0]

        def evac(out, in_, rreads, rwrites, func=None, scale=1.0, bias=None):
            if func is not None or bias is not None:
                kw = {}
                if bias is not None:
                    kw['bias'] = bias
                cx.op('act', lambda: nc.scalar.activation(out=out, in_=in_, func=(func or AF.Identity), scale=scale, **kw),
                      reads=rreads, writes=rwrites)
                return
            evi[0] += 1
            if evi[0] % 2 == 0:
                cx.op('act', lambda: nc.scalar.activation(out=out, in_=in_, func=AF.Copy, scale=scale), reads=rreads, writes=rwrites)
            else:
                if scale == 1.0:
                    cx.op('dve', lambda: nc.vector.tensor_copy(out=out, in_=in_), reads=rreads, writes=rwrites)
                else:
                    cx.op('dve', lambda: nc.vector.tensor_scalar(out=out, in0=in_, scalar1=scale, scalar2=None, op0=ALU.mult),
                          reads=rreads, writes=rwrites)

        blocks = [(0, 'q'), (512, 'q'), (1024, 'k'), (1536, 'k'), (2048, 'v'), (2560, 'v'), (3072, 'o'), (3584, 'o'),
                  (4128, 'u'), (4640, 'u')] + [(5152 + 512 * i, 'mg') for i in range(8)]
        pf = Prefetch(wc, WB, [(('w_in', c0), winv[:, :, c0:c0 + 512], KC) for _ in range(NTT) for (c0, _k) in blocks])
        bnc = Pool(cx, [sb(f"bnc{i}", [128, KC, 512], BF16) for i in range(2)])
        pre_list = list(pre_list)
        pre_state = {'i': 0, 'pend': None}

        def precast_slot():
            if pre_state['pend'] is not None:
                wc.bounce_store(pre_state['pend'])
                pre_state['pend'] = None
            if pre_state['i'] < len(pre_list):
                key, view, kc_n = pre_list[pre_state['i']]
                pre_state['i'] += 1
                bt, br = bnc.next()
                pre_state['pend'] = wc.bounce_load(bt, br, key, view, kc_n)
        for tt in range(NTT):
            tok0 = tt * 512
            for sub in range(4):
                xx, rx = xt.next()
                sx, rs = stat.next()
                hh, rh = hx.next()
                cx.dma('sp', xx[:], T['x'][tok0 + sub * 128: tok0 + (sub + 1) * 128, :], writes=[rx])
                cx.op('act', lambda: nc.scalar.activation(out=junk[:], in_=xx[:], func=AF.Square, accum_out=sx[:, 0:1]),
                      reads=[rx], writes=[r_junk, rs])
                cx.op('dve', lambda: nc.vector.tensor_scalar(out=sx[:, 1:2], in0=sx[:, 0:1], scalar1=1.0 / D, scalar2=EPS,
                                                             op0=ALU.mult, op1=ALU.add), reads=[rs], writes=[rs])
                cx.op('act', lambda: nc.scalar.activation(out=sx[:, 2:3], in_=sx[:, 1:2], func=AF.Sqrt), reads=[rs], writes=[rs])
                cx.op('dve', lambda: nc.vector.reciprocal(out=sx[:, 3:4], in_=sx[:, 2:3]), reads=[rs], writes=[rs])
                cx.op('dve', lambda: nc.vector.scalar_tensor_tensor(out=hh[:], in0=xx[:], scalar=sx[:, 3:4], in1=A1[:],
                                                                    op0=ALU.mult, op1=ALU.mult),
                      reads=[rx, rs, r_A1], writes=[rh])
                cx.op('pool', lambda: nc.gpsimd.tensor_tensor(out=hh[:], in0=hh[:], in1=B1[:], op=ALU.add),
                      reads=[rh, r_B1], writes=[rh])
                for g in range(4):
                    pt, rp = ptr.next()
                    for j in range(4):
                        k = g * 4 + j
                        cx.op('pe', lambda: nc.tensor.transpose(pt[:, j * 128:(j + 1) * 128], hh[:, k * 128:(k + 1) * 128], ident[:]),
                              reads=[rh, r_ident], writes=[rp])
                    evac(hT[:, g * 4:(g + 1) * 4, sub * 128:(sub + 1) * 128], pt[:].rearrange("p (a b) -> p a b", a=4), [rp], [r_hT])
            for (c0, kind) in blocks:
                wt, rw = pf.take(('w_in', c0))
                if tt >= 1:
                    precast_slot()
                if kind in ('q', 'k', 'mg'):
                    for j in range(4):
                        pt, rp = pfm.next()
                        for k in range(KC):
                            cx.op('pe', lambda: nc.tensor.matmul(pt[:], wt[:, k, j * 128:(j + 1) * 128], hT[:, k, :],
                                                                 start=(k == 0), stop=(k == KC - 1)),
                                  reads=[rw, r_hT], writes=[rp])
                        sg, rsg = stg_b.next()
                        if kind == 'q':
                            head = (c0 // 128) + j
                            evac(sg[:], pt[:], [rp], [rsg])
                            cx.dma('sp', T['qT_s'][head, :, tok0:tok0 + 512], sg[:], reads=[rsg])
                        elif kind == 'k':
                            head = ((c0 - 1024) // 128) + j
                            evac(sg[:], pt[:], [rp], [rsg], scale=128.0 ** -0.5)
                            cx.dma('sp', T['kT_s'][head, :, tok0:tok0 + 512], sg[:], reads=[rsg])
                        else:
                            ch = ((c0 - 5152) // 128) + j
                            evac(sg[:], pt[:], [rp], [rsg], func=AF.Sigmoid)
                            cx.dma('sp', T['mgT_s'][ch * 128:(ch + 1) * 128, tok0:tok0 + 512], sg[:], reads=[rsg])
                if kind in ('k', 'v', 'o', 'u'):
                    for k in range(KC):
                        for sub in range(4):
                            cx.op('pe', lambda: nc.tensor.matmul(ptm[sub][:], hT[:, k, sub * 128:(sub + 1) * 128], wt[:, k, :],
                                                                 start=(k == 0), stop=(k == KC - 1)),
                                  reads=[rw, r_hT], writes=[r_ptm[sub]])
                    for sub in range(4):
                        rows = slice(tok0 + sub * 128, tok0 + (sub + 1) * 128)
                        if kind == 'u':
                            sg, rsg = stg_f.next()
                            cc = c0 - 4128
                            evac(sg[:], ptm[sub][:], [r_ptm[sub]], [rsg])
                            cx.dma('sp', T['utok_s'][rows, cc:cc + 512], sg[:], reads=[rsg])
                        else:
                            sg, rsg = stg_b.next()
                            if kind == 'k':
                                cc = c0 - 1024
                                evac(sg[:], ptm[sub][:], [r_ptm[sub]], [rsg], scale=128.0 ** -0.5)
                                cx.dma('sp', T['ktok_s'][rows, cc:cc + 512], sg[:], reads=[rsg])
                            elif kind == 'v':
                                cc = c0 - 2048
                                evac(sg[:], ptm[sub][:], [r_ptm[sub]], [rsg])
                                cx.dma('sp', T['vtok_s'][rows, cc:cc + 512], sg[:], reads=[rsg])
                            else:
                                cc = c0 - 3072
                                evac(sg[:], ptm[sub][:], [r_ptm[sub]], [rsg], func=AF.Sigmoid)
                                cx.dma('sp', T['otok_s'][rows, cc:cc + 512], sg[:], reads=[rsg])
                pf.release(1)
            pt, rp = pfm.next()
            for k in range(KC):
                cx.op('pe', lambda: nc.tensor.matmul(pt[0:32, :], WG[:, k, :], hT[:, k, :], start=(k == 0), stop=(k == KC - 1)),
                      reads=[r_WG, r_hT], writes=[rp])
            sg, rsg = stg_f.next()
            evac(sg[0:32, :], pt[0:32, :], [rp, r_bg], [rsg], bias=bg[:, 0:1])
            cx.dma('sp', T['gates_s'][:, tok0:tok0 + 512], sg[0:32, :], reads=[rsg])
        while pre_state['pend'] is not None or pre_state['i'] < len(pre_list):
            precast_slot()
        cx.barrier()


def stage3_reqs(T, NTT):
    wupv = T['w_up_a'].rearrange("(k p) n -> p k n", p=128)
    wgluv = T['w_glu'].rearrange("(k p) n -> p k n", p=128)
    woutv = T['w_out'].rearrange("(k p) n -> p k n", p=128)
    wfiv = T['w_ffn_in'].rearrange("(k p) n -> p k n", p=128)
    wfov = T['w_ffn_out'].rearrange("(k p) n -> p k n", p=128)
    fparts = [(0, 4), (4, 4), (8, 3)]
    reqs = []
    for _tt in range(NTT):
        for grp in range(4):
            c0 = grp * 512
            reqs.append((('up', c0), wupv[:, :, c0:c0 + 512], 8))
            reqs.append((('glu', c0), wgluv[:, :, c0:c0 + 512], 8))
            reqs.append((('glu', 2048 + c0), wgluv[:, :, 2048 + c0:2048 + c0 + 512], 8))
        for nb in range(4):
            reqs.append((('out', nb), woutv[:, :, nb * 512:(nb + 1) * 512], KC))
        for (fb0, nfb) in fparts:
            for fb in range(nfb):
                f0 = (fb0 + fb) * 512
                reqs.append((('fi', f0), wfiv[:, :, f0:f0 + 512], KC))
                reqs.append((('fi', D_FF + f0), wfiv[:, :, D_FF + f0:D_FF + f0 + 512], KC))
            nfc = nfb * 4
            for nb in range(4):
                reqs.append((('fo', fb0, nb), wfov[:, fb0 * 4:fb0 * 4 + nfc, nb * 512:(nb + 1) * 512], nfc))
    return reqs


def stage3_rest(nc, cx, T, NT, ident, r_ident, wc):
    NTT = NT // 512
    with ExitStack() as st:
        sb = lambda name, shape, dt=F32: st.enter_context(nc.sbuf_tensor("c_" + name, shape, dt))

        def rowload(tile_ap, res_list, idx):
            cx.dma('sp', tile_ap, T['mod_s'][idx, :].partition_broadcast(128), writes=res_list)

        identb = sb("identb", [128, 128], BF16); r_identb = cx.res()
        cx.op('dve', lambda: nc.vector.tensor_copy(out=identb[:], in_=ident[:]), reads=[r_ident], writes=[r_identb])
        x1 = sb("x1", [128, 4, D]); r_x1 = [cx.res() for _ in range(4)]
        HA = sb("HA", [128, KC, 512], BF16); r_HA = cx.res()
        WB = Pool(cx, [sb(f"WB{i}", [128, KC, 512], BF16) for i in range(4)])
        stat = Pool(cx, [sb(f"stat{i}", [128, 4]) for i in range(2)])
        tmp512 = Pool(cx, [sb(f"tmp512{i}", [128, 512]) for i in range(3)])
        reqs = stage3_reqs(T, NTT)
        fparts = [(0, 4), (4, 4), (8, 3)]
        pf = Prefetch(wc, WB, reqs)
        banks = [(st.enter_context(nc.psum_tensor(f"cps{i}", [128, 512], F32)), cx.res()) for i in range(8)]

        class BPool:
            def __init__(self, idxs, bf16=False):
                self.items = [((banks[i][0][:].bitcast(BF16) if bf16 else banks[i][0][:]), banks[i][1]) for i in idxs]
                self.i = 0

            def next(self):
                it = self.items[self.i]
                self.i = (self.i + 1) % len(self.items)
                return it

        def rstd_of(src_ap, src_res, junk_ap, junk_res):
            sx, rs = stat.next()
            cx.op('act', lambda: nc.scalar.activation(out=junk_ap, in_=src_ap, func=AF.Square, accum_out=sx[:, 0:1]),
                  reads=src_res, writes=junk_res + [rs])
            cx.op('dve', lambda: nc.vector.tensor_scalar(out=sx[:, 1:2], in0=sx[:, 0:1], scalar1=1.0 / D, scalar2=EPS,
                                                         op0=ALU.mult, op1=ALU.add), reads=[rs], writes=[rs])
            cx.op('act', lambda: nc.scalar.activation(out=sx[:, 2:3], in_=sx[:, 1:2], func=AF.Sqrt), reads=[rs], writes=[rs])
            cx.op('dve', lambda: nc.vector.reciprocal(out=sx[:, 3:4], in_=sx[:, 2:3]), reads=[rs], writes=[rs])
            return sx, rs

        for tt in range(NTT):
            tok0 = tt * 512
            for sub in range(4):
                cx.dma('sp', x1[:, sub, :], T['x'][tok0 + sub * 128: tok0 + (sub + 1) * 128, :], writes=[r_x1[sub]])
            with ExitStack() as sa:
                sba = lambda name, shape, dt=F32: sa.enter_context(nc.sbuf_tensor(f"ca{tt}_" + name, shape, dt))
                gt1 = sba("gt1", [128, D]); r_gt1 = cx.res()
                rowload(gt1[:], [r_gt1], 2)
                MD = sba("MD", [128, 2, 1024]); r_MD = cx.res()
                cx.dma('sp', MD[:, 0, :], T['mh_g'].partition_broadcast(128), writes=[r_MD])
                cx.dma('sp', MD[:, 1, :], T['s5_D'].partition_broadcast(128), writes=[r_MD])
                mhg, Drow = MD[:, 0, :], MD[:, 1, :]
                mT = sba("mT", [128, KC, 512], BF16); r_mT = cx.res()
                LDA = sba("LDA", [128, 2, 1024]); r_LDA = [cx.res(), cx.res()]
                LDB = sba("LDB", [128, 2, 1024]); r_LDB = [cx.res(), cx.res()]
                LDO = sba("LDO", [128, 2, 1024], BF16); r_LDO = [cx.res(), cx.res()]
                hb16 = Pool(cx, [sba(f"hb16{i}", [128, 1024], BF16) for i in range(2)])
                mg = Pool(cx, [sba(f"mg{i}", [128, 512], BF16) for i in range(4)])
                sx8p = Pool(cx, [sba(f"sx8_{i}", [128, 3, 8]) for i in range(2)])
                li = [0]

                def ldnext():
                    i = li[0] % 2
                    li[0] += 1
                    return i
                ptb = BPool([6, 7], bf16=True)
                for sub in range(4):
                    rws = slice(tok0 + sub * 128, tok0 + (sub + 1) * 128)
                    i = ldnext()
                    ta, ra, tb, rb, to, ro = LDA[:, i, :], r_LDA[i], LDB[:, i, :], r_LDB[i], LDO[:, i, :], r_LDO[i]
                    cx.dma('sp', ta, T['hf_s'][rws, :], writes=[ra])
                    cx.dma('sp', tb, T['hb_s'][rws, :], writes=[rb])
                    cx.dma('sp', to, T['otok_s'][rws, :], writes=[ro])
                    cx.op('dve', lambda: nc.vector.tensor_tensor(out=ta, in0=ta, in1=tb, op=ALU.add), reads=[ra, rb], writes=[ra])
                    cx.op('pool', lambda: nc.gpsimd.tensor_tensor(out=tb, in0=ta, in1=ta, op=ALU.mult), reads=[ra], writes=[rb])
                    sx8, r_sx8 = sx8p.next()
                    cx.op('dve', lambda: nc.vector.tensor_reduce(out=sx8[:, 0, :], in_=tb.rearrange("p (h e) -> p h e", h=8), axis=AX.X, op=ALU.add),
                          reads=[rb], writes=[r_sx8])
                    cx.op('dve', lambda: nc.vector.tensor_scalar(out=sx8[:, 1, :], in0=sx8[:, 0, :], scalar1=1.0 / 128, scalar2=EPS, op0=ALU.mult, op1=ALU.add),
                          reads=[r_sx8], writes=[r_sx8])
                    cx.op('act', lambda: nc.scalar.activation(out=sx8[:, 1, :], in_=sx8[:, 1, :], func=AF.Sqrt), reads=[r_sx8], writes=[r_sx8])
                    cx.op('dve', lambda: nc.vector.reciprocal(out=sx8[:, 2, :], in_=sx8[:, 1, :]), reads=[r_sx8], writes=[r_sx8])
                    cx.op('dve', lambda: nc.vector.tensor_tensor(out=ta.rearrange("p (h e) -> p h e", h=8), in0=ta.rearrange("p (h e) -> p h e", h=8),
                                                                 in1=sx8[:, 2, :].unsqueeze(2).to_broadcast([128, 8, 128]), op=ALU.mult),
                          reads=[ra, r_sx8], writes=[ra])
                    cx.op('pool', lambda: nc.gpsimd.tensor_tensor(out=ta, in0=ta, in1=mhg, op=ALU.mult), reads=[ra, r_MD], writes=[ra])
                    h16, rh16 = hb16.next()
                    cx.op('dve', lambda: nc.vector.tensor_tensor(out=h16[:], in0=ta, in1=to, op=ALU.mult), reads=[ra, ro], writes=[rh16])
                    pt, rp = ptb.next()
                    for e in range(8):
                        cx.op('pe', lambda: nc.tensor.transpose(pt[:, e * 128:(e + 1) * 128], h16[:, e * 128:(e + 1) * 128], identb[:]),
                              reads=[rh16, r_identb], writes=[rp])
                    cx.op('act', lambda: nc.scalar.copy(out=HA[:, 0:8, sub * 128:(sub + 1) * 128], in_=pt[:, :].rearrange("p (e t) -> p e t", e=8)),
                          reads=[rp], writes=[r_HA])
                    i = ldnext()
                    ta, ra, tb, rb = LDA[:, i, :], r_LDA[i], LDB[:, i, :], r_LDB[i]
                    cx.dma('sp', ta, T['ys_s'][rws, :], writes=[ra])
                    cx.dma('sp', tb, T['utok_s'][rws, :], writes=[rb])
                    cx.op('pool', lambda: nc.gpsimd.tensor_tensor(out=tb, in0=tb, in1=Drow, op=ALU.mult), reads=[rb, r_MD], writes=[rb])
                    cx.op('dve', lambda: nc.vector.tensor_tensor(out=ta, in0=ta, in1=tb, op=ALU.add), reads=[ra, rb], writes=[ra])
                    h16, rh16 = hb16.next()
                    cx.op('act', lambda: nc.scalar.activation(out=h16[:], in_=ta, func=AF.Gelu_apprx_tanh), reads=[ra], writes=[rh16])
                    pt, rp = ptb.next()
                    for e in range(8):
                        cx.op('pe', lambda: nc.tensor.transpose(pt[:, e * 128:(e + 1) * 128], h16[:, e * 128:(e + 1) * 128], identb[:]),
                              reads=[rh16, r_identb], writes=[rp])
                    cx.op('act', lambda: nc.scalar.copy(out=HA[:, 8:16, sub * 128:(sub + 1) * 128], in_=pt[:, :].rearrange("p (e t) -> p e t", e=8)),
                          reads=[rp], writes=[r_HA])
                P1 = BPool([0, 1]); P2 = BPool([2, 3]); P3 = BPool([4, 5])
                for grp in range(4):
                    c0 = grp * 512
                    wu, rwu = pf.take(('up', c0))
                    wa, rwa = pf.take(('glu', c0))
                    wb_, rwb = pf.take(('glu', 2048 + c0))
                    for j in range(4):
                        dc = grp * 4 + j
                        p1, rp1 = P1.next(); p2, rp2 = P2.next(); p3, rp3 = P3.next()
                        for e in range(8):
                            cx.op('pe', lambda: nc.tensor.matmul(p1[:], wu[:, e, j * 128:(j + 1) * 128], HA[:, e, :], start=(e == 0), stop=(e == 7)),
                                  reads=[rwu, r_HA], writes=[rp1])
                        for e in range(8):
                            cx.op('pe', lambda: nc.tensor.matmul(p2[:], wa[:, e, j * 128:(j + 1) * 128], HA[:, 8 + e, :], start=(e == 0), stop=(e == 7)),
                                  reads=[rwa, r_HA], writes=[rp2])
                        for e in range(8):
                            cx.op('pe', lambda: nc.tensor.matmul(p3[:], wb_[:, e, j * 128:(j + 1) * 128], HA[:, 8 + e, :], start=(e == 0), stop=(e == 7)),
                                  reads=[rwb, r_HA], writes=[rp3])
                        mga, rmga = mg.next(); mgb, rmgb = mg.next()
                        cx.dma('sp', mga[:], T['mgT_s'][dc * 128:(dc + 1) * 128, tok0:tok0 + 512], writes=[rmga])
                        cx.dma('sp', mgb[:], T['mgT_s'][2048 + dc * 128:2048 + (dc + 1) * 128, tok0:tok0 + 512], writes=[rmgb])
                        t1, rt1 = tmp512.next(); t2, rt2 = tmp512.next()
                        cx.op('act', lambda: nc.scalar.activation(out=t1[:], in_=p3[:], func=AF.Sigmoid), reads=[rp3], writes=[rt1])
                        cx.op('dve', lambda: nc.vector.tensor_tensor(out=t1[:], in0=p2[:], in1=t1[:], op=ALU.mult), reads=[rp2, rt1], writes=[rt1])
                        cx.op('pool', lambda: nc.gpsimd.tensor_tensor(out=t1[:], in0=t1[:], in1=mgb[:], op=ALU.mult), reads=[rt1, rmgb], writes=[rt1])
                        cx.op('dve', lambda: nc.vector.tensor_tensor(out=t2[:], in0=p1[:], in1=mga[:], op=ALU.mult), reads=[rp1, rmga], writes=[rt2])
                        cx.op('pool', lambda: nc.gpsimd.tensor_tensor(out=mT[:, dc, :], in0=t1[:], in1=t2[:], op=ALU.add), reads=[rt1, rt2], writes=[r_mT])
                    pf.release(3)
                A2row, B2row = MD[:].rearrange("p a b -> p (a b)"), LDB[:].rearrange("p a b -> p (a b)")
                hxt, junk = LDA[:].rearrange("p a b -> p (a b)"), LDO[:].rearrange("p a b -> p (a b)")
                rowload(A2row, [r_MD], 4)
                rowload(B2row, r_LDB, 3)
                wos = [pf.take(('out', nb)) for nb in range(4)]
                po = [banks[i] for i in (4, 5, 6, 7)]
                ptr = BPool([0, 1, 2, 3])
                for sub in range(4):
                    for nb in range(4):
                        wo, rwo = wos[nb]
                        for k in range(KC):
                            cx.op('pe', lambda: nc.tensor.matmul(po[nb][0][:], mT[:, k, sub * 128:(sub + 1) * 128], wo[:, k, :],
                                                                 start=(k == 0), stop=(k == KC - 1)),
                                  reads=[rwo, r_mT], writes=[po[nb][1]])
                    for nb in range(4):
                        t1, rt1 = tmp512.next()
                        cx.op('dve', lambda: nc.vector.tensor_tensor(out=t1[:], in0=po[nb][0][:], in1=gt1[:, nb * 512:(nb + 1) * 512], op=ALU.mult),
                              reads=[po[nb][1], r_gt1], writes=[rt1])
                        cx.op('pool', lambda: nc.gpsimd.tensor_tensor(out=x1[:, sub, nb * 512:(nb + 1) * 512], in0=x1[:, sub, nb * 512:(nb + 1) * 512],
                                                                      in1=t1[:], op=ALU.add), reads=[rt1, r_x1[sub]], writes=[r_x1[sub]])
                    sx, rs = rstd_of(x1[:, sub, :], [r_x1[sub]], junk, r_LDO)
                    cx.op('dve', lambda: nc.vector.scalar_tensor_tensor(out=hxt, in0=x1[:, sub, :], scalar=sx[:, 3:4], in1=A2row,
                                                                        op0=ALU.mult, op1=ALU.mult), reads=[r_x1[sub], rs, r_MD], writes=r_LDA)
                    cx.op('pool', lambda: nc.gpsimd.tensor_tensor(out=hxt, in0=hxt, in1=B2row, op=ALU.add), reads=r_LDA + r_LDB, writes=r_LDA)
                    for g in range(4):
                        pt, rp = ptr.next()
                        for j in range(4):
                            k = g * 4 + j
                            cx.op('pe', lambda: nc.tensor.transpose(pt[:, j * 128:(j + 1) * 128], hxt[:, k * 128:(k + 1) * 128], ident[:]),
                                  reads=r_LDA + [r_ident], writes=[rp])
                        cx.op('act', lambda: nc.scalar.copy(out=HA[:, g * 4:(g + 1) * 4, sub * 128:(sub + 1) * 128],
                                                            in_=pt[:].rearrange("p (a b) -> p a b", a=4)), reads=[rp], writes=[r_HA])
                pf.release(4)
                cx.barrier()
            with ExitStack() as sbk:
                sbb = lambda name, shape, dt=F32: sbk.enter_context(nc.sbuf_tensor(f"cb{tt}_" + name, shape, dt))
                gt2 = sbb("gt2", [128, D]); r_gt2 = cx.res()
                rowload(gt2[:], [r_gt2], 5)
                gfrow = sbb("gfrow", [128, D]); r_gfrow = cx.res()
                cx.dma('sp', gfrow[:], T['gf'].partition_broadcast(128), writes=[r_gfrow])
                gT = sbb("gT", [128, KC, 512], BF16); r_gT = cx.res()
                hx = Pool(cx, [sbb(f"hx{i}", [128, D]) for i in range(2)])
                junk2 = sbb("junk", [128, D], BF16); r_junk2 = cx.res()
                Pa = BPool([0, 1]); Pb = BPool([2, 3])
                po = [banks[i] for i in (4, 5, 6, 7)]
                for pi, (fb0, nfb) in enumerate(fparts):
                    for fb in range(nfb):
                        f0 = (fb0 + fb) * 512
                        wa, rwa = pf.take(('fi', f0))
                        wb_, rwb = pf.take(('fi', D_FF + f0))
                        for j in range(4):
                            pa, rpa = Pa.next(); pb, rpb = Pb.next()
                            for k in range(KC):
                                cx.op('pe', lambda: nc.tensor.matmul(pa[:], wa[:, k, j * 128:(j + 1) * 128], HA[:, k, :], start=(k == 0), stop=(k == KC - 1)),
                                      reads=[rwa, r_HA], writes=[rpa])
                            for k in range(KC):
                                cx.op('pe', lambda: nc.tensor.matmul(pb[:], wb_[:, k, j * 128:(j + 1) * 128], HA[:, k, :], start=(k == 0), stop=(k == KC - 1)),
                                      reads=[rwb, r_HA], writes=[rpb])
                            t1, rt1 = tmp512.next()
                            cx.op('act', lambda: nc.scalar.activation(out=t1[:], in_=pa[:], func=AF.Silu), reads=[rpa], writes=[rt1])
                            cx.op('dve', lambda: nc.vector.tensor_tensor(out=gT[:, fb * 4 + j, :], in0=pb[:], in1=t1[:], op=ALU.mult),
                                  reads=[rpb, rt1], writes=[r_gT])
                        pf.release(2)
                    nfc = nfb * 4
                    last = (pi == len(fparts) - 1)

                    def ffn_out_evac(sub, nb, bank):
                        t1, rt1 = tmp512.next()
                        cx.op('dve', lambda: nc.vector.tensor_tensor(out=t1[:], in0=bank[0][:], in1=gt2[:, nb * 512:(nb + 1) * 512], op=ALU.mult),
                              reads=[bank[1], r_gt2], writes=[rt1])
                        cx.op('pool', lambda: nc.gpsimd.tensor_tensor(out=x1[:, sub, nb * 512:(nb + 1) * 512], in0=x1[:, sub, nb * 512:(nb + 1) * 512],
                                                                      in1=t1[:], op=ALU.add), reads=[rt1, r_x1[sub]], writes=[r_x1[sub]])
                    if not last:
                        for nb in range(4):
                            wo, rwo = pf.take(('fo', fb0, nb))
                            for k in range(nfc):
                                for sub in range(4):
                                    cx.op('pe', lambda: nc.tensor.matmul(po[sub][0][:], gT[:, k, sub * 128:(sub + 1) * 128], wo[:, k, :],
                                                                         start=(k == 0), stop=(k == nfc - 1)),
                                          reads=[rwo, r_gT], writes=[po[sub][1]])
                            pf.release(1)
                            for sub in range(4):
                                ffn_out_evac(sub, nb, po[sub])
                    else:
                        wos = [pf.take(('fo', fb0, nb)) for nb in range(4)]
                        for sub in range(4):
                            for nb in range(4):
                                wo, rwo = wos[nb]
                                for k in range(nfc):
                                    cx.op('pe', lambda: nc.tensor.matmul(po[nb][0][:], gT[:, k, sub * 128:(sub + 1) * 128], wo[:, k, :],
                                                                         start=(k == 0), stop=(k == nfc - 1)),
                                          reads=[rwo, r_gT], writes=[po[nb][1]])
                            for nb in range(4):
                                ffn_out_evac(sub, nb, po[nb])
                            sx, rs = rstd_of(x1[:, sub, :], [r_x1[sub]], junk2[:], [r_junk2])
                            hh, rh = hx.next()
                            cx.op('dve', lambda: nc.vector.scalar_tensor_tensor(out=hh[:], in0=x1[:, sub, :], scalar=sx[:, 3:4], in1=gfrow[:],
                                                                                op0=ALU.mult, op1=ALU.mult), reads=[r_x1[sub], rs, r_gfrow], writes=[rh])
                            cx.dma('sp', T['y'][tok0 + sub * 128: tok0 + (sub + 1) * 128, :], hh[:], reads=[rh])
                        pf.release(4)
                cx.barrier()
        cx.barrier()


SCRATCH = lambda NT: [
    ("mod_s", [6, D], F32), ("qT_s", [8, 128, NT], BF16), ("kT_s", [8, 128, NT], BF16), ("ktok_s", [NT, 1024], BF16),
    ("vtok_s", [NT, 1024], BF16), ("otok_s", [NT, 1024], BF16), ("utok_s", [NT, 1024], F32), ("mgT_s", [4096, NT], BF16),
    ("gates_s", [32, NT], F32), ("hf_s", [NT, 1024], F32), ("hb_s", [NT, 1024], F32), ("ys_s", [NT, 1024], F32),
    ("UT_s", [128, 64, NT // 8], BF16), ("WW_s", [2, 2, 64, 64, NT // 8], BF16), ("GB_s", [2, 2, 64, 64, 128], BF16),
    ("MT_s", [128, 64, 128], BF16), ("wbf_s", [NWBLK, 128, KC * 512], BF16)]

INPUTS = lambda NT: [
    ("x", [NT, D]), ("cvec", [D]), ("keep", [128, 1]), ("C0", [2, 8, 128, 128]), ("n0", [2, 8, 128]), ("m0", [2, 8]),
    ("s0r", [2, 64, 64]), ("s0i", [2, 64, 64]), ("ident", [128, 128]), ("masks", [2, 64, 64]), ("expo", [2, 18]), ("bmask", [2, 128, 128]),
    ("w_mod", [D, 6 * D]), ("b_mod", [6 * D]), ("g1", [D]), ("g2", [D]), ("w_in", [D, N_IN]), ("b_gates", [32]), ("mh_g", [1024]),
    ("lam_re", [2, 64, 64]), ("lam_im", [2, 64, 64]), ("log_step", [2, 64]), ("B_re", [2, 64, 64, 16]), ("B_im", [2, 64, 64, 16]),
    ("C_re", [2, 64, 16, 64]), ("C_im", [2, 64, 16, 64]), ("s5_D", [1024]), ("w_up_a", [1024, D]), ("w_glu", [1024, 2 * D]),
    ("w_out", [D, D]), ("w_ffn_in", [D, 2 * D_FF]), ("w_ffn_out", [D_FF, D]), ("gf", [D])]


def OUTPUTS(NT):
    NS = NT // 256
    return [("y", [NT, D]), ("out_C", [NS, 2, 8, 128, 128]), ("out_n", [NS, 2, 8, 128]), ("out_m", [NS, 2, 8]),
            ("out_sr", [NS, 2, 64, 64]), ("out_si", [NS, 2, 64, 64])]


def build_full(NT, stages=(0, 1, 2, 3, 4), dbg_scratch=()):
    nc = bass.Bass("TRN2", target_bir_lowering=False)
    T = {}
    for (nm, shp) in INPUTS(NT):
        T[nm] = nc.dram_tensor(nm, shp, F32, kind="ExternalInput").ap()
    for (nm, shp) in OUTPUTS(NT):
        T[nm] = nc.dram_tensor(nm, shp, F32, kind="ExternalOutput").ap()
    for (nm, shp, dt) in SCRATCH(NT):
        T[nm] = nc.dram_tensor(nm, shp, dt, kind=("ExternalOutput" if nm in dbg_scratch else "Internal")).ap()
    with ExitStack() as gst:
        cx = Ctx(nc, gst)
        ident = gst.enter_context(nc.sbuf_tensor("ident_sb", [128, 128], F32)); r_ident = cx.res("ident")
        cx.dma('sp', ident[:], T['ident'][:, :], writes=[r_ident])
        if 0 in stages:
            stage0_mod(nc, cx, T)
        wc = WCache(nc, cx, T)
        if 1 in stages:
            pre = []
            if 4 in stages and NT >= 1024:
                seen = set()
                for (key, view, kc_n) in stage3_reqs(T, 1):
                    if key not in seen:
                        seen.add(key)
                        pre.append((key, view, kc_n))
            stage1_inproj(nc, cx, T, NT, ident, r_ident, wc, pre)
        if 2 in stages:
            stage_mlstm(nc, cx, T, NT, ident, r_ident)
        def precast_stage3():
            seen = set()
            for (key, view, kc_n) in stage3_reqs(T, 1):
                if key not in seen:
                    seen.add(key)
                    wc.precast(key, view, kc_n)
        if 3 in stages:
            stage_s5(nc, cx, T, NT, ident, r_ident, hook=None)
        if 4 in stages:
            stage3_rest(nc, cx, T, NT, ident, r_ident, wc)
        cx.barrier()
    return nc


def host_consts():
    s_ = np.arange(64)
    masks = np.stack([(s_[:, None] <= s_[None, :]), (s_[:, None] >= s_[None, :])]).astype(np.float32)
    expo, bmask = s5_consts()
    return {"ident": np.eye(128, dtype=np.float32), "masks": masks, "expo": expo, "bmask": bmask}


def weight_map(inp):
    f = lambda a: np.ascontiguousarray(np.asarray(a, dtype=np.float32))
    return {"w_mod": f(inp["w_mod"][0]), "b_mod": f(inp["b_mod"][0]), "g1": f(inp["norm1_g"][0]), "g2": f(inp["norm2_g"][0]),
            "w_in": f(inp["w_in"][0]), "b_gates": f(inp["b_gates"][0]), "mh_g": f(inp["mh_norm_g"][0]),
            "lam_re": f(inp["s5_lam_re"][0]), "lam_im": f(inp["s5_lam_im"][0]), "log_step": f(inp["s5_log_step"][0]),
            "B_re": f(inp["s5_B_re"][0]), "B_im": f(inp["s5_B_im"][0]), "C_re": f(inp["s5_C_re"][0]), "C_im": f(inp["s5_C_im"][0]),
            "s5_D": f(inp["s5_D"][0]).reshape(1024), "w_up_a": f(inp["w_up_a"][0]), "w_glu": f(inp["w_glu"][0]), "w_out": f(inp["w_out"][0]),
            "w_ffn_in": f(inp["w_ffn_in"][0]), "w_ffn_out": f(inp["w_ffn_out"][0]), "gf": f(inp["norm_f_g"])}


NT_CORE = 4096


def kernel(**inputs):
    inp = {k: np.asarray(v) for k, v in inputs.items()}
    NT = NT_CORE
    W = weight_map(inp)
    C = host_consts()
    f = lambda a: np.ascontiguousarray(np.asarray(a, dtype=np.float32))
    z = lambda *shape: np.zeros(shape, np.float32)
    maps = []
    for b in range(4):
        maps.append(dict(W, **C, x=f(inp["x_sample"][b]), cvec=f(inp["c"][b]), keep=np.ones((128, 1), np.float32),
                         C0=f(inp["state_mlstm_C"][b, 0]), n0=f(inp["state_mlstm_n"][b, 0]), m0=f(inp["state_mlstm_m"][b, 0]),
                         s0r=f(inp["state_s5_re"][b, 0]), s0i=f(inp["state_s5_im"][b, 0])))
    xp = f(inp["x_prompt"]).reshape(2, NT, D)
    pm = []
    for i in range(2):
        pm.append(dict(W, **C, x=xp[i], cvec=f(inp["c_ctx"]), keep=z(128, 1), C0=z(2, 8, 128, 128), n0=z(2, 8, 128), m0=z(2, 8),
                       s0r=z(2, 64, 64), s0i=z(2, 64, 64)))
    maps += pm + pm
    nc = build_full(NT)
    res = run_bass_kernel_spmd(nc, maps, core_ids=list(range(8)))
    R_ = res.results
    y_sample = np.stack([R_[b]["y"] for b in range(4)]).astype(np.float32)
    y_prompt = np.concatenate([R_[4]["y"], R_[5]["y"]], 0).reshape(32, 256, D).astype(np.float32)
    cat = lambda k: np.concatenate([R_[4][k], R_[5][k]], 0)[:, None].astype(np.float32)
    return (y_prompt, y_sample, cat("out_C"), cat("out_n"), cat("out_m"), cat("out_sr"), cat("out_si"))
```

```python
import numpy as np
from contextlib import ExitStack
import concourse.bass as bass
import concourse.mybir as mybir
from concourse.bass_utils import run_bass_kernel_spmd

F32 = mybir.dt.float32
BF16 = mybir.dt.bfloat16
AF = mybir.ActivationFunctionType
ALU = mybir.AluOpType
AX = mybir.AxisListType

D = 2048
KC = D // 128
D_A = 1024
H_A = 8
N_IN = 9248
D_FF = 5632
EPS = 1e-6


class Sem:
    def __init__(self, h, dma):
        self.h = h
        self.count = 0
        self.dma = dma


class Res:
    def __init__(self, name):
        self.name = name
        self.w = None
        self.r = []
        self.dsem = None


class Ctx:
    def __init__(self, nc, stack):
        self.nc = nc
        self.stack = stack
        self.eng = {'pe': nc.tensor, 'act': nc.scalar, 'dve': nc.vector, 'pool': nc.gpsimd, 'sp': nc.sync}
        self.esem = {}
        for e in ['pe', 'act', 'dve', 'pool']:
            self.esem[e] = Sem(stack.enter_context(nc.semaphore('s_' + e)), False)
        self.waited = {e: {} for e in self.eng}
        self.nres = 0
        self.dsems_all = []
        self.dsems_free = []
        self.live = []

    def res(self, name=None):
        self.nres += 1
        r = Res(name or f"r{self.nres}")
        self.live.append(r)
        return r

    def dsem(self, r):
        if r.dsem is None:
            if self.dsems_free:
                r.dsem = self.dsems_free.pop()
            else:
                s = Sem(self.stack.enter_context(self.nc.semaphore(f'd{len(self.dsems_all)}')), True)
                self.dsems_all.append(s)
                r.dsem = s
        return r.dsem

    def _wait(self, e, ev):
        if ev is None:
            return
        s, v = ev
        if s.dma:
            v = s.count
        w = self.waited[e]
        if w.get(id(s), -1) >= v:
            return
        w[id(s)] = v
        self.eng[e].wait_ge(s.h, v)

    def _skip(self, e, ev):
        return e == 'pe' and ev[0] is self.esem['pe']

    def _deps(self, e, reads, writes):
        for r in reads:
            if r.w is not None and not self._skip(e, r.w):
                self._wait(e, r.w)
        for r in writes:
            if r.w is not None and not self._skip(e, r.w):
                self._wait(e, r.w)
            for ev in r.r:
                if not self._skip(e, ev):
                    self._wait(e, ev)

    def op(self, e, fn, reads=(), writes=()):
        self._deps(e, reads, writes)
        ins = fn()
        s = self.esem[e]
        s.count += 1
        ins.then_inc(s.h, 1)
        ev = (s, s.count)
        for r in reads:
            r.r.append(ev)
        for r in writes:
            r.w = ev
            r.r = []
        return ins

    def dma(self, e, out, in_, reads=(), writes=(), sres=None, **kw):
        self._deps(e, reads, writes)
        if sres is None:
            sres = (list(writes) + list(reads))[0]
        s = self.dsem(sres)
        ins = self.eng[e].dma_start(out=out, in_=in_, **kw)
        s.count += 16
        ins.then_inc(s.h, 16)
        ev = (s, s.count)
        for r in reads:
            r.r.append(ev)
        for r in writes:
            r.w = ev
            r.r = []
        return ins

    def barrier(self, engines=('pe', 'act', 'dve', 'pool', 'sp')):
        for e in engines:
            for s in list(self.esem.values()) + self.dsems_all:
                if s.count > 0:
                    self._wait(e, (s, s.count))
        for r in self.live:
            if r.dsem is not None:
                self.dsems_free.append(r.dsem)
                r.dsem = None
            r.w = None
            r.r = []


class Pool:
    def __init__(self, cx, tiles):
        self.cx = cx
        self.tiles = tiles
        self.res = [cx.res() for _ in tiles]
        self.i = 0

    def next(self):
        t, r = self.tiles[self.i], self.res[self.i]
        self.i = (self.i + 1) % len(self.tiles)
        return t, r


import math

def stage_mlstm(nc, cx, T, NT, ident, r_ident):
    NCH = NT // 64
    NST = NT // 512
    NS = NT // 256
    with ExitStack() as st:
        sb = lambda name, shape, dt=F32: st.enter_context(nc.sbuf_tensor("m_" + name, shape, dt))
        psum = lambda name, shape: st.enter_context(nc.psum_tensor("mp_" + name, shape, F32))

        keep = sb("keep", [128, 1]); r_keep = cx.res()
        cx.dma('sp', keep[:], T['keep'][:, :], writes=[r_keep])
        ones8 = sb("ones8", [8, 128]); r_ones8 = cx.res()
        cx.op('dve', lambda: nc.vector.memset(ones8[:], 1.0), writes=[r_ones8])
        masks = []
        for d in range(2):
            mk = sb(f"mask{d}", [64, 64]); r_mk = cx.res()
            cx.dma('sp', mk[:], T['masks'][d, :, :], writes=[r_mk])
            masks.append((mk, r_mk))

        WK, CL, DECB, MM = [], [], [], []
        for d in range(2):
            WK.append((sb(f"WK{d}", [64, NCH, 8]), cx.res()))
            CL.append((sb(f"CL{d}", [64, NCH, 8]), cx.res()))
            DECB.append((sb(f"DECB{d}", [128, NCH, 8]), cx.res()))
            MM.append((sb(f"MM{d}", [8, NCH]), cx.res()))

        with ExitStack() as st2:
            sb2 = lambda name, shape, dt=F32: st2.enter_context(nc.sbuf_tensor("m2_" + name, shape, dt))
            PX = st2.enter_context(nc.psum_tensor("mp_PX", [128, 512], F32)); r_PX = cx.res()
            def prep(d):
                order = list(range(NCH)) if d == 0 else list(range(NCH - 1, -1, -1))
                Ig = sb2(f"Ig{d}", [8, NCH, 64]); r_I = cx.res()
                Fa = sb2(f"Fa{d}", [8, NCH, 64]); r_Fa = cx.res()
                Fb = sb2(f"Fb{d}", [8, NCH, 64]); r_Fb = cx.res()
                cx.dma('sp', Ig[:], T['gates_s'][8 * d:8 * d + 8, :].rearrange("p (c s) -> p c s", s=64), writes=[r_I])
                cx.dma('sp', Fa[:], T['gates_s'][16 + 8 * d:16 + 8 * d + 8, :].rearrange("p (c s) -> p c s", s=64), writes=[r_Fa])
                cx.op('act', lambda: nc.scalar.activation(out=Fa[:], in_=Fa[:], func=AF.Exp, scale=-1.0), reads=[r_Fa], writes=[r_Fa])
                cx.op('act', lambda: nc.scalar.activation(out=Fa[:], in_=Fa[:], func=AF.Ln, bias=1.0), reads=[r_Fa], writes=[r_Fa])
                A, rA, B, rB = Fa, r_Fa, Fb, r_Fb
                for sh in (1, 2, 4, 8, 16, 32):
                    if d == 0:
                        cx.op('act', lambda: nc.scalar.copy(out=B[:, :, :sh], in_=A[:, :, :sh]), reads=[rA], writes=[rB])
                    else:
                        cx.op('act', lambda: nc.scalar.copy(out=B[:, :, 64 - sh:], in_=A[:, :, 64 - sh:]), reads=[rA], writes=[rB])
                    if d == 0:
                        cx.op('dve', lambda: nc.vector.tensor_tensor(out=B[:, :, sh:], in0=A[:, :, sh:], in1=A[:, :, :64 - sh], op=ALU.add),
                              reads=[rA], writes=[rB])
                    else:
                        cx.op('dve', lambda: nc.vector.tensor_tensor(out=B[:, :, :64 - sh], in0=A[:, :, :64 - sh], in1=A[:, :, sh:], op=ALU.add),
                              reads=[rA], writes=[rB])
                    A, rA, B, rB = B, rB, A, rA
                    yield
                NB, r_NB = A, rA
                R, r_R = B, rB
                cx.op('dve', lambda: nc.vector.tensor_tensor(out=R[:], in0=Ig[:], in1=NB[:], op=ALU.add), reads=[r_I, r_NB], writes=[r_R])
                rmax = sb2(f"rmax{d}", [8, NCH]); r_rmax = cx.res()
                cx.op('dve', lambda: nc.vector.tensor_reduce(out=rmax[:], in_=R[:], axis=AX.X, op=ALU.max), reads=[r_R], writes=[r_rmax])
                M63 = sb2(f"M63{d}", [8, NCH]); r_M63 = cx.res()
                mpe = sb2(f"mpe{d}", [8, NCH]); r_mpe = cx.res()
                mm, r_mm = MM[d]
                lastcol = 63 if d == 0 else 0
                c0 = order[0]
                cx.dma('sp', mpe[:, c0:c0 + 1], T['m0'][d, :].rearrange("(h o) -> h o", o=1), writes=[r_mpe])
                for k in range(NCH):
                    c = order[k]
                    cx.op('dve', lambda: nc.vector.tensor_tensor(out=M63[:, c:c + 1], in0=mpe[:, c:c + 1], in1=rmax[:, c:c + 1], op=ALU.max),
                          reads=[r_mpe, r_rmax], writes=[r_M63])
                    cx.op('dve', lambda: nc.vector.tensor_tensor(out=mm[:, c:c + 1], in0=M63[:, c:c + 1], in1=NB[:, c, lastcol:lastcol + 1],
                                                                 op=ALU.subtract), reads=[r_M63, r_NB], writes=[r_mm])
                    if k + 1 < NCH:
                        cn = order[k + 1]
                        if (k + 1) % 4 == 0:
                            cx.op('dve', lambda: nc.vector.tensor_scalar(out=mpe[:, cn:cn + 1], in0=mm[:, c:c + 1], scalar1=keep[0:8, 0:1],
                                                                         scalar2=None, op0=ALU.mult), reads=[r_mm, r_keep], writes=[r_mpe])
                        else:
                            cx.op('dve', lambda: nc.vector.tensor_copy(out=mpe[:, cn:cn + 1], in_=mm[:, c:c + 1]), reads=[r_mm], writes=[r_mpe])
                    yield
                M63b = M63[:].unsqueeze(2).to_broadcast([8, NCH, 64])
                cx.op('dve', lambda: nc.vector.tensor_tensor(out=R[:], in0=R[:], in1=M63b, op=ALU.subtract), reads=[r_R, r_M63], writes=[r_R])
                cx.op('act', lambda: nc.scalar.activation(out=R[:], in_=R[:], func=AF.Exp), reads=[r_R], writes=[r_R])
                cx.op('dve', lambda: nc.vector.tensor_tensor(out=NB[:], in0=NB[:], in1=M63b, op=ALU.subtract), reads=[r_NB, r_M63], writes=[r_NB])
                cx.op('act', lambda: nc.scalar.activation(out=NB[:], in_=NB[:], func=AF.Exp), reads=[r_NB], writes=[r_NB])
                DEC = sb2(f"DEC{d}", [8, NCH]); r_DEC = cx.res()
                cx.op('dve', lambda: nc.vector.tensor_tensor(out=DEC[:], in0=mpe[:], in1=M63[:], op=ALU.subtract), reads=[r_mpe, r_M63], writes=[r_DEC])
                cx.op('act', lambda: nc.scalar.activation(out=DEC[:], in_=DEC[:], func=AF.Exp), reads=[r_DEC], writes=[r_DEC])
                if NCH > 4:
                    bsl = slice(4, NCH, 4) if d == 0 else slice(3, NCH - 4, 4)
                    cx.op('dve', lambda: nc.vector.tensor_scalar(out=DEC[:, bsl], in0=DEC[:, bsl], scalar1=keep[0:8, 0:1], scalar2=None, op0=ALU.mult),
                          reads=[r_DEC, r_keep], writes=[r_DEC])
                for (src, rsrc, (dst, rdst)) in ((R, r_R, WK[d]), (NB, r_NB, CL[d])):
                    for c in range(NCH):
                        cx.op('pe', lambda: nc.tensor.transpose(PX[0:64, c * 8:(c + 1) * 8], src[:, c, :], ident[0:8, 0:8]),
                              reads=[rsrc, r_ident], writes=[r_PX])
                    cx.op('dve', lambda: nc.vector.tensor_copy(out=dst[:].rearrange("p c h -> p (c h)"), in_=PX[0:64, 0:NCH * 8]),
                          reads=[r_PX], writes=[rdst])
                DECX = sb2(f"DECX{d}", [8, NCH, 8]); r_DECX = cx.res()
                cx.op('dve', lambda: nc.vector.tensor_tensor(out=DECX[:], in0=DEC[:].unsqueeze(2).to_broadcast([8, NCH, 8]),
                                                             in1=ident[0:8, 0:8].unsqueeze(1).to_broadcast([8, NCH, 8]), op=ALU.mult),
                      reads=[r_DEC, r_ident], writes=[r_DECX])
                cx.op('pe', lambda: nc.tensor.matmul(PX[:, 0:NCH * 8], ones8[:], DECX[:].rearrange("p c h -> p (c h)"), start=True, stop=True),
                      reads=[r_ones8, r_DECX], writes=[r_PX])
                db, r_db = DECB[d]
                cx.op('dve', lambda: nc.vector.tensor_copy(out=db[:].rearrange("p c h -> p (c h)"), in_=PX[:, 0:NCH * 8]), reads=[r_PX], writes=[r_db])
                msl = slice(3, NCH, 4) if d == 0 else slice(0, NCH, 4)
                cx.dma('sp', T['out_m'][:, d, :].rearrange("s h -> h s"), mm[:, msl], reads=[r_mm], allow_slow_non_contiguous=True)
            gens_p = [prep(0), prep(1)]
            alive_p = [True, True]
            while any(alive_p):
                for d_ in range(2):
                    if alive_p[d_]:
                        try:
                            next(gens_p[d_])
                        except StopIteration:
                            alive_p[d_] = False
            cx.barrier()

        ST_TOK = 256
        CPS = ST_TOK // 64
        NSUP = NT // ST_TOK
        ones64 = sb("ones64", [64, 1], BF16); r_ones64 = cx.res()
        cx.op('pool', lambda: nc.gpsimd.memset(ones64[:], 1.0), writes=[r_ones64])
        hout = [T['hf_s'], T['hb_s']]

        def run_dir(d):
            PSTd = (psum(f"PSTd{d}", [128, 512]), cx.res())
            PNUMd = (psum(f"PNUMd{d}", [128, 512]), cx.res())
            PUPDd = (psum(f"PUPDd{d}", [128, 512]), cx.res())
            PDUd = (psum(f"PDUd{d}", [128, 512]), cx.res())
            r_pun = cx.res()
            qT = Pool(cx, [sb(f"qT{d}{i}", [128, 8, ST_TOK], BF16) for i in range(2)])
            kT = Pool(cx, [sb(f"kT{d}{i}", [128, 8, ST_TOK], BF16) for i in range(2)])
            kt = Pool(cx, [sb(f"kt{d}{i}", [64, CPS, 8, 128], BF16) for i in range(2)])
            v1 = Pool(cx, [sb(f"v1{d}{i}", [64, CPS, 8, 128], BF16) for i in range(2)])
            Cst = [(sb(f"C{d}{i}", [128, 8, 129]), cx.res()) for i in range(2)]
            Cb = sb(f"Cb{d}", [128, 8, 129], BF16); r_Cb = cx.res()
            Sm1 = sb(f"Sm1{d}", [64, 8, 64]); r_Sm1 = cx.res()
            Sm = sb(f"Sm{d}", [64, 8, 64], BF16); r_Sm = cx.res()
            Kt = sb(f"Kt{d}", [64, 8, 128], BF16); r_Kt = cx.res()
            dn = sb(f"dn{d}", [64, 8]); r_dn = cx.res()
            dn2 = sb(f"dn2{d}", [64, 8]); r_dn2 = cx.res()
            rc = sb(f"rc{d}", [64, 8]); r_rc = cx.res()
            hst = Pool(cx, [sb(f"hst{d}{i}", [64, 4, 128]) for i in range(3)])
            cx.dma('sp', Cst[0][0][:, :, 0:128], T['C0'][d].rearrange("h d e -> d h e"), writes=[Cst[0][1]])
            cx.dma('sp', Cst[0][0][:, :, 128], T['n0'][d].rearrange("h d -> d h"), writes=[Cst[0][1]], allow_slow_non_contiguous=True)
            wk, r_wk = WK[d]; cl, r_cl = CL[d]; db, r_db = DECB[d]
            mk, r_mk = masks[d]

            def load_super(stile):
                q, rq = qT.next(); kk, rk = kT.next(); ktk, rkt = kt.next(); vv, rv = v1.next()
                ts = slice(stile * ST_TOK, (stile + 1) * ST_TOK)
                cx.dma('sp', q[:], T['qT_s'][:, :, ts].rearrange("h d t -> d h t"), writes=[rq])
                cx.dma('sp', kk[:], T['kT_s'][:, :, ts].rearrange("h d t -> d h t"), writes=[rk])
                cx.dma('sp', ktk[:], T['ktok_s'][ts, :].rearrange("(c s) (h e) -> s c h e", s=64, e=128), writes=[rkt])
                cx.dma('sp', vv[:], T['vtok_s'][ts, :].rearrange("(c s) (h e) -> s c h e", s=64, e=128), writes=[rv])
                return (q, rq, kk, rk, ktk, rkt, vv, rv)

            order = list(range(NCH)) if d == 0 else list(range(NCH - 1, -1, -1))
            sup_order = []
            for c in order:
                if not sup_order or sup_order[-1] != c // CPS:
                    sup_order.append(c // CPS)
            loaded = {sup_order[0]: load_super(sup_order[0])}
            yield
            for k in range(NCH):
                c = order[k]
                stile, cs = c // CPS, c % CPS
                if k % CPS == 0:
                    si_ = sup_order.index(stile)
                    if si_ + 1 < len(sup_order):
                        loaded[sup_order[si_ + 1]] = load_super(sup_order[si_ + 1])
                q, rq, kk, rk, ktk, rkt, vv, rv = loaded[stile]
                cur, nxt = k % 2, (k + 1) % 2
                Cc, rCc = Cst[cur]; Cn, rCn = Cst[nxt]
                tsl = slice(cs * 64, (cs + 1) * 64)
                pst, r_pst = PSTd; pnum, r_pnum = PNUMd; pupd, r_pupd = PUPDd; pdu, r_pdu = PDUd
                cx.op('dve', lambda: nc.vector.tensor_tensor(out=Cn[:], in0=Cc[:], in1=db[:, c, :].unsqueeze(2).to_broadcast([128, 8, 129]), op=ALU.mult),
                      reads=[rCc, r_db], writes=[rCn])
                cx.op('act', lambda: nc.scalar.copy(out=Cb[:], in_=Cn[:]), reads=[rCn], writes=[r_Cb])
                for h in range(8):
                    cx.op('pe', lambda: nc.tensor.matmul(pst[0:64, h * 64:(h + 1) * 64], kk[:, h, tsl], q[:, h, tsl], start=True, stop=True),
                          reads=[rk, rq], writes=[r_pst])
                yield
                cx.op('dve', lambda: nc.vector.tensor_tensor(out=Sm1[:], in0=pst[0:64, :].rearrange("p (h j) -> p h j", h=8),
                                                             in1=wk[:, c, :].unsqueeze(2).to_broadcast([64, 8, 64]), op=ALU.mult),
                      reads=[r_pst, r_wk], writes=[r_Sm1])
                cx.op('pool', lambda: nc.gpsimd.tensor_tensor(out=Sm[:], in0=Sm1[:], in1=mk[:].unsqueeze(1).to_broadcast([64, 8, 64]), op=ALU.mult),
                      reads=[r_Sm1, r_mk], writes=[r_Sm])
                cx.op('pool', lambda: nc.gpsimd.tensor_tensor(out=Kt[:], in0=ktk[:, cs, :, :], in1=wk[:, c, :].unsqueeze(2).to_broadcast([64, 8, 128]), op=ALU.mult),
                      reads=[rkt, r_wk], writes=[r_Kt])
                yield
                for h in range(8):
                    cx.op('pe', lambda: nc.tensor.matmul(pdu[0:64, h:h + 1], Sm[:, h, :], ones64[:, 0:1], start=True, stop=False),
                          reads=[r_Sm, r_ones64], writes=[r_pdu])
                    cx.op('pe', lambda: nc.tensor.matmul(pdu[0:64, h:h + 1], q[:, h, tsl], Cb[:, h, 128:129], start=False, stop=True),
                          reads=[rq, r_Cb], writes=[r_pdu])
                for half in range(2):
                    for hh in range(4):
                        h = half * 4 + hh
                        cx.op('pe', lambda: nc.tensor.matmul(pnum[0:64, hh * 128:(hh + 1) * 128], Sm[:, h, :], vv[:, cs, h, :], start=True, stop=False),
                              reads=[r_Sm, rv], writes=[r_pnum])
                        cx.op('pe', lambda: nc.tensor.matmul(pnum[0:64, hh * 128:(hh + 1) * 128], q[:, h, tsl], Cb[:, h, 0:128], start=False, stop=True),
                              reads=[rq, r_Cb], writes=[r_pnum])
                    if half == 0:
                        cx.op('dve', lambda: nc.vector.tensor_tensor(out=dn[:], in0=pdu[0:64, 0:8], in1=cl[:, c, :], op=ALU.max),
                              reads=[r_pdu, r_cl], writes=[r_dn])
                        cx.op('dve', lambda: nc.vector.tensor_scalar(out=dn2[:], in0=pdu[0:64, 0:8], scalar1=-1.0, scalar2=None, op0=ALU.mult),
                              reads=[r_pdu], writes=[r_dn2])
                        cx.op('dve', lambda: nc.vector.tensor_tensor(out=dn[:], in0=dn[:], in1=dn2[:], op=ALU.max), reads=[r_dn, r_dn2], writes=[r_dn])
                        cx.op('dve', lambda: nc.vector.reciprocal(out=rc[:], in_=dn[:]), reads=[r_dn], writes=[r_rc])
                    for hh in range(4):
                        h = half * 4 + hh
                        cx.op('pe', lambda: nc.tensor.matmul(pupd[:, hh * 128:(hh + 1) * 128], Kt[:, h, :], vv[:, cs, h, :], start=True, stop=True),
                              reads=[r_Kt, rv], writes=[r_pupd])
                    if half == 0:
                        for h in range(8):
                            cx.op('pe', lambda: nc.tensor.matmul(pdu[:, 8 + h:9 + h], Kt[:, h, :], ones64[:, 0:1], start=True, stop=True),
                                  reads=[r_Kt, r_ones64], writes=[r_pun])
                    yield
                    hs, rhs = hst.next()
                    cx.op('dve', lambda: nc.vector.tensor_tensor(out=hs[:], in0=pnum[0:64, :].rearrange("p (h e) -> p h e", h=4),
                                                                 in1=rc[:, half * 4:(half + 1) * 4].unsqueeze(2).to_broadcast([64, 4, 128]), op=ALU.mult),
                          reads=[r_pnum, r_rc], writes=[rhs])
                    cx.dma('sp', hout[d][c * 64:(c + 1) * 64, half * 512:(half + 1) * 512], hs[:].rearrange("p h e -> p (h e)"), reads=[rhs])
                    cx.op('dve', lambda: nc.vector.tensor_tensor(out=Cn[:, half * 4:(half + 1) * 4, 0:128], in0=Cn[:, half * 4:(half + 1) * 4, 0:128],
                                                                 in1=pupd[:, :].rearrange("p (h e) -> p h e", h=4), op=ALU.add),
                          reads=[rCn, r_pupd], writes=[rCn])
                    if half == 0:
                        cx.op('dve', lambda: nc.vector.tensor_tensor(out=Cn[:, :, 128], in0=Cn[:, :, 128], in1=pdu[:, 8:16], op=ALU.add),
                              reads=[rCn, r_pun], writes=[rCn])
                    yield
                if (k + 1) % 4 == 0:
                    slot = c // 4
                    cx.dma('sp', T['out_C'][slot, d].rearrange("h d e -> d h e"), Cn[:, :, 0:128], reads=[rCn])
                    cx.dma('sp', T['out_n'][slot, d].rearrange("h d -> d h"), Cn[:, :, 128], reads=[rCn], allow_slow_non_contiguous=True)

        gens = [run_dir(0), run_dir(1)]
        alive = [True, True]
        while any(alive):
            for d in range(2):
                if alive[d]:
                    try:
                        next(gens[d])
                    except StopIteration:
                        alive[d] = False
        cx.barrier()

MAGIC = 12582912.0
TWO_PI = 2.0 * math.pi


def stage_s5(nc, cx, T, NT, ident, r_ident, hook=None):
    NK = NT // 8
    KS = 64
    NSEG = NT // 512
    NS = NT // 256
    with ExitStack() as st:
        sb = lambda name, shape, dt=F32: st.enter_context(nc.sbuf_tensor("s_" + name, shape, dt))
        keep = sb("keep", [128, 1]); r_keep = cx.res()
        cx.dma('sp', keep[:], T['keep'][:, :], writes=[r_keep])
        identb = sb("identb", [128, 128], BF16); r_identb = cx.res()
        cx.op('dve', lambda: nc.vector.tensor_copy(out=identb[:], in_=ident[:]), reads=[r_ident], writes=[r_identb])
        with ExitStack() as st_rec:
            sbr = lambda name, shape, dt=F32: st_rec.enter_context(nc.sbuf_tensor("sr_" + name, shape, dt))
            HT = [(sbr(f"HT{d}", [128, 64, 128], BF16), cx.res()) for d in range(2)]
            COEF = [(sbr(f"COEF{d}", [64, 2, 2, 64]), cx.res()) for d in range(2)]
            W0 = [(sbr(f"W0{d}", [64, 2, 64]), cx.res()) for d in range(2)]
            with ExitStack() as st2:
                sb2 = lambda name, shape, dt=F32: st2.enter_context(nc.sbuf_tensor("s2_" + name, shape, dt))
                psum = lambda name, shape, dt=F32: st2.enter_context(nc.psum_tensor("s2p_" + name, shape, dt))
                PA = Pool(cx, [psum(f"PA{i}", [128, 512]) for i in range(2)])
                PB = Pool(cx, [psum(f"PB{i}", [128, 512]) for i in range(2)])
                MTf = sb2("MTf", [128, 64, 128]); r_MTf = cx.res()
                MT16 = sb2("MT16", [128, 64, 128], BF16); r_MT16 = cx.res()
                pmtmp = sb2("pmtmp", [128, 4, 128]); r_pmtmp = cx.res()
                bmask = []
                for d in range(2):
                    bm = sb2(f"bmask{d}", [128, 128]); r_bm = cx.res()
                    cx.dma('sp', bm[:], T['bmask'][d], writes=[r_bm])
                    bmask.append((bm, r_bm))
                _cache = {}

                def salloc(name, shape, dt=F32):
                    if name not in _cache:
                        _cache[name] = (st2.enter_context(nc.sbuf_tensor("s2c_" + name, shape, dt)), cx.res())
                    return _cache[name]
                for d in range(2):
                  if True:
                    E = 'dve'
                    sh3 = [64, 64, 18]
                    sh8 = [64, 64, 8]
                    PR, r_PR = salloc("PR", sh3)
                    PI, r_PI = salloc("PI", sh3)
                    QR, r_QR = salloc("QR", sh8)
                    QI, r_QI = salloc("QI", sh8)
                    Br, r_Br = salloc("Br", [64, 64, 16])
                    Bi, r_Bi = salloc("Bi", [64, 64, 16])
                    CrT, r_CrT = salloc("CrT", [64, 64, 16])
                    CiT, r_CiT = salloc("CiT", [64, 64, 16])
                    if True:
                        lr, r_lr = salloc("lr", [64, 64])
                        li, r_li = salloc("li", [64, 64])
                        dtb, r_dtb = salloc("dtb", [64, 64])
                        expo, r_expo = salloc("expo", [64, 18])
                        cx.dma('sp', lr[:], T['lam_re'][d].rearrange("g p -> p g"), writes=[r_lr], allow_slow_non_contiguous=True)
                        cx.dma('sp', li[:], T['lam_im'][d].rearrange("g p -> p g"), writes=[r_li], allow_slow_non_contiguous=True)
                        cx.dma('sp', dtb[:], T['log_step'][d].partition_broadcast(64), writes=[r_dtb])
                        cx.dma('sp', expo[:], T['expo'][d].partition_broadcast(64), writes=[r_expo])
                        cx.dma('sp', Br[:], T['B_re'][d].rearrange("g p c -> p g c"), writes=[r_Br])
                        cx.dma('sp', Bi[:], T['B_im'][d].rearrange("g p c -> p g c"), writes=[r_Bi])
                        cx.op('act', lambda: nc.scalar.activation(out=dtb[:], in_=dtb[:], func=AF.Exp), reads=[r_dtb], writes=[r_dtb])
                        LD, r_LD = salloc("LD", [64, 64])
                        TH, r_TH = salloc("TH", [64, 64])
                        cx.op(E, lambda: nc.vector.tensor_tensor(out=LD[:], in0=lr[:], in1=dtb[:], op=ALU.mult), reads=[r_lr, r_dtb], writes=[r_LD])
                        cx.op(E, lambda: nc.vector.tensor_tensor(out=TH[:], in0=li[:], in1=dtb[:], op=ALU.mult), reads=[r_li, r_dtb], writes=[r_TH])
                        MAG, r_MAG = salloc("MAG", sh3)
                        ANG, r_ANG = salloc("ANG", sh3)
                        SN, r_SN = salloc("SN", sh3)
                        CS, r_CS = salloc("CS", sh3)
                        eb = expo[:].unsqueeze(1).to_broadcast(sh3)
                        cx.op(E, lambda: nc.vector.tensor_tensor(out=MAG[:], in0=LD[:].unsqueeze(2).to_broadcast(sh3), in1=eb, op=ALU.mult),
                              reads=[r_LD, r_expo], writes=[r_MAG])
                        cx.op('act', lambda: nc.scalar.activation(out=MAG[:], in_=MAG[:], func=AF.Exp), reads=[r_MAG], writes=[r_MAG])
                        cx.op(E, lambda: nc.vector.tensor_tensor(out=ANG[:], in0=TH[:].unsqueeze(2).to_broadcast(sh3), in1=eb, op=ALU.mult),
                              reads=[r_TH, r_expo], writes=[r_ANG])
                        for (dst, rdst, ph) in ((SN, r_SN, 0.0), (CS, r_CS, math.pi / 2)):
                            cx.op(E, lambda: nc.vector.tensor_scalar(out=dst[:], in0=ANG[:], scalar1=ph, scalar2=1.0 / TWO_PI, op0=ALU.add, op1=ALU.mult),
                                  reads=[r_ANG], writes=[rdst])
                            cx.op(E, lambda: nc.vector.tensor_scalar(out=dst[:], in0=dst[:], scalar1=MAGIC, scalar2=None, op0=ALU.add), reads=[rdst], writes=[rdst])
                            cx.op(E, lambda: nc.vector.tensor_scalar(out=dst[:], in0=dst[:], scalar1=-MAGIC, scalar2=-TWO_PI, op0=ALU.add, op1=ALU.mult),
                                  reads=[rdst], writes=[rdst])
                            cx.op(E, lambda: nc.vector.scalar_tensor_tensor(out=dst[:], in0=ANG[:], scalar=ph, in1=dst[:], op0=ALU.add, op1=ALU.add),
                                  reads=[r_ANG, rdst], writes=[rdst])
                            cx.op(E, lambda: nc.vector.tensor_scalar(out=dst[:], in0=dst[:], scalar1=-math.pi, scalar2=math.pi, op0=ALU.max, op1=ALU.min),
                                  reads=[rdst], writes=[rdst])
                            cx.op('act', lambda: nc.scalar.activation(out=dst[:], in_=dst[:], func=AF.Sin), reads=[rdst], writes=[rdst])
                        cx.op(E, lambda: nc.vector.tensor_tensor(out=PR[:], in0=MAG[:], in1=CS[:], op=ALU.mult), reads=[r_MAG, r_CS], writes=[r_PR])
                        cx.op(E, lambda: nc.vector.tensor_tensor(out=PI[:], in0=MAG[:], in1=SN[:], op=ALU.mult), reads=[r_MAG, r_SN], writes=[r_PI])
                        tmpa, r_ta = salloc("tmpa", [64, 64])
                        tmpb, r_tb = salloc("tmpb", [64, 64])
                        den, r_den = salloc("den", [64, 64])
                        er, r_er = salloc("er", [64, 64])
                        kr, r_kr = salloc("kr", [64, 64])
                        ki, r_ki = salloc("ki", [64, 64])
                        ar, ai = PR[:, :, 17], PI[:, :, 17]
                        cx.op(E, lambda: nc.vector.tensor_tensor(out=den[:], in0=lr[:], in1=lr[:], op=ALU.mult), reads=[r_lr], writes=[r_den])
                        cx.op(E, lambda: nc.vector.tensor_tensor(out=tmpa[:], in0=li[:], in1=li[:], op=ALU.mult), reads=[r_li], writes=[r_ta])
                        cx.op(E, lambda: nc.vector.tensor_tensor(out=den[:], in0=den[:], in1=tmpa[:], op=ALU.add), reads=[r_den, r_ta], writes=[r_den])
                        cx.op(E, lambda: nc.vector.reciprocal(out=den[:], in_=den[:]), reads=[r_den], writes=[r_den])
                        cx.op(E, lambda: nc.vector.tensor_scalar(out=er[:], in0=ar, scalar1=-1.0, scalar2=None, op0=ALU.add), reads=[r_PR], writes=[r_er])
                        cx.op(E, lambda: nc.vector.tensor_tensor(out=tmpa[:], in0=er[:], in1=lr[:], op=ALU.mult), reads=[r_er, r_lr], writes=[r_ta])
                        cx.op(E, lambda: nc.vector.tensor_tensor(out=tmpb[:], in0=ai, in1=li[:], op=ALU.mult), reads=[r_PI, r_li], writes=[r_tb])
                        cx.op(E, lambda: nc.vector.tensor_tensor(out=tmpa[:], in0=tmpa[:], in1=tmpb[:], op=ALU.add), reads=[r_ta, r_tb], writes=[r_ta])
                        cx.op(E, lambda: nc.vector.tensor_tensor(out=kr[:], in0=tmpa[:], in1=den[:], op=ALU.mult), reads=[r_ta, r_den], writes=[r_kr])
                        cx.op(E, lambda: nc.vector.tensor_tensor(out=tmpa[:], in0=ai, in1=lr[:], op=ALU.mult), reads=[r_PI, r_lr], writes=[r_ta])
                        cx.op(E, lambda: nc.vector.tensor_tensor(out=tmpb[:], in0=er[:], in1=li[:], op=ALU.mult), reads=[r_er, r_li], writes=[r_tb])
                        cx.op(E, lambda: nc.vector.tensor_tensor(out=tmpa[:], in0=tmpa[:], in1=tmpb[:], op=ALU.subtract), reads=[r_ta, r_tb], writes=[r_ta])
                        cx.op(E, lambda: nc.vector.tensor_tensor(out=ki[:], in0=tmpa[:], in1=den[:], op=ALU.mult), reads=[r_ta, r_den], writes=[r_ki])
                        q1, r_q1 = salloc("q1", sh8)
                        krb = kr[:].unsqueeze(2).to_broadcast(sh8)
                        kib = ki[:].unsqueeze(2).to_broadcast(sh8)
                        cx.op(E, lambda: nc.vector.tensor_tensor(out=QR[:], in0=PR[:, :, 0:8], in1=krb, op=ALU.mult), reads=[r_PR, r_kr], writes=[r_QR])
                        cx.op(E, lambda: nc.vector.tensor_tensor(out=q1[:], in0=PI[:, :, 0:8], in1=kib, op=ALU.mult), reads=[r_PI, r_ki], writes=[r_q1])
                        cx.op(E, lambda: nc.vector.tensor_tensor(out=QR[:], in0=QR[:], in1=q1[:], op=ALU.subtract), reads=[r_QR, r_q1], writes=[r_QR])
                        cx.op(E, lambda: nc.vector.tensor_tensor(out=QI[:], in0=PR[:, :, 0:8], in1=kib, op=ALU.mult), reads=[r_PR, r_ki], writes=[r_QI])
                        cx.op(E, lambda: nc.vector.tensor_tensor(out=q1[:], in0=PI[:, :, 0:8], in1=krb, op=ALU.mult), reads=[r_PI, r_kr], writes=[r_q1])
                        cx.op(E, lambda: nc.vector.tensor_tensor(out=QI[:], in0=QI[:], in1=q1[:], op=ALU.add), reads=[r_QI, r_q1], writes=[r_QI])
                        cf, r_cf = COEF[d]
                        a8r, a8i = PR[:, :, 16], PI[:, :, 16]
                        cx.op(E, lambda: nc.vector.tensor_copy(out=cf[:, 0, 0, :], in_=a8r), reads=[r_PR], writes=[r_cf])
                        cx.op(E, lambda: nc.vector.tensor_scalar(out=cf[:, 0, 1, :], in0=a8i, scalar1=-1.0, scalar2=None, op0=ALU.mult), reads=[r_PI], writes=[r_cf])
                        cx.op(E, lambda: nc.vector.tensor_copy(out=cf[:, 1, 0, :], in_=a8i), reads=[r_PI], writes=[r_cf])
                        cx.op(E, lambda: nc.vector.tensor_copy(out=cf[:, 1, 1, :], in_=a8r), reads=[r_PR], writes=[r_cf])
                        s0, r_s0 = salloc("s0", [64, 2, 64])
                        cx.dma('sp', s0[:, 0, :], T['s0r'][d].rearrange("g p -> p g"), writes=[r_s0], allow_slow_non_contiguous=True)
                        cx.dma('sp', s0[:, 1, :], T['s0i'][d].rearrange("g p -> p g"), writes=[r_s0], allow_slow_non_contiguous=True)
                        p0, r_p0 = salloc("p0", [64, 2, 2, 64])
                        w0, r_w0 = W0[d]
                        cx.op(E, lambda: nc.vector.tensor_copy(out=w0[:], in_=s0[:]), reads=[r_s0], writes=[r_w0])
                        for ci_, (src, dst, rdst) in enumerate(((T['C_re'], CrT, r_CrT), (T['C_im'], CiT, r_CiT))):
                            cin, r_cin = salloc(f"cin{ci_}", [128, 8, 64])
                            cx.dma('sp', cin[:], src[d].rearrange("(gt gi) c p -> (gi c) gt p", gi=8), writes=[r_cin])
                            for half in range(2):
                                pt, rp = PA.next()
                                for q in range(4):
                                    gt = half * 4 + q
                                    cx.op('pe', lambda: nc.tensor.transpose(pt[0:64, q * 128:(q + 1) * 128], cin[:, gt, :], ident[:]),
                                          reads=[r_cin, r_ident], writes=[rp])
                                cx.op('act', lambda: nc.scalar.copy(out=dst[:, half * 32:(half + 1) * 32, :].rearrange("p g c -> p (g c)"), in_=pt[0:64, :]),
                                      reads=[rp], writes=[rdst])
                    ht, r_ht = HT[d]
                    bm, r_bm = bmask[d]
                    GQ = 8
                    for gq in range(64 // GQ):
                      if True:
                        gs = slice(gq * GQ, (gq + 1) * GQ)
                        sh4 = [64, GQ, 8, 16]
                        HR, r_HR = salloc("HR", sh4)
                        HI, r_HI = salloc("HI", sh4)
                        GR, r_GR = salloc("GR", sh4)
                        GN, r_GN = salloc("GN", sh4)
                        TM, r_TM = salloc("TM", sh4)
                        qrb = QR[:, gs, :].unsqueeze(3).to_broadcast(sh4)
                        qib = QI[:, gs, :].unsqueeze(3).to_broadcast(sh4)
                        brb = Br[:, gs, :].unsqueeze(2).to_broadcast(sh4)
                        bib = Bi[:, gs, :].unsqueeze(2).to_broadcast(sh4)
                        prb = PR[:, gs, 8:16].unsqueeze(3).to_broadcast(sh4)
                        pib = PI[:, gs, 8:16].unsqueeze(3).to_broadcast(sh4)
                        crb = CrT[:, gs, :].unsqueeze(2).to_broadcast(sh4)
                        cib = CiT[:, gs, :].unsqueeze(2).to_broadcast(sh4)
                        cx.op('dve', lambda: nc.vector.tensor_tensor(out=HR[:], in0=qrb, in1=brb, op=ALU.mult), reads=[r_QR, r_Br], writes=[r_HR])
                        cx.op('pool', lambda: nc.gpsimd.tensor_tensor(out=TM[:], in0=qib, in1=bib, op=ALU.mult), reads=[r_QI, r_Bi], writes=[r_TM])
                        cx.op('dve', lambda: nc.vector.tensor_tensor(out=HR[:], in0=HR[:], in1=TM[:], op=ALU.subtract), reads=[r_HR, r_TM], writes=[r_HR])
                        cx.op('dve', lambda: nc.vector.tensor_tensor(out=HI[:], in0=qrb, in1=bib, op=ALU.mult), reads=[r_QR, r_Bi], writes=[r_HI])
                        cx.op('pool', lambda: nc.gpsimd.tensor_tensor(out=TM[:], in0=qib, in1=brb, op=ALU.mult), reads=[r_QI, r_Br], writes=[r_TM])
                        cx.op('dve', lambda: nc.vector.tensor_tensor(out=HI[:], in0=HI[:], in1=TM[:], op=ALU.add), reads=[r_HI, r_TM], writes=[r_HI])
                        cx.op('dve', lambda: nc.vector.tensor_tensor(out=GR[:], in0=prb, in1=crb, op=ALU.mult), reads=[r_PR, r_CrT], writes=[r_GR])
                        cx.op('pool', lambda: nc.gpsimd.tensor_tensor(out=TM[:], in0=pib, in1=cib, op=ALU.mult), reads=[r_PI, r_CiT], writes=[r_TM])
                        cx.op('dve', lambda: nc.vector.tensor_tensor(out=GR[:], in0=GR[:], in1=TM[:], op=ALU.subtract), reads=[r_GR, r_TM], writes=[r_GR])
                        cx.op('dve', lambda: nc.vector.tensor_tensor(out=GN[:], in0=prb, in1=cib, op=ALU.mult), reads=[r_PR, r_CiT], writes=[r_GN])
                        cx.op('pool', lambda: nc.gpsimd.tensor_tensor(out=TM[:], in0=pib, in1=crb, op=ALU.mult), reads=[r_PI, r_CrT], writes=[r_TM])
                        cx.op('dve', lambda: nc.vector.tensor_tensor(out=GN[:], in0=GN[:], in1=TM[:], op=ALU.add), reads=[r_GN, r_TM], writes=[r_GN])
                        cx.op('dve', lambda: nc.vector.tensor_scalar(out=GN[:], in0=GN[:], scalar1=-1.0, scalar2=None, op0=ALU.mult), reads=[r_GN], writes=[r_GN])
                        HRf = HR[:].rearrange("p g j c -> p g (j c)")
                        HIf = HI[:].rearrange("p g j c -> p g (j c)")
                        GRf = GR[:].rearrange("p g j c -> p g (j c)")
                        GNf = GN[:].rearrange("p g j c -> p g (j c)")
                        for (src, rsrc, c0) in ((HRf, r_HR, 0), (HIf, r_HI, 64)):
                            for g8 in range(GQ // 8):
                                pt, rp = PA.next()
                                for gi in range(8):
                                    gl = g8 * 8 + gi
                                    cx.op('pe', lambda: nc.tensor.transpose(pt[:, gi * 64:(gi + 1) * 64], src[:, gl, :], ident[0:64, 0:64]),
                                          reads=[rsrc, r_ident], writes=[rp])
                                g0 = gq * GQ + g8 * 8
                                cx.op('act', lambda: nc.scalar.copy(out=ht[:, g0:g0 + 8, c0:c0 + 64], in_=pt[:, :].rearrange("p (g q) -> p g q", q=64)),
                                      reads=[rp], writes=[r_ht])
                        for g4 in range(GQ // 4):
                            pt, rp = PB.next()
                            for gi in range(4):
                                gl = g4 * 4 + gi
                                cx.op('pe', lambda: nc.tensor.matmul(pt[:, gi * 128:(gi + 1) * 128], HRf[:, gl, :], GRf[:, gl, :], start=True, stop=False),
                                      reads=[r_HR, r_GR], writes=[rp])
                                cx.op('pe', lambda: nc.tensor.matmul(pt[:, gi * 128:(gi + 1) * 128], HIf[:, gl, :], GNf[:, gl, :], start=False, stop=True),
                                      reads=[r_HI, r_GN], writes=[rp])
                            g0 = gq * GQ + g4 * 4
                            pv = pt[:, :].rearrange("p (g q) -> p g q", q=128)
                            bmb = bm[:].unsqueeze(1).to_broadcast([128, 4, 128])
                            if d == 0:
                                cx.op('dve', lambda: nc.vector.tensor_tensor(out=MTf[:, g0:g0 + 4, :], in0=pv, in1=bmb, op=ALU.mult),
                                      reads=[rp, r_bm], writes=[r_MTf])
                            else:
                                cx.op('dve', lambda: nc.vector.tensor_tensor(out=pmtmp[:], in0=pv, in1=bmb, op=ALU.mult), reads=[rp, r_bm], writes=[r_pmtmp])
                                cx.op('dve', lambda: nc.vector.tensor_tensor(out=MT16[:, g0:g0 + 4, :], in0=pmtmp[:], in1=MTf[:, g0:g0 + 4, :], op=ALU.add),
                                      reads=[r_pmtmp, r_MTf], writes=[r_MT16])
                        for (src, rsrc, ri) in ((GRf, r_GR, 0), (GNf, r_GN, 1)):
                            gb, r_gb = salloc(f"gb{ri}", [64, GQ, 128], BF16)
                            cx.op('act', lambda: nc.scalar.copy(out=gb[:], in_=src), reads=[rsrc], writes=[r_gb])
                            cx.dma('sp', T['GB_s'][d, ri, :, gs, :], gb[:], reads=[r_gb])
                cx.barrier()
                cx.dma('sp', T['MT_s'][:, :, :], MT16[:], reads=[r_MT16])
                cx.barrier()
            if hook is not None:
                hook()
            with ExitStack() as st3:
                sb3 = lambda name, shape, dt=F32: st3.enter_context(nc.sbuf_tensor("s3_" + name, shape, dt))
                PU = Pool(cx, [st3.enter_context(nc.psum_tensor(f"s3p_PU{i}", [128, 1024], BF16)) for i in range(2)])
                Ub = Pool(cx, [sb3(f"Ub{i}", [64, 8, 1024], BF16) for i in range(2)])
                Ug = Pool(cx, [sb3(f"Ug{i}", [64, 64, 128], BF16) for i in range(2)])
                UTt = Pool(cx, [sb3(f"UTt{i}", [128, 64, 64], BF16) for i in range(2)])
                for seg in range(NSEG):
                    ub, rub = Ub.next(); ug, rug = Ug.next(); ut, rut = UTt.next()
                    cx.dma('pool', ub[:], T['utok_s'][seg * 512:(seg + 1) * 512, :].rearrange("(k j) c -> k j c", j=8), writes=[rub])
                    cx.op('dve', lambda: nc.vector.tensor_copy(out=ug[:].rearrange("p g (j c) -> p g j c", c=16),
                                                               in_=ub[:].rearrange("p j (g c) -> p g j c", c=16)), reads=[rub], writes=[rug])
                    for g8 in range(8):
                        pt, rp = PU.next()
                        for gi in range(8):
                            g = g8 * 8 + gi
                            cx.op('pe', lambda: nc.tensor.transpose(pt[:, gi * 64:(gi + 1) * 64], ug[:, g, :], identb[0:64, 0:64]),
                                  reads=[rug, r_identb], writes=[rp])
                        cx.op('act', lambda: nc.scalar.copy(out=ut[:, g8 * 8:(g8 + 1) * 8, :], in_=pt[:, 0:512].rearrange("p (g k) -> p g k", k=64)),
                              reads=[rp], writes=[rut])
                    cx.dma('sp', T['UT_s'][:, :, seg * 64:(seg + 1) * 64], ut[:], reads=[rut])
                cx.barrier()
            with ExitStack() as st4:
                sb4 = lambda name, shape, dt=F32: st4.enter_context(nc.sbuf_tensor("s4_" + name, shape, dt))
                PE_ = [[(st4.enter_context(nc.psum_tensor(f"s4p_E{d}{ri}", [128, 512], F32)), cx.res()) for ri in range(2)] for d in range(2)]
                PO = (st4.enter_context(nc.psum_tensor("s4p_O", [128, 512], F32)), cx.res())
                UT = [(sb4(f"UT{d}", [128, 64, 64], BF16), cx.res()) for d in range(2)]
                EE = [(sb4(f"EE{d}", [64, 2, 64, 64]), cx.res()) for d in range(2)]
                WW = [(sb4(f"WW{d}", [64, 2, 64, 64], BF16), cx.res()) for d in range(2)]
                Wst = [[(sb4(f"W{d}{i}", [64, 2, 64]), cx.res()) for i in range(2)] for d in range(2)]
                Sst = [(sb4(f"S{d}", [64, 2, 64]), cx.res()) for d in range(2)]
                Pst = [(sb4(f"P{d}", [64, 2, 2, 64]), cx.res()) for d in range(2)]
                P3r = [[(sb4(f"P3_{d}{i}", [64, 2, 3, 64]), cx.res(), cx.res()) for i in range(4)] for d in range(2)]
                OUTS = [(sb4(f"OUTS{d}", [64, 2, NS, 64]), cx.res()) for d in range(2)]
                OS1 = (sb4("OS", [64, 2, NS, 64]), cx.res())
                ENG = ['dve', 'dve']
                EOP = [nc.vector, nc.vector]
                for d in range(2):
                    cx.op(ENG[d], lambda: EOP[d].tensor_copy(out=Wst[d][0][0][:], in_=W0[d][0][:]), reads=[W0[d][1]], writes=[Wst[d][0][1]])
                step = [0, 0]
                for si in range(NSEG):
                    segs = [si, NSEG - 1 - si]
                    for d in range(2):
                        seg = segs[d]
                        ut, rut = UT[d]; ee, ree = EE[d]
                        ht, r_ht = HT[d]
                        cx.dma('sp', ut[:], T['UT_s'][:, :, seg * 64:(seg + 1) * 64], writes=[rut])
                        for g8 in range(8):
                            for ri in range(2):
                                pt, rp = PE_[d][ri]
                                for gi in range(8):
                                    g = g8 * 8 + gi
                                    cx.op('pe', lambda: nc.tensor.matmul(pt[0:64, gi * 64:(gi + 1) * 64], ht[:, g, ri * 64:(ri + 1) * 64], ut[:, g, :],
                                                                         start=True, stop=True), reads=[r_ht, rut], writes=[rp])
                                cx.op('act', lambda: nc.scalar.copy(out=ee[:, ri, g8 * 8:(g8 + 1) * 8, :], in_=pt[0:64, :].rearrange("p (g k) -> p g k", k=64)),
                                      reads=[rp], writes=[ree])
                    def fill(kk_f, rr_f):
                        for d in range(2):
                            kk_d = kk_f if d == 0 else KS - 1 - kk_f
                            P3, rP3m, rP3s = P3r[d][rr_f]
                            cx.op('act', lambda: nc.scalar.copy(out=P3[:, :, 2, :], in_=EE[d][0][:, :, :, kk_d]), reads=[EE[d][1]], writes=[rP3s])
                    fill(0, step[0] % 4)
                    for kk_ in range(KS):
                        kks = [kk_, KS - 1 - kk_]
                        cur, nxt = step[0] % 2, (step[0] + 1) % 2
                        rr = step[0] % 4
                        for d in range(2):
                            U, rU = Wst[d][cur]; P3, rP3m, rP3s = P3r[d][rr]; cf, r_cf = COEF[d]
                            cx.op('dve', lambda: nc.vector.tensor_tensor(out=P3[:, :, 0:2, :], in0=cf[:], in1=U[:].unsqueeze(1).to_broadcast([64, 2, 2, 64]), op=ALU.mult),
                                  reads=[r_cf, rU], writes=[rP3m])
                        for d in range(2):
                            P3, rP3m, rP3s = P3r[d][rr]; Un, rUn = Wst[d][nxt]
                            cx.op('dve', lambda: nc.vector.tensor_reduce(out=Un[:], in_=P3[:].rearrange("p o i g -> p o g i"), axis=AX.X, op=ALU.add),
                                  reads=[rP3m, rP3s], writes=[rUn])
                        if kk_ + 1 < KS:
                            fill(kk_ + 1, (step[0] + 1) % 4)
                        for d in range(2):
                            Un, rUn = Wst[d][nxt]
                            cx.op('act', lambda: nc.scalar.copy(out=WW[d][0][:, :, :, kks[d]], in_=Un[:]), reads=[rUn], writes=[WW[d][1]])
                        step[0] += 1
                        if step[0] % 32 == 0:
                            for d in range(2):
                                Un, rUn = Wst[d][nxt]
                                cglob = segs[d] * KS + kks[d]
                                slot = cglob // 32
                                osd, r_osd = OUTS[d]
                                cx.op('act', lambda: nc.scalar.copy(out=osd[:, :, slot, :], in_=Un[:]), reads=[rUn], writes=[r_osd])
                                cx.op('dve', lambda: nc.vector.tensor_scalar(out=Un[:], in0=Un[:], scalar1=keep[0:64, 0:1], scalar2=None, op0=ALU.mult),
                                      reads=[rUn, r_keep], writes=[rUn])
                    for d in range(2):
                        ww, rww = WW[d]
                        for ri in range(2):
                            cx.dma('sp', T['WW_s'][d, ri, :, :, segs[d] * 64:(segs[d] + 1) * 64], ww[:, ri, :, :], reads=[rww])
                for d in range(2):
                    osd, r_osd = OUTS[d]; os_, r_os = OS1
                    pt, rp = PO
                    for ri in range(2):
                        for s0_ in range(0, NS, 8):
                            ns = min(8, NS - s0_)
                            for s in range(ns):
                                cx.op('pe', lambda: nc.tensor.transpose(pt[0:64, s * 64:(s + 1) * 64], osd[:, ri, s0_ + s, :], ident[0:64, 0:64]),
                                      reads=[r_osd, r_ident], writes=[rp])
                            cx.op('act', lambda: nc.scalar.copy(out=os_[:, ri, s0_:s0_ + ns, :], in_=pt[0:64, 0:ns * 64].rearrange("p (s q) -> p s q", q=64)),
                                  reads=[rp], writes=[r_os])
                    cx.dma('sp', T['out_sr'][:, d, :, :].rearrange("s g p -> g s p"), os_[:, 0, :, :], reads=[r_os])
                    cx.dma('sp', T['out_si'][:, d, :, :].rearrange("s g p -> g s p"), os_[:, 1, :, :], reads=[r_os])
                cx.barrier()
        with ExitStack() as st5:
            sb5 = lambda name, shape, dt=F32: st5.enter_context(nc.sbuf_tensor("s5_" + name, shape, dt))
            GBs = [(sb5(f"GBs{d}", [128, 64, 128], BF16), cx.res()) for d in range(2)]
            MT16 = sb5("MT16y", [128, 64, 128], BF16); r_MT16 = cx.res()
            cx.dma('sp', MT16[:], T['MT_s'][:, :, :], writes=[r_MT16])
            for d in range(2):
                cx.dma('sp', GBs[d][0][:], T['GB_s'][d].rearrange("r p g q -> (r p) g q"), writes=[GBs[d][1]])
            UTy = Pool(cx, [sb5(f"UTy{i}", [128, 64, 64], BF16) for i in range(2)])
            WWy = [Pool(cx, [sb5(f"WWy{d}{i}", [128, 64, 64], BF16) for i in range(2)]) for d in range(2)]
            YS = Pool(cx, [sb5(f"YS{i}", [128, 8, 64]) for i in range(2)])
            YT = Pool(cx, [sb5(f"YT{i}", [64, 8, 1024]) for i in range(1)])
            PY = Pool(cx, [st5.enter_context(nc.psum_tensor(f"s5p_Y{i}", [128, 512], F32)) for i in range(2)])
            PT = Pool(cx, [st5.enter_context(nc.psum_tensor(f"s5p_T{i}", [128, 1024], F32)) for i in range(2)])
            for seg in range(NSEG):
                ut, rut = UTy.next()
                cx.dma('sp', ut[:], T['UT_s'][:, :, seg * 64:(seg + 1) * 64], writes=[rut])
                wws = []
                for d in range(2):
                    w_, rw_ = WWy[d].next()
                    cx.dma('sp', w_[:], T['WW_s'][d, :, :, :, seg * 64:(seg + 1) * 64].rearrange("r p g k -> (r p) g k"), writes=[rw_])
                    wws.append((w_, rw_))
                yt, ryt = YT.next()
                for g8 in range(8):
                    py, rpy = PY.next()
                    for gi in range(8):
                        g = g8 * 8 + gi
                        cx.op('pe', lambda: nc.tensor.matmul(py[:, gi * 64:(gi + 1) * 64], MT16[:, g, :], ut[:, g, :], start=True, stop=False),
                              reads=[r_MT16, rut], writes=[rpy])
                        for d in range(2):
                            cx.op('pe', lambda: nc.tensor.matmul(py[:, gi * 64:(gi + 1) * 64], GBs[d][0][:, g, :], wws[d][0][:, g, :], start=False, stop=(d == 1)),
                                  reads=[GBs[d][1], wws[d][1]], writes=[rpy])
                    ys, rys = YS.next()
                    cx.op('act', lambda: nc.scalar.copy(out=ys[:], in_=py[:, :].rearrange("p (g k) -> p g k", k=64)), reads=[rpy], writes=[rys])
                    pt, rpt = PT.next()
                    for gi in range(8):
                        cx.op('pe', lambda: nc.tensor.transpose(pt[0:64, gi * 128:(gi + 1) * 128], ys[:, gi, :], ident[:]),
                              reads=[rys, r_ident], writes=[rpt])
                    cx.op('dve', lambda: nc.vector.tensor_copy(out=yt[:, :, g8 * 128:(g8 + 1) * 128].rearrange("p t (g c) -> p g t c", c=16),
                                                               in_=pt[0:64, :].rearrange("p (g t c) -> p g t c", t=8, c=16)),
                          reads=[rpt], writes=[ryt])
                cx.dma('sp', T['ys_s'][seg * 512:(seg + 1) * 512, :].rearrange("(k t) c -> k t c", t=8), yt[:], reads=[ryt])
            cx.barrier()


def s5_consts():
    expo = np.zeros((2, 18), np.float32)
    for j in range(8):
        expo[0, j] = 7 - j; expo[0, 8 + j] = j - 7
        expo[1, j] = j; expo[1, 8 + j] = -j
    expo[:, 16] = 8; expo[:, 17] = 1
    jj = np.arange(128) // 16
    bmask = -np.stack([(jj[:, None] > jj[None, :]), (jj[:, None] < jj[None, :])]).astype(np.float32)
    return expo, bmask


class WCache:
    def __init__(self, nc, cx, T):
        self.nc, self.cx, self.T = nc, cx, T
        self.blocks = {}

    def view(self, blk, kc_n):
        return self.T['wbf_s'][blk, :, 0:kc_n * 512].rearrange("p (k n) -> p k n", n=512)

    def load(self, pool, key, src_view, kc_n):
        cx = self.cx
        wt, rw = pool.next()
        if key not in self.blocks:
            blk = len(self.blocks)
            assert blk < NWBLK
            wres = cx.res()
            self.blocks[key] = (blk, wres)
            cx.dma('pool', wt[:, 0:kc_n, :], src_view, writes=[rw])
            cx.dma('sp', self.view(blk, kc_n), wt[:, 0:kc_n, :], reads=[rw], writes=[wres], sres=rw)
        else:
            blk, wres = self.blocks[key]
            cx.dma('sp', wt[:, 0:kc_n, :], self.view(blk, kc_n), reads=[wres], writes=[rw], sres=rw)
        return wt, rw

    def bounce_load(self, tile, res, key, src_view, kc_n):
        assert key not in self.blocks
        cx = self.cx
        blk = len(self.blocks)
        assert blk < NWBLK
        wres = cx.res()
        self.blocks[key] = (blk, wres)
        cx.dma('pool', tile[:, 0:kc_n, :], src_view, writes=[res])
        return (tile, res, blk, wres, kc_n)

    def bounce_store(self, pend):
        tile, res, blk, wres, kc_n = pend
        self.cx.dma('sp', self.view(blk, kc_n), tile[:, 0:kc_n, :], reads=[res], writes=[wres], sres=res)

    def precast(self, key, src_view, kc_n):
        if key in self.blocks:
            return
        cx = self.cx
        if not hasattr(self, 'pre_res'):
            self.pre_res = cx.res()
        blk = len(self.blocks)
        assert blk < NWBLK
        wres = cx.res()
        self.blocks[key] = (blk, wres)
        cx.dma('pool', self.view(blk, kc_n), src_view, writes=[wres], sres=self.pre_res)


class Prefetch:
    def __init__(self, wc, pool, reqs):
        self.wc, self.pool, self.reqs = wc, pool, reqs
        self.nbuf = len(pool.tiles)
        self.issued = []
        self.taken = 0
        self.released = 0

    def top_up(self):
        while len(self.issued) < len(self.reqs) and len(self.issued) < self.released + self.nbuf:
            key, view, kc_n = self.reqs[len(self.issued)]
            self.issued.append(self.wc.load(self.pool, key, view, kc_n))

    def take(self, key):
        self.top_up()
        assert self.taken < len(self.issued), "prefetch: consumer holds too many blocks"
        assert self.reqs[self.taken][0] == key, (self.reqs[self.taken][0], key)
        r = self.issued[self.taken]
        self.taken += 1
        return r

    def release(self, n=1):
        self.released += n
        self.top_up()


NWBLK = 68


def stage0_mod(nc, cx, T):
    with ExitStack() as st:
        sb = lambda name, shape, dt=F32: st.enter_context(nc.sbuf_tensor("z_" + name, shape, dt))
        pp = Pool(cx, [st.enter_context(nc.psum_tensor(f"zp{i}", [128, 512], F32)) for i in range(2)])
        MOD = sb("MOD", [128, 6, D]); r_mod = cx.res()
        cT = sb("cT", [128, KC]); r_cT = cx.res()
        cs = sb("cs", [128, KC]); r_cs = cx.res()
        csrep = sb("csrep", [128, KC, 128]); r_csrep = cx.res()
        g1row = sb("g1row", [128, D]); r_g1row = cx.res()
        g2row = sb("g2row", [128, D]); r_g2row = cx.res()
        wm = Pool(cx, [sb(f"wm{i}", [128, KC, 512]) for i in range(2)])
        bm = Pool(cx, [sb(f"bm{i}", [128, 512]) for i in range(2)])
        cx.dma('sp', cT[:], T['cvec'].rearrange("(k p) -> p k", p=128), writes=[r_cT], allow_slow_non_contiguous=True)
        cx.dma('sp', g1row[:], T['g1'].partition_broadcast(128), writes=[r_g1row])
        cx.dma('sp', g2row[:], T['g2'].partition_broadcast(128), writes=[r_g2row])
        cx.op('act', lambda: nc.scalar.activation(out=cs[:], in_=cT[:], func=AF.Silu), reads=[r_cT], writes=[r_cs])
        cx.op('dve', lambda: nc.vector.tensor_copy(out=csrep[:], in_=cs[:].unsqueeze(2).to_broadcast([128, KC, 128])),
              reads=[r_cs], writes=[r_csrep])
        wmv = T['w_mod'].rearrange("(k p) n -> p k n", p=128)
        for blk in range(24):
            wt, rw = wm.next()
            bt, rb = bm.next()
            pt, rp = pp.next()
            cx.dma('sp', wt[:], wmv[:, :, blk * 512:(blk + 1) * 512], writes=[rw])
            cx.dma('sp', bt[:], T['b_mod'][blk * 512:(blk + 1) * 512].partition_broadcast(128), writes=[rb])
            for k in range(KC):
                cx.op('pe', lambda: nc.tensor.matmul(pt[:], csrep[:, k, :], wt[:, k, :], start=(k == 0), stop=(k == KC - 1)),
                      reads=[r_csrep, rw], writes=[rp])
            mi, c0 = blk // 4, (blk % 4) * 512
            cx.op('dve', lambda: nc.vector.tensor_tensor(out=MOD[:, mi, c0:c0 + 512], in0=pt[:], in1=bt[:], op=ALU.add),
                  reads=[rp, rb], writes=[r_mod])
        for (mi, grow, rg) in ((1, g1row, r_g1row), (4, g2row, r_g2row)):
            cx.op('dve', lambda: nc.vector.scalar_tensor_tensor(out=MOD[:, mi, :], in0=MOD[:, mi, :], scalar=1.0, in1=grow[:],
                                                                op0=ALU.add, op1=ALU.mult),
                  reads=[r_mod, rg], writes=[r_mod])
        cx.dma('sp', T['mod_s'][:, :], MOD[0:1, :, :], reads=[r_mod])
        cx.barrier()


def stage1_inproj(nc, cx, T, NT, ident, r_ident, wc, pre_list=()):
    NTT = NT // 512
    with ExitStack() as st:
        sb = lambda name, shape, dt=F32: st.enter_context(nc.sbuf_tensor("a_" + name, shape, dt))
        psb = [st.enter_context(nc.psum_tensor(f"ap{i}", [128, 512], F32)) for i in range(8)]
        A1 = sb("A1", [128, D]); r_A1 = cx.res()
        B1 = sb("B1", [128, D]); r_B1 = cx.res()
        cx.dma('sp', A1[:], T['mod_s'][1, :].partition_broadcast(128), writes=[r_A1])
        cx.dma('sp', B1[:], T['mod_s'][0, :].partition_broadcast(128), writes=[r_B1])
        xt = Pool(cx, [sb(f"xt{i}", [128, D]) for i in range(2)])
        junk = sb("junk", [128, D], BF16); r_junk = cx.res()
        stat = Pool(cx, [sb(f"stat{i}", [128, 4]) for i in range(2)])
        hx = Pool(cx, [sb(f"hx{i}", [128, D]) for i in range(2)])
        hT = sb("hT", [128, KC, 512], BF16); r_hT = cx.res()
        WB = Pool(cx, [sb(f"WB{i}", [128, KC, 512], BF16) for i in range(4)])
        WG = sb("WG", [128, KC, 32], BF16); r_WG = cx.res()
        bg = sb("bg", [32, 1]); r_bg = cx.res()
        stg_b = Pool(cx, [sb(f"stgb{i}", [128, 512], BF16) for i in range(4)])
        stg_f = Pool(cx, [sb(f"stgf{i}", [128, 512], F32) for i in range(3)])
        ptr = Pool(cx, [psb[0], psb[1]])
        pfm = Pool(cx, [psb[2], psb[3]])
        ptm = [psb[4], psb[5], psb[6], psb[7]]
        r_ptm = [cx.res() for _ in range(4)]
        winv = T['w_in'].rearrange("(k p) n -> p k n", p=128)
        cx.dma('pool', WG[:], winv[:, :, 4096:4128], writes=[r_WG])
        cx.dma('sp', bg[:], T['b_gates'].rearrange("(p o) -> p o", o=1), writes=[r_bg])
        identb = sb("identb", [128, 128], BF16); r_identb = cx.res()
        cx.op('dve', lambda: nc.vector.tensor_copy(out=identb[:], in_=ident[:]), reads=[r_ident], writes=[r_identb])
        evi = [0]

        def evac(out, in_, rreads, rwrites, func=None, scale=1.0, bias=None):
            if func is not None or bias is not None:
                kw = {}
                if bias is not None:
                    kw['bias'] = bias
                cx.op('act', lambda: nc.scalar.activation(out=out, in_=in_, func=(func or AF.Identity), scale=scale, **kw),
                      reads=rreads, writes=rwrites)
                return
            evi[0] += 1
            if evi[0] % 2 == 0:
                cx.op('act', lambda: nc.scalar.activation(out=out, in_=in_, func=AF.Copy, scale=scale), reads=rreads, writes=rwrites)
            else:
                if scale == 1.0:
                    cx.op('dve', lambda: nc.vector.tensor_copy(out=out, in_=in_), reads=rreads, writes=rwrites)
                else:
                    cx.op('dve', lambda: nc.vector.tensor_scalar(out=out, in0=in_, scalar1=scale, scalar2=None, op0=ALU.mult),
                          reads=rreads, writes=rwrites)

        blocks = [(0, 'q'), (512, 'q'), (1024, 'k'), (1536, 'k'), (2048, 'v'), (2560, 'v'), (3072, 'o'), (3584, 'o'),
                  (4128, 'u'), (4640, 'u')] + [(5152 + 512 * i, 'mg') for i in range(8)]
        pf = Prefetch(wc, WB, [(('w_in', c0), winv[:, :, c0:c0 + 512], KC) for _ in range(NTT) for (c0, _k) in blocks])
        bnc = Pool(cx, [sb(f"bnc{i}", [128, KC, 512], BF16) for i in range(2)])
        pre_list = list(pre_list)
        pre_state = {'i': 0, 'pend': None}

        def precast_slot():
            if pre_state['pend'] is not None:
                wc.bounce_store(pre_state['pend'])
                pre_state['pend'] = None
            if pre_state['i'] < len(pre_list):
                key, view, kc_n = pre_list[pre_state['i']]
                pre_state['i'] += 1
                bt, br = bnc.next()
                pre_state['pend'] = wc.bounce_load(bt, br, key, view, kc_n)
        for tt in range(NTT):
            tok0 = tt * 512
            for sub in range(4):
                xx, rx = xt.next()
                sx, rs = stat.next()
                hh, rh = hx.next()
                cx.dma('sp', xx[:], T['x'][tok0 + sub * 128: tok0 + (sub + 1) * 128, :], writes=[rx])
                cx.op('act', lambda: nc.scalar.activation(out=junk[:], in_=xx[:], func=AF.Square, accum_out=sx[:, 0:1]),
                      reads=[rx], writes=[r_junk, rs])
                cx.op('dve', lambda: nc.vector.tensor_scalar(out=sx[:, 1:2], in0=sx[:, 0:1], scalar1=1.0 / D, scalar2=EPS,
                                                             op0=ALU.mult, op1=ALU.add), reads=[rs], writes=[rs])
                cx.op('act', lambda: nc.scalar.activation(out=sx[:, 2:3], in_=sx[:, 1:2], func=AF.Sqrt), reads=[rs], writes=[rs])
                cx.op('dve', lambda: nc.vector.reciprocal(out=sx[:, 3:4], in_=sx[:, 2:3]), reads=[rs], writes=[rs])
                cx.op('dve', lambda: nc.vector.scalar_tensor_tensor(out=hh[:], in0=xx[:], scalar=sx[:, 3:4], in1=A1[:],
                                                                    op0=ALU.mult, op1=ALU.mult),
                      reads=[rx, rs, r_A1], writes=[rh])
                cx.op('pool', lambda: nc.gpsimd.tensor_tensor(out=hh[:], in0=hh[:], in1=B1[:], op=ALU.add),
                      reads=[rh, r_B1], writes=[rh])
                for g in range(4):
                    pt, rp = ptr.next()
                    for j in range(4):
                        k = g * 4 + j
                        cx.op('pe', lambda: nc.tensor.transpose(pt[:, j * 128:(j + 1) * 128], hh[:, k * 128:(k + 1) * 128], ident[:]),
                              reads=[rh, r_ident], writes=[rp])
                    evac(hT[:, g * 4:(g + 1) * 4, sub * 128:(sub + 1) * 128], pt[:].rearrange("p (a b) -> p a b", a=4), [rp], [r_hT])
            for (c0, kind) in blocks:
                wt, rw = pf.take(('w_in', c0))
                if tt >= 1:
                    precast_slot()
                if kind in ('q', 'k', 'mg'):
                    for j in range(4):
                        pt, rp = pfm.next()
                        for k in range(KC):
                            cx.op('pe', lambda: nc.tensor.matmul(pt[:], wt[:, k, j * 128:(j + 1) * 128], hT[:, k, :],
                                                                 start=(k == 0), stop=(k == KC - 1)),
                                  reads=[rw, r_hT], writes=[rp])
                        sg, rsg = stg_b.next()
                        if kind == 'q':
                            head = (c0 // 128) + j
                            evac(sg[:], pt[:], [rp], [rsg])
                            cx.dma('sp', T['qT_s'][head, :, tok0:tok0 + 512], sg[:], reads=[rsg])
                        elif kind == 'k':
                            head = ((c0 - 1024) // 128) + j
                            evac(sg[:], pt[:], [rp], [rsg], scale=128.0 ** -0.5)
                            cx.dma('sp', T['kT_s'][head, :, tok0:tok0 + 512], sg[:], reads=[rsg])
                        else:
                            ch = ((c0 - 5152) // 128) + j
                            evac(sg[:], pt[:], [rp], [rsg], func=AF.Sigmoid)
                            cx.dma('sp', T['mgT_s'][ch * 128:(ch + 1) * 128, tok0:tok0 + 512], sg[:], reads=[rsg])
                if kind in ('k', 'v', 'o', 'u'):
                    for k in range(KC):
                        for sub in range(4):
                            cx.op('pe', lambda: nc.tensor.matmul(ptm[sub][:], hT[:, k, sub * 128:(sub + 1) * 128], wt[:, k, :],
                                                                 start=(k == 0), stop=(k == KC - 1)),
                                  reads=[rw, r_hT], writes=[r_ptm[sub]])
                    for sub in range(4):
                        rows = slice(tok0 + sub * 128, tok0 + (sub + 1) * 128)
                        if kind == 'u':
                            sg, rsg = stg_f.next()
                            cc = c0 - 4128
                            evac(sg[:], ptm[sub][:], [r_ptm[sub]], [rsg])
                            cx.dma('sp', T['utok_s'][rows, cc:cc + 512], sg[:], reads=[rsg])
                        else:
                            sg, rsg = stg_b.next()
                            if kind == 'k':
                                cc = c0 - 1024
                                evac(sg[:], ptm[sub][:], [r_ptm[sub]], [rsg], scale=128.0 ** -0.5)
                                cx.dma('sp', T['ktok_s'][rows, cc:cc + 512], sg[:], reads=[rsg])
                            elif kind == 'v':
                                cc = c0 - 2048
                                evac(sg[:], ptm[sub][:], [r_ptm[sub]], [rsg])
                                cx.dma('sp', T['vtok_s'][rows, cc:cc + 512], sg[:], reads=[rsg])
                            else:
                                cc = c0 - 3072
                                evac(sg[:], ptm[sub][:], [r_ptm[sub]], [rsg], func=AF.Sigmoid)
                                cx.dma('sp', T['otok_s'][rows, cc:cc + 512], sg[:], reads=[rsg])
                pf.release(1)
            pt, rp = pfm.next()
            for k in range(KC):
                cx.op('pe', lambda: nc.tensor.matmul(pt[0:32, :], WG[:, k, :], hT[:, k, :], start=(k == 0), stop=(k == KC - 1)),
                      reads=[r_WG, r_hT], writes=[rp])
            sg, rsg = stg_f.next()
            evac(sg[0:32, :], pt[0:32, :], [rp, r_bg], [rsg], bias=bg[:, 0:1])
            cx.dma('sp', T['gates_s'][:, tok0:tok0 + 512], sg[0:32, :], reads=[rsg])
        while pre_state['pend'] is not None or pre_state['i'] < len(pre_list):
            precast_slot()
        cx.barrier()


def stage3_reqs(T, NTT):
    wupv = T['w_up_a'].rearrange("(k p) n -> p k n", p=128)
    wgluv = T['w_glu'].rearrange("(k p) n -> p k n", p=128)
    woutv = T['w_out'].rearrange("(k p) n -> p k n", p=128)
    wfiv = T['w_ffn_in'].rearrange("(k p) n -> p k n", p=128)
    wfov = T['w_ffn_out'].rearrange("(k p) n -> p k n", p=128)
    fparts = [(0, 4), (4, 4), (8, 3)]
    reqs = []
    for _tt in range(NTT):
        for grp in range(4):
            c0 = grp * 512
            reqs.append((('up', c0), wupv[:, :, c0:c0 + 512], 8))
            reqs.append((('glu', c0), wgluv[:, :, c0:c0 + 512], 8))
            reqs.append((('glu', 2048 + c0), wgluv[:, :, 2048 + c0:2048 + c0 + 512], 8))
        for nb in range(4):
            reqs.append((('out', nb), woutv[:, :, nb * 512:(nb + 1) * 512], KC))
        for (fb0, nfb) in fparts:
            for fb in range(nfb):
                f0 = (fb0 + fb) * 512
                reqs.append((('fi', f0), wfiv[:, :, f0:f0 + 512], KC))
                reqs.append((('fi', D_FF + f0), wfiv[:, :, D_FF + f0:D_FF + f0 + 512], KC))
            nfc = nfb * 4
            for nb in range(4):
                reqs.append((('fo', fb0, nb), wfov[:, fb0 * 4:fb0 * 4 + nfc, nb * 512:(nb + 1) * 512], nfc))
    return reqs


def stage3_rest(nc, cx, T, NT, ident, r_ident, wc):
    NTT = NT // 512
    with ExitStack() as st:
        sb = lambda name, shape, dt=F32: st.enter_context(nc.sbuf_tensor("c_" + name, shape, dt))

        def rowload(tile_ap, res_list, idx):
            cx.dma('sp', tile_ap, T['mod_s'][idx, :].partition_broadcast(128), writes=res_list)

        identb = sb("identb", [128, 128], BF16); r_identb = cx.res()
        cx.op('dve', lambda: nc.vector.tensor_copy(out=identb[:], in_=ident[:]), reads=[r_ident], writes=[r_identb])
        x1 = sb("x1", [128, 4, D]); r_x1 = [cx.res() for _ in range(4)]
        HA = sb("HA", [128, KC, 512], BF16); r_HA = cx.res()
        WB = Pool(cx, [sb(f"WB{i}", [128, KC, 512], BF16) for i in range(4)])
        stat = Pool(cx, [sb(f"stat{i}", [128, 4]) for i in range(2)])
        tmp512 = Pool(cx, [sb(f"tmp512{i}", [128, 512]) for i in range(3)])
        reqs = stage3_reqs(T, NTT)
        fparts = [(0, 4), (4, 4), (8, 3)]
        pf = Prefetch(wc, WB, reqs)
        banks = [(st.enter_context(nc.psum_tensor(f"cps{i}", [128, 512], F32)), cx.res()) for i in range(8)]

        class BPool:
            def __init__(self, idxs, bf16=False):
                self.items = [((banks[i][0][:].bitcast(BF16) if bf16 else banks[i][0][:]), banks[i][1]) for i in idxs]
                self.i = 0

            def next(self):
                it = self.items[self.i]
                self.i = (self.i + 1) % len(self.items)
                return it

        def rstd_of(src_ap, src_res, junk_ap, junk_res):
            sx, rs = stat.next()
            cx.op('act', lambda: nc.scalar.activation(out=junk_ap, in_=src_ap, func=AF.Square, accum_out=sx[:, 0:1]),
                  reads=src_res, writes=junk_res + [rs])
            cx.op('dve', lambda: nc.vector.tensor_scalar(out=sx[:, 1:2], in0=sx[:, 0:1], scalar1=1.0 / D, scalar2=EPS,
                                                         op0=ALU.mult, op1=ALU.add), reads=[rs], writes=[rs])
            cx.op('act', lambda: nc.scalar.activation(out=sx[:, 2:3], in_=sx[:, 1:2], func=AF.Sqrt), reads=[rs], writes=[rs])
            cx.op('dve', lambda: nc.vector.reciprocal(out=sx[:, 3:4], in_=sx[:, 2:3]), reads=[rs], writes=[rs])
            return sx, rs

        for tt in range(NTT):
            tok0 = tt * 512
            for sub in range(4):
                cx.dma('sp', x1[:, sub, :], T['x'][tok0 + sub * 128: tok0 + (sub + 1) * 128, :], writes=[r_x1[sub]])
            with ExitStack() as sa:
                sba = lambda name, shape, dt=F32: sa.enter_context(nc.sbuf_tensor(f"ca{tt}_" + name, shape, dt))
                gt1 = sba("gt1", [128, D]); r_gt1 = cx.res()
                rowload(gt1[:], [r_gt1], 2)
                MD = sba("MD", [128, 2, 1024]); r_MD = cx.res()
                cx.dma('sp', MD[:, 0, :], T['mh_g'].partition_broadcast(128), writes=[r_MD])
                cx.dma('sp', MD[:, 1, :], T['s5_D'].partition_broadcast(128), writes=[r_MD])
                mhg, Drow = MD[:, 0, :], MD[:, 1, :]
                mT = sba("mT", [128, KC, 512], BF16); r_mT = cx.res()
                LDA = sba("LDA", [128, 2, 1024]); r_LDA = [cx.res(), cx.res()]
                LDB = sba("LDB", [128, 2, 1024]); r_LDB = [cx.res(), cx.res()]
                LDO = sba("LDO", [128, 2, 1024], BF16); r_LDO = [cx.res(), cx.res()]
                hb16 = Pool(cx, [sba(f"hb16{i}", [128, 1024], BF16) for i in range(2)])
                mg = Pool(cx, [sba(f"mg{i}", [128, 512], BF16) for i in range(4)])
                sx8p = Pool(cx, [sba(f"sx8_{i}", [128, 3, 8]) for i in range(2)])
                li = [0]

                def ldnext():
                    i = li[0] % 2
                    li[0] += 1
                    return i
                ptb = BPool([6, 7], bf16=True)
                for sub in range(4):
                    rws = slice(tok0 + sub * 128, tok0 + (sub + 1) * 128)
                    i = ldnext()
                    ta, ra, tb, rb, to, ro = LDA[:, i, :], r_LDA[i], LDB[:, i, :], r_LDB[i], LDO[:, i, :], r_LDO[i]
                    cx.dma('sp', ta, T['hf_s'][rws, :], writes=[ra])
                    cx.dma('sp', tb, T['hb_s'][rws, :], writes=[rb])
                    cx.dma('sp', to, T['otok_s'][rws, :], writes=[ro])
                    cx.op('dve', lambda: nc.vector.tensor_tensor(out=ta, in0=ta, in1=tb, op=ALU.add), reads=[ra, rb], writes=[ra])
                    cx.op('pool', lambda: nc.gpsimd.tensor_tensor(out=tb, in0=ta, in1=ta, op=ALU.mult), reads=[ra], writes=[rb])
                    sx8, r_sx8 = sx8p.next()
                    cx.op('dve', lambda: nc.vector.tensor_reduce(out=sx8[:, 0, :], in_=tb.rearrange("p (h e) -> p h e", h=8), axis=AX.X, op=ALU.add),
                          reads=[rb], writes=[r_sx8])
                    cx.op('dve', lambda: nc.vector.tensor_scalar(out=sx8[:, 1, :], in0=sx8[:, 0, :], scalar1=1.0 / 128, scalar2=EPS, op0=ALU.mult, op1=ALU.add),
                          reads=[r_sx8], writes=[r_sx8])
                    cx.op('act', lambda: nc.scalar.activation(out=sx8[:, 1, :], in_=sx8[:, 1, :], func=AF.Sqrt), reads=[r_sx8], writes=[r_sx8])
                    cx.op('dve', lambda: nc.vector.reciprocal(out=sx8[:, 2, :], in_=sx8[:, 1, :]), reads=[r_sx8], writes=[r_sx8])
                    cx.op('dve', lambda: nc.vector.tensor_tensor(out=ta.rearrange("p (h e) -> p h e", h=8), in0=ta.rearrange("p (h e) -> p h e", h=8),
                                                                 in1=sx8[:, 2, :].unsqueeze(2).to_broadcast([128, 8, 128]), op=ALU.mult),
                          reads=[ra, r_sx8], writes=[ra])
                    cx.op('pool', lambda: nc.gpsimd.tensor_tensor(out=ta, in0=ta, in1=mhg, op=ALU.mult), reads=[ra, r_MD], writes=[ra])
                    h16, rh16 = hb16.next()
                    cx.op('dve', lambda: nc.vector.tensor_tensor(out=h16[:], in0=ta, in1=to, op=ALU.mult), reads=[ra, ro], writes=[rh16])
                    pt, rp = ptb.next()
                    for e in range(8):
                        cx.op('pe', lambda: nc.tensor.transpose(pt[:, e * 128:(e + 1) * 128], h16[:, e * 128:(e + 1) * 128], identb[:]),
                              reads=[rh16, r_identb], writes=[rp])
                    cx.op('act', lambda: nc.scalar.copy(out=HA[:, 0:8, sub * 128:(sub + 1) * 128], in_=pt[:, :].rearrange("p (e t) -> p e t", e=8)),
                          reads=[rp], writes=[r_HA])
                    i = ldnext()
                    ta, ra, tb, rb = LDA[:, i, :], r_LDA[i], LDB[:, i, :], r_LDB[i]
                    cx.dma('sp', ta, T['ys_s'][rws, :], writes=[ra])
                    cx.dma('sp', tb, T['utok_s'][rws, :], writes=[rb])
                    cx.op('pool', lambda: nc.gpsimd.tensor_tensor(out=tb, in0=tb, in1=Drow, op=ALU.mult), reads=[rb, r_MD], writes=[rb])
                    cx.op('dve', lambda: nc.vector.tensor_tensor(out=ta, in0=ta, in1=tb, op=ALU.add), reads=[ra, rb], writes=[ra])
                    h16, rh16 = hb16.next()
                    cx.op('act', lambda: nc.scalar.activation(out=h16[:], in_=ta, func=AF.Gelu_apprx_tanh), reads=[ra], writes=[rh16])
                    pt, rp = ptb.next()
                    for e in range(8):
                        cx.op('pe', lambda: nc.tensor.transpose(pt[:, e * 128:(e + 1) * 128], h16[:, e * 128:(e + 1) * 128], identb[:]),
                              reads=[rh16, r_identb], writes=[rp])
                    cx.op('act', lambda: nc.scalar.copy(out=HA[:, 8:16, sub * 128:(sub + 1) * 128], in_=pt[:, :].rearrange("p (e t) -> p e t", e=8)),
                          reads=[rp], writes=[r_HA])
                P1 = BPool([0, 1]); P2 = BPool([2, 3]); P3 = BPool([4, 5])
                for grp in range(4):
                    c0 = grp * 512
                    wu, rwu = pf.take(('up', c0))
                    wa, rwa = pf.take(('glu', c0))
                    wb_, rwb = pf.take(('glu', 2048 + c0))
                    for j in range(4):
                        dc = grp * 4 + j
                        p1, rp1 = P1.next(); p2, rp2 = P2.next(); p3, rp3 = P3.next()
                        for e in range(8):
                            cx.op('pe', lambda: nc.tensor.matmul(p1[:], wu[:, e, j * 128:(j + 1) * 128], HA[:, e, :], start=(e == 0), stop=(e == 7)),
                                  reads=[rwu, r_HA], writes=[rp1])
                        for e in range(8):
                            cx.op('pe', lambda: nc.tensor.matmul(p2[:], wa[:, e, j * 128:(j + 1) * 128], HA[:, 8 + e, :], start=(e == 0), stop=(e == 7)),
                                  reads=[rwa, r_HA], writes=[rp2])
                        for e in range(8):
                            cx.op('pe', lambda: nc.tensor.matmul(p3[:], wb_[:, e, j * 128:(j + 1) * 128], HA[:, 8 + e, :], start=(e == 0), stop=(e == 7)),
                                  reads=[rwb, r_HA], writes=[rp3])
                        mga, rmga = mg.next(); mgb, rmgb = mg.next()
                        cx.dma('sp', mga[:], T['mgT_s'][dc * 128:(dc + 1) * 128, tok0:tok0 + 512], writes=[rmga])
                        cx.dma('sp', mgb[:], T['mgT_s'][2048 + dc * 128:2048 + (dc + 1) * 128, tok0:tok0 + 512], writes=[rmgb])
                        t1, rt1 = tmp512.next(); t2, rt2 = tmp512.next()
                        cx.op('act', lambda: nc.scalar.activation(out=t1[:], in_=p3[:], func=AF.Sigmoid), reads=[rp3], writes=[rt1])
                        cx.op('dve', lambda: nc.vector.tensor_tensor(out=t1[:], in0=p2[:], in1=t1[:], op=ALU.mult), reads=[rp2, rt1], writes=[rt1])
                        cx.op('pool', lambda: nc.gpsimd.tensor_tensor(out=t1[:], in0=t1[:], in1=mgb[:], op=ALU.mult), reads=[rt1, rmgb], writes=[rt1])
                        cx.op('dve', lambda: nc.vector.tensor_tensor(out=t2[:], in0=p1[:], in1=mga[:], op=ALU.mult), reads=[rp1, rmga], writes=[rt2])
                        cx.op('pool', lambda: nc.gpsimd.tensor_tensor(out=mT[:, dc, :], in0=t1[:], in1=t2[:], op=ALU.add), reads=[rt1, rt2], writes=[r_mT])
                    pf.release(3)
                A2row, B2row = MD[:].rearrange("p a b -> p (a b)"), LDB[:].rearrange("p a b -> p (a b)")
                hxt, junk = LDA[:].rearrange("p a b -> p (a b)"), LDO[:].rearrange("p a b -> p (a b)")
                rowload(A2row, [r_MD], 4)
                rowload(B2row, r_LDB, 3)
                wos = [pf.take(('out', nb)) for nb in range(4)]
                po = [banks[i] for i in (4, 5, 6, 7)]
                ptr = BPool([0, 1, 2, 3])
                for sub in range(4):
                    for nb in range(4):
                        wo, rwo = wos[nb]
                        for k in range(KC):
                            cx.op('pe', lambda: nc.tensor.matmul(po[nb][0][:], mT[:, k, sub * 128:(sub + 1) * 128], wo[:, k, :],
                                                                 start=(k == 0), stop=(k == KC - 1)),
                                  reads=[rwo, r_mT], writes=[po[nb][1]])
                    for nb in range(4):
                        t1, rt1 = tmp512.next()
                        cx.op('dve', lambda: nc.vector.tensor_tensor(out=t1[:], in0=po[nb][0][:], in1=gt1[:, nb * 512:(nb + 1) * 512], op=ALU.mult),
                              reads=[po[nb][1], r_gt1], writes=[rt1])
                        cx.op('pool', lambda: nc.gpsimd.tensor_tensor(out=x1[:, sub, nb * 512:(nb + 1) * 512], in0=x1[:, sub, nb * 512:(nb + 1) * 512],
                                                                      in1=t1[:], op=ALU.add), reads=[rt1, r_x1[sub]], writes=[r_x1[sub]])
                    sx, rs = rstd_of(x1[:, sub, :], [r_x1[sub]], junk, r_LDO)
                    cx.op('dve', lambda: nc.vector.scalar_tensor_tensor(out=hxt, in0=x1[:, sub, :], scalar=sx[:, 3:4], in1=A2row,
                                                                        op0=ALU.mult, op1=ALU.mult), reads=[r_x1[sub], rs, r_MD], writes=r_LDA)
                    cx.op('pool', lambda: nc.gpsimd.tensor_tensor(out=hxt, in0=hxt, in1=B2row, op=ALU.add), reads=r_LDA + r_LDB, writes=r_LDA)
                    for g in range(4):
                        pt, rp = ptr.next()
                        for j in range(4):
                            k = g * 4 + j
                            cx.op('pe', lambda: nc.tensor.transpose(pt[:, j * 128:(j + 1) * 128], hxt[:, k * 128:(k + 1) * 128], ident[:]),
                                  reads=r_LDA + [r_ident], writes=[rp])
                        cx.op('act', lambda: nc.scalar.copy(out=HA[:, g * 4:(g + 1) * 4, sub * 128:(sub + 1) * 128],
                                                            in_=pt[:].rearrange("p (a b) -> p a b", a=4)), reads=[rp], writes=[r_HA])
                pf.release(4)
                cx.barrier()
            with ExitStack() as sbk:
                sbb = lambda name, shape, dt=F32: sbk.enter_context(nc.sbuf_tensor(f"cb{tt}_" + name, shape, dt))
                gt2 = sbb("gt2", [128, D]); r_gt2 = cx.res()
                rowload(gt2[:], [r_gt2], 5)
                gfrow = sbb("gfrow", [128, D]); r_gfrow = cx.res()
                cx.dma('sp', gfrow[:], T['gf'].partition_broadcast(128), writes=[r_gfrow])
                gT = sbb("gT", [128, KC, 512], BF16); r_gT = cx.res()
                hx = Pool(cx, [sbb(f"hx{i}", [128, D]) for i in range(2)])
                junk2 = sbb("junk", [128, D], BF16); r_junk2 = cx.res()
                Pa = BPool([0, 1]); Pb = BPool([2, 3])
                po = [banks[i] for i in (4, 5, 6, 7)]
                for pi, (fb0, nfb) in enumerate(fparts):
                    for fb in range(nfb):
                        f0 = (fb0 + fb) * 512
                        wa, rwa = pf.take(('fi', f0))
                        wb_, rwb = pf.take(('fi', D_FF + f0))
                        for j in range(4):
                            pa, rpa = Pa.next(); pb, rpb = Pb.next()
                            for k in range(KC):
                                cx.op('pe', lambda: nc.tensor.matmul(pa[:], wa[:, k, j * 128:(j + 1) * 128], HA[:, k, :], start=(k == 0), stop=(k == KC - 1)),
                                      reads=[rwa, r_HA], writes=[rpa])
                            for k in range(KC):
                                cx.op('pe', lambda: nc.tensor.matmul(pb[:], wb_[:, k, j * 128:(j + 1) * 128], HA[:, k, :], start=(k == 0), stop=(k == KC - 1)),
                                      reads=[rwb, r_HA], writes=[rpb])
                            t1, rt1 = tmp512.next()
                            cx.op('act', lambda: nc.scalar.activation(out=t1[:], in_=pa[:], func=AF.Silu), reads=[rpa], writes=[rt1])
                            cx.op('dve', lambda: nc.vector.tensor_tensor(out=gT[:, fb * 4 + j, :], in0=pb[:], in1=t1[:], op=ALU.mult),
                                  reads=[rpb, rt1], writes=[r_gT])
                        pf.release(2)
                    nfc = nfb * 4
                    last = (pi == len(fparts) - 1)

                    def ffn_out_evac(sub, nb, bank):
                        t1, rt1 = tmp512.next()
                        cx.op('dve', lambda: nc.vector.tensor_tensor(out=t1[:], in0=bank[0][:], in1=gt2[:, nb * 512:(nb + 1) * 512], op=ALU.mult),
                              reads=[bank[1], r_gt2], writes=[rt1])
                        cx.op('pool', lambda: nc.gpsimd.tensor_tensor(out=x1[:, sub, nb * 512:(nb + 1) * 512], in0=x1[:, sub, nb * 512:(nb + 1) * 512],
                                                                      in1=t1[:], op=ALU.add), reads=[rt1, r_x1[sub]], writes=[r_x1[sub]])
                    if not last:
                        for nb in range(4):
                            wo, rwo = pf.take(('fo', fb0, nb))
                            for k in range(nfc):
                                for sub in range(4):
                                    cx.op('pe', lambda: nc.tensor.matmul(po[sub][0][:], gT[:, k, sub * 128:(sub + 1) * 128], wo[:, k, :],
                                                                         start=(k == 0), stop=(k == nfc - 1)),
                                          reads=[rwo, r_gT], writes=[po[sub][1]])
                            pf.release(1)
                            for sub in range(4):
                                ffn_out_evac(sub, nb, po[sub])
                    else:
                        wos = [pf.take(('fo', fb0, nb)) for nb in range(4)]
                        for sub in range(4):
                            for nb in range(4):
                                wo, rwo = wos[nb]
                                for k in range(nfc):
                                    cx.op('pe', lambda: nc.tensor.matmul(po[nb][0][:], gT[:, k, sub * 128:(sub + 1) * 128], wo[:, k, :],
                                                                         start=(k == 0), stop=(k == nfc - 1)),
                                          reads=[rwo, r_gT], writes=[po[nb][1]])
                            for nb in range(4):
                                ffn_out_evac(sub, nb, po[nb])
                            sx, rs = rstd_of(x1[:, sub, :], [r_x1[sub]], junk2[:], [r_junk2])
                            hh, rh = hx.next()
                            cx.op('dve', lambda: nc.vector.scalar_tensor_tensor(out=hh[:], in0=x1[:, sub, :], scalar=sx[:, 3:4], in1=gfrow[:],
                                                                                op0=ALU.mult, op1=ALU.mult), reads=[r_x1[sub], rs, r_gfrow], writes=[rh])
                            cx.dma('sp', T['y'][tok0 + sub * 128: tok0 + (sub + 1) * 128, :], hh[:], reads=[rh])
                        pf.release(4)
                cx.barrier()
        cx.barrier()


SCRATCH = lambda NT: [
    ("mod_s", [6, D], F32), ("qT_s", [8, 128, NT], BF16), ("kT_s", [8, 128, NT], BF16), ("ktok_s", [NT, 1024], BF16),
    ("vtok_s", [NT, 1024], BF16), ("otok_s", [NT, 1024], BF16), ("utok_s", [NT, 1024], F32), ("mgT_s", [4096, NT], BF16),
    ("gates_s", [32, NT], F32), ("hf_s", [NT, 1024], F32), ("hb_s", [NT, 1024], F32), ("ys_s", [NT, 1024], F32),
    ("UT_s", [128, 64, NT // 8], BF16), ("WW_s", [2, 2, 64, 64, NT // 8], BF16), ("GB_s", [2, 2, 64, 64, 128], BF16),
    ("MT_s", [128, 64, 128], BF16), ("wbf_s", [NWBLK, 128, KC * 512], BF16)]

INPUTS = lambda NT: [
    ("x", [NT, D]), ("cvec", [D]), ("keep", [128, 1]), ("C0", [2, 8, 128, 128]), ("n0", [2, 8, 128]), ("m0", [2, 8]),
    ("s0r", [2, 64, 64]), ("s0i", [2, 64, 64]), ("ident", [128, 128]), ("masks", [2, 64, 64]), ("expo", [2, 18]), ("bmask", [2, 128, 128]),
    ("w_mod", [D, 6 * D]), ("b_mod", [6 * D]), ("g1", [D]), ("g2", [D]), ("w_in", [D, N_IN]), ("b_gates", [32]), ("mh_g", [1024]),
    ("lam_re", [2, 64, 64]), ("lam_im", [2, 64, 64]), ("log_step", [2, 64]), ("B_re", [2, 64, 64, 16]), ("B_im", [2, 64, 64, 16]),
    ("C_re", [2, 64, 16, 64]), ("C_im", [2, 64, 16, 64]), ("s5_D", [1024]), ("w_up_a", [1024, D]), ("w_glu", [1024, 2 * D]),
    ("w_out", [D, D]), ("w_ffn_in", [D, 2 * D_FF]), ("w_ffn_out", [D_FF, D]), ("gf", [D])]


def OUTPUTS(NT):
    NS = NT // 256
    return [("y", [NT, D]), ("out_C", [NS, 2, 8, 128, 128]), ("out_n", [NS, 2, 8, 128]), ("out_m", [NS, 2, 8]),
            ("out_sr", [NS, 2, 64, 64]), ("out_si", [NS, 2, 64, 64])]


def build_full(NT, stages=(0, 1, 2, 3, 4), dbg_scratch=()):
    nc = bass.Bass("TRN2", target_bir_lowering=False)
    T = {}
    for (nm, shp) in INPUTS(NT):
        T[nm] = nc.dram_tensor(nm, shp, F32, kind="ExternalInput").ap()
    for (nm, shp) in OUTPUTS(NT):
        T[nm] = nc.dram_tensor(nm, shp, F32, kind="ExternalOutput").ap()
    for (nm, shp, dt) in SCRATCH(NT):
        T[nm] = nc.dram_tensor(nm, shp, dt, kind=("ExternalOutput" if nm in dbg_scratch else "Internal")).ap()
    with ExitStack() as gst:
        cx = Ctx(nc, gst)
        ident = gst.enter_context(nc.sbuf_tensor("ident_sb", [128, 128], F32)); r_ident = cx.res("ident")
        cx.dma('sp', ident[:], T['ident'][:, :], writes=[r_ident])
        if 0 in stages:
            stage0_mod(nc, cx, T)
        wc = WCache(nc, cx, T)
        if 1 in stages:
            pre = []
            if 4 in stages and NT >= 1024:
                seen = set()
                for (key, view, kc_n) in stage3_reqs(T, 1):
                    if key not in seen:
                        seen.add(key)
                        pre.append((key, view, kc_n))
            stage1_inproj(nc, cx, T, NT, ident, r_ident, wc, pre)
        if 2 in stages:
            stage_mlstm(nc, cx, T, NT, ident, r_ident)
        def precast_stage3():
            seen = set()
            for (key, view, kc_n) in stage3_reqs(T, 1):
                if key not in seen:
                    seen.add(key)
                    wc.precast(key, view, kc_n)
        if 3 in stages:
            stage_s5(nc, cx, T, NT, ident, r_ident, hook=None)
        if 4 in stages:
            stage3_rest(nc, cx, T, NT, ident, r_ident, wc)
        cx.barrier()
    return nc


def host_consts():
    s_ = np.arange(64)
    masks = np.stack([(s_[:, None] <= s_[None, :]), (s_[:, None] >= s_[None, :])]).astype(np.float32)
    expo, bmask = s5_consts()
    return {"ident": np.eye(128, dtype=np.float32), "masks": masks, "expo": expo, "bmask": bmask}


def weight_map(inp):
    f = lambda a: np.ascontiguousarray(np.asarray(a, dtype=np.float32))
    return {"w_mod": f(inp["w_mod"][0]), "b_mod": f(inp["b_mod"][0]), "g1": f(inp["norm1_g"][0]), "g2": f(inp["norm2_g"][0]),
            "w_in": f(inp["w_in"][0]), "b_gates": f(inp["b_gates"][0]), "mh_g": f(inp["mh_norm_g"][0]),
            "lam_re": f(inp["s5_lam_re"][0]), "lam_im": f(inp["s5_lam_im"][0]), "log_step": f(inp["s5_log_step"][0]),
            "B_re": f(inp["s5_B_re"][0]), "B_im": f(inp["s5_B_im"][0]), "C_re": f(inp["s5_C_re"][0]), "C_im": f(inp["s5_C_im"][0]),
            "s5_D": f(inp["s5_D"][0]).reshape(1024), "w_up_a": f(inp["w_up_a"][0]), "w_glu": f(inp["w_glu"][0]), "w_out": f(inp["w_out"][0]),
            "w_ffn_in": f(inp["w_ffn_in"][0]), "w_ffn_out": f(inp["w_ffn_out"][0]), "gf": f(inp["norm_f_g"])}


NT_CORE = 4096


def kernel(**inputs):
    inp = {k: np.asarray(v) for k, v in inputs.items()}
    NT = NT_CORE
    W = weight_map(inp)
    C = host_consts()
    f = lambda a: np.ascontiguousarray(np.asarray(a, dtype=np.float32))
    z = lambda *shape: np.zeros(shape, np.float32)
    maps = []
    for b in range(4):
        maps.append(dict(W, **C, x=f(inp["x_sample"][b]), cvec=f(inp["c"][b]), keep=np.ones((128, 1), np.float32),
                         C0=f(inp["state_mlstm_C"][b, 0]), n0=f(inp["state_mlstm_n"][b, 0]), m0=f(inp["state_mlstm_m"][b, 0]),
                         s0r=f(inp["state_s5_re"][b, 0]), s0i=f(inp["state_s5_im"][b, 0])))
    xp = f(inp["x_prompt"]).reshape(2, NT, D)
    pm = []
    for i in range(2):
        pm.append(dict(W, **C, x=xp[i], cvec=f(inp["c_ctx"]), keep=z(128, 1), C0=z(2, 8, 128, 128), n0=z(2, 8, 128), m0=z(2, 8),
                       s0r=z(2, 64, 64), s0i=z(2, 64, 64)))
    maps += pm + pm
    nc = build_full(NT)
    res = run_bass_kernel_spmd(nc, maps, core_ids=list(range(8)))
    R_ = res.results
    y_sample = np.stack([R_[b]["y"] for b in range(4)]).astype(np.float32)
    y_prompt = np.concatenate([R_[4]["y"], R_[5]["y"]], 0).reshape(32, 256, D).astype(np.float32)
    cat = lambda k: np.concatenate([R_[4][k], R_[5][k]], 0)[:, None].astype(np.float32)
    return (y_prompt, y_sample, cat("out_C"), cat("out_n"), cat("out_m"), cat("out_sr"), cat("out_si"))
```
